# Optimizing a Trainium2 kernel written in Bass

```python
import jax, jax.numpy as jnp
from jax import lax
import numpy as np

D_MODEL = 2048
BATCH = 4
SEQ = 2048
DEPTH = 1
DEC_BATCH = 128
DEC_SEQ = 1
PAST_LEN = 16384
PAGE_SIZE = 128

D_CONV = D_MODEL // 2
CONV_WIDTH = 3
D_HGRN = D_MODEL // 2
HGRN_KDIM = 128
HGRN_HEADS = D_HGRN // HGRN_KDIM
HGRN_VDIM = D_HGRN // HGRN_HEADS
CHUNK = 64
D_FF = 4 * D_MODEL
EPS = 1e-6
IN_SIZES = (D_CONV, D_CONV, D_CONV, D_HGRN, D_HGRN, D_HGRN, D_HGRN, D_MODEL, D_MODEL)
N_IN = sum(IN_SIZES)

kernel_name = "hybrid_shortconv_hgrn2_gated_merge_step"


def rmsnorm(x, g):
    xf = x.astype(jnp.float32)
    y = xf * lax.rsqrt(jnp.mean(xf * xf, axis=-1, keepdims=True) + EPS)
    return (y * g.astype(jnp.float32)).astype(x.dtype)


def hgrn2_chunked(q, k, v, log_f, s0):
    bn, t, h, _ = q.shape
    dv = v.shape[-1]
    L = min(CHUNK, t)
    pad = (-t) % L

    def prep(a):
        a = jnp.pad(a.astype(jnp.float32), ((0, 0), (0, pad), (0, 0), (0, 0)))
        return a.reshape(bn, -1, L, h, a.shape[-1]).transpose(0, 3, 1, 2, 4)

    qc, kc, vc, lf = prep(q), prep(k), prep(v), prep(log_f)
    b = jnp.cumsum(lf, axis=3)
    q_in = qc * jnp.exp(b)
    k_in = kc * jnp.exp(-b)
    scores = jnp.einsum('bhnld,bhnsd->bhnls', q_in, k_in)
    causal = jnp.tril(jnp.ones((L, L), dtype=bool))
    scores = jnp.where(causal, scores, 0.0)
    o_intra = jnp.einsum('bhnls,bhnse->bhnle', scores, vc)
    b_last = b[:, :, :, -1:, :]
    k_end = kc * jnp.exp(b_last - b)
    delta = jnp.einsum('bhnld,bhnle->nbhde', k_end, vc)
    decay = jnp.exp(b_last[:, :, :, 0, :]).transpose(2, 0, 1, 3)

    def step(S, inp):
        dec, dl = inp
        return dec[..., None] * S + dl, S

    s_fin, s_starts = lax.scan(step, s0.astype(jnp.float32), (decay, delta))
    o_inter = jnp.einsum('bhnld,nbhde->bhnle', q_in, s_starts)
    o = (o_intra + o_inter).transpose(0, 2, 3, 1, 4).reshape(bn, -1, h, dv)[:, :t]
    return o, s_fin


def layer(x, conv_state, hgrn_state, lb, norm_mix, w_in, conv_w, onorm_g,
          w_branch_a, w_branch_b, w_out, norm_ffn, w_up, w_down):
    bn, t, _ = x.shape
    hx = rmsnorm(x, norm_mix)
    proj = hx @ w_in
    cuts = [int(c) for c in np.cumsum(IN_SIZES)[:-1]]
    hc, bg, cg, q, fz, iv, og, ga, gb = jnp.split(proj, cuts, axis=-1)

    u = cg * hc
    buf = jnp.concatenate([conv_state.astype(u.dtype), u], axis=1)
    conv = conv_w[0] * buf[:, :t] + conv_w[1] * buf[:, 1:t + 1] + conv_w[2] * buf[:, 2:t + 2]
    a = bg * conv
    new_conv = buf[:, t:]

    f = lb + (1.0 - lb) * jax.nn.sigmoid(fz.astype(jnp.float32))
    log_f = jnp.log(f)
    kk = 1.0 - f
    shp = (bn, t, HGRN_HEADS, HGRN_KDIM)
    o, s_new = hgrn2_chunked(jax.nn.silu(q).reshape(shp), kk.reshape(shp),
                             iv.reshape(bn, t, HGRN_HEADS, HGRN_VDIM), log_f.reshape(shp),
                             hgrn_state)
    o = rmsnorm(o.astype(x.dtype), onorm_g) * jax.nn.silu(og.reshape(bn, t, HGRN_HEADS, HGRN_VDIM))
    o = o.reshape(bn, t, D_HGRN)

    mix = jax.nn.sigmoid(ga) * (a @ w_branch_a) + jax.nn.sigmoid(gb) * (o @ w_branch_b)
    x = x + mix @ w_out

    h2 = rmsnorm(x, norm_ffn)
    x = x + jnp.square(jax.nn.relu(h2 @ w_up)) @ w_down
    return x, new_conv, s_new.astype(hgrn_state.dtype)


def setup_inputs(seed: int = 0) -> dict:
    key = jax.random.key(seed)
    ks = jax.random.split(key, 16)
    f32 = jnp.float32
    nrm = lambda k, s, sc: jax.random.normal(k, s, f32) * sc
    return {
        "x_prompt": nrm(ks[0], (BATCH, SEQ, D_MODEL), 1.0),
        "x_sample": nrm(ks[1], (DEC_BATCH, DEC_SEQ, D_MODEL), 1.0),
        "state_conv": nrm(ks[2], (DEPTH, DEC_BATCH, CONV_WIDTH - 1, D_CONV), 1.0),
        "state_hgrn": nrm(ks[3], (DEPTH, DEC_BATCH, HGRN_HEADS, HGRN_KDIM, HGRN_VDIM), 1.0),
        "norm_mix": 1.0 + nrm(ks[4], (DEPTH, D_MODEL), 0.02),
        "w_in": nrm(ks[5], (DEPTH, D_MODEL, N_IN), D_MODEL ** -0.5),
        "conv_w": nrm(ks[6], (DEPTH, CONV_WIDTH, D_CONV), CONV_WIDTH ** -0.5),
        "lb_logits": nrm(ks[7], (DEPTH + 1, D_HGRN), 0.1),
        "onorm_g": 1.0 + nrm(ks[8], (DEPTH, HGRN_VDIM), 0.02),
        "w_branch_a": nrm(ks[9], (DEPTH, D_CONV, D_MODEL), D_CONV ** -0.5),
        "w_branch_b": nrm(ks[10], (DEPTH, D_HGRN, D_MODEL), D_HGRN ** -0.5),
        "w_out": nrm(ks[11], (DEPTH, D_MODEL, D_MODEL), D_MODEL ** -0.5),
        "norm_ffn": 1.0 + nrm(ks[12], (DEPTH, D_MODEL), 0.02),
        "w_up": nrm(ks[13], (DEPTH, D_MODEL, D_FF), D_MODEL ** -0.5),
        "w_down": nrm(ks[14], (DEPTH, D_FF, D_MODEL), D_FF ** -0.5),
        "norm_final": 1.0 + nrm(ks[15], (D_MODEL,), 0.02),
    }


def reference(x_prompt, x_sample, state_conv, state_hgrn, norm_mix, w_in, conv_w,
              lb_logits, onorm_g, w_branch_a, w_branch_b, w_out, norm_ffn, w_up,
              w_down, norm_final):
    lb_all = jnp.cumsum(jax.nn.softmax(lb_logits.astype(jnp.float32), axis=0), axis=0)
    yp, ys = x_prompt, x_sample
    conv_p, hgrn_p, conv_s, hgrn_s = [], [], [], []
    for l in range(DEPTH):
        params = (lb_all[l], norm_mix[l], w_in[l], conv_w[l], onorm_g[l], w_branch_a[l],
                  w_branch_b[l], w_out[l], norm_ffn[l], w_up[l], w_down[l])
        zc = jnp.zeros((yp.shape[0], CONV_WIDTH - 1, D_CONV), yp.dtype)
        zh = jnp.zeros((yp.shape[0], HGRN_HEADS, HGRN_KDIM, HGRN_VDIM), yp.dtype)
        yp, c, s = layer(yp, zc, zh, *params)
        conv_p.append(c)
        hgrn_p.append(s)
        ys, c, s = layer(ys, state_conv[l], state_hgrn[l], *params)
        conv_s.append(c)
        hgrn_s.append(s)
    yp = rmsnorm(yp, norm_final)
    ys = rmsnorm(ys, norm_final)
    return (yp, ys, jnp.stack(conv_p), jnp.stack(hgrn_p), jnp.stack(conv_s), jnp.stack(hgrn_s))
```

```python
import bisect
import os
from contextlib import ExitStack
import numpy as np
import concourse.bass as bass
import concourse.mybir as mybir
from concourse.bass_utils import run_bass_kernel_spmd

F32 = mybir.dt.float32
BF16 = mybir.dt.bfloat16
ALU = mybir.AluOpType
AF = mybir.ActivationFunctionType

D = 2048
NK = 16
TM = 1024
TSM = 16
TT = TM + TSM
EPS = 1e-6
SAME_SYNC = True
SCHEDULE = True
LOOKAHEAD = 12
SCHED_SEGS = (1, 2)
TBL_COST = 1.3
PRIO = 1
SLACK = 0.5
N_CORES = 8
ALLOC_LOG = []


class Sched:
    ENGS = ("pe", "act", "dve", "pool", "sp")

    def __init__(self):
        self.ops = []
        self.keys = {}
        self.bounds = []
        self.fences = set()
        self.region_sched = {}
        self.scratch = None
        self.dma_sem_ops = {}

    def op(self, eng, fn, reads=(), writes=(), dma=None, cost=0.3, lat=0.0, tbl=None):
        gi = len(self.ops)
        deps = set()
        ps_r = [k for k in reads if k and k[0] == "ps"]
        if ps_r:
            reads = [k for k in reads if not (k and k[0] == "ps")]
            writes = list(writes) + ps_r
        for k in reads:
            st = self.keys.get(k)
            if st is not None and st[0] is not None:
                deps.add(st[0])
        for k in writes:
            st = self.keys.get(k)
            if st is not None:
                if st[0] is not None:
                    deps.add(st[0])
                deps.update(st[1])
        for k in reads:
            self.keys.setdefault(k, [None, []])[1].append(gi)
        for k in writes:
            self.keys[k] = [gi, []]
        deps.discard(gi)
        self.ops.append(dict(eng=eng, fn=fn, deps=deps, dma=dma, gi=gi, cost=cost, lat=lat, tbl=tbl))
        return gi

    def fence(self, sched=False):
        self.bounds.append(len(self.ops))
        self.fences.add(len(self.ops))
        self.region_sched[len(self.ops)] = sched

    def mark(self, sched=False):
        self.bounds.append(len(self.ops))
        self.region_sched[len(self.ops)] = sched

    def barrier_op(self, lo, hi, key):
        gi = self.op("dve", lambda e: e.memset(self.scratch, 0.0), writes=[key], cost=0.1)
        self.ops[gi]["deps"] |= set(range(lo, hi))
        return gi

    def finalize(self):
        ops = self.ops
        bounds = [0] + [b for b in self.bounds if 0 < b < len(ops)] + [len(ops)]
        bounds = sorted(set(bounds))
        order = []
        t_eng = {e: 0.0 for e in self.ENGS}
        finish = {}
        for si in range(len(bounds) - 1):
            lo, hi = bounds[si], bounds[si + 1]
            seg = range(lo, hi)
            t0 = max(t_eng.values())
            for e in self.ENGS:
                t_eng[e] = t0
            if (not SCHEDULE) or (not self.region_sched.get(lo, False)):
                order.extend(seg)
                continue
            ndep = {}
            users = {}
            for i in seg:
                c = 0
                for d in ops[i]["deps"]:
                    if d >= lo:
                        c += 1
                        users.setdefault(d, []).append(i)
                ndep[i] = c
            blevel = {}
            for i in reversed(seg):
                m = 0.0
                for u in users.get(i, ()):
                    if blevel[u] > m:
                        m = blevel[u]
                blevel[i] = ops[i]["cost"] + ops[i]["lat"] + m
            ready = {e: [] for e in self.ENGS}
            import heapq
            for i in seg:
                if ndep[i] == 0:
                    heapq.heappush(ready[ops[i]["eng"]], i)
            nleft = hi - lo
            cur_tbl = [None]
            while nleft:
                best = None
                for e in self.ENGS:
                    cand = heapq.nsmallest(LOOKAHEAD, ready[e])
                    for i in cand:
                        o = ops[i]
                        est = t_eng[e]
                        for d in o["deps"]:
                            if d >= lo:
                                fd = finish[d] + (0.15 if ops[d]["eng"] != e else 0.1)
                                if fd > est:
                                    est = fd
                        pen = TBL_COST if (o["tbl"] is not None and o["tbl"] != cur_tbl[0]) else 0.0
                        if PRIO == 0:
                            key = (est + pen, i)
                        elif PRIO == 1:
                            key = (est + pen, -blevel[i], i)
                        else:
                            key = (round((est + pen) / SLACK), -blevel[i], i)
                        if best is None or key < best[0]:
                            best = (key, i, e, est, pen)
                _, i, e, est, pen = best
                ready[e].remove(i)
                heapq.heapify(ready[e])
                o = ops[i]
                if o["tbl"] is not None:
                    cur_tbl[0] = o["tbl"]
                t_eng[e] = est + pen + o["cost"]
                finish[i] = est + pen + o["cost"] + o["lat"]
                order.append(i)
                nleft -= 1
                for u in users.get(i, ()):
                    ndep[u] -= 1
                    if ndep[u] == 0:
                        heapq.heappush(ready[ops[u]["eng"]], u)
        self.est_total = max(t_eng.values())
        if os.environ.get("DBG_SCHED"):
            te = {e: 0.0 for e in self.ENGS}
            fin = {}
            seg_of = lambda i: bisect.bisect_right(bounds, i) - 1
            cur = 0
            seg_start = 0.0
            busy = {e: 0.0 for e in self.ENGS}
            for i in order + [None]:
                sg = seg_of(i) if i is not None else -1
                if sg != cur:
                    t0 = max(te.values())
                    print("SCHED seg %d: %.1f us  busy %s" % (cur, t0 - seg_start,
                          " ".join("%s=%.0f" % (e, busy[e]) for e in self.ENGS)))
                    seg_start = t0
                    busy = {e: 0.0 for e in self.ENGS}
                    for e in self.ENGS:
                        te[e] = t0
                    cur = sg
                if i is None:
                    break
                o = ops[i]
                e = o["eng"]
                est = te[e]
                for d in o["deps"]:
                    if d in fin:
                        fd = fin[d] + (0.15 if ops[d]["eng"] != e else 0.1)
                        est = max(est, fd)
                pen = 0.0
                if o["tbl"] is not None:
                    if o["tbl"] != getattr(self, "_dbg_tbl", None):
                        pen = TBL_COST
                    self._dbg_tbl = o["tbl"]
                te[e] = est + pen + o["cost"]
                busy[e] += o["cost"] + pen
                fin[i] = est + pen + o["cost"] + o["lat"]
            print("SCHED model total %.1f us, nops %d" % (max(te.values()), len(ops)))
        fence_list = sorted(self.fences)
        newpos = {old: new for new, old in enumerate(order)}
        new_ops = []
        for new, old in enumerate(order):
            o = ops[old]
            o["deps"] = set(newpos[d] for d in o["deps"])
            o["gi"] = new
            o["seg"] = bisect.bisect_right(fence_list, old)
            new_ops.append(o)
        self.ops = ops = new_ops
        last_on_eng = {}
        last_dma = {}
        pending = {}
        cur_seg = 0
        seg_last_eng, seg_last_dma = {}, {}
        for o in ops:
            if o["seg"] != cur_seg:
                deps = set(last_on_eng.values()) | set(last_dma.values())
                for e in self.ENGS:
                    pending[e] = pending.get(e, set()) | deps
                cur_seg = o["seg"]
            fd = pending.pop(o["eng"], None)
            if fd:
                o["deps"] |= fd
                o["deps"].discard(o["gi"])
            last_on_eng[o["eng"]] = o["gi"]
            if o["dma"] is not None:
                last_dma[o["dma"]] = o["gi"]
        self.dma_sem_ops = {}
        for o in ops:
            if o["dma"] is not None:
                self.dma_sem_ops.setdefault(o["dma"], []).append(o["gi"])

    def emit(self, nc, es):
        self.finalize()
        ops = self.ops
        compute_engs = ("pe", "act", "dve", "pool")
        needed = set()
        for o in ops:
            for d in o["deps"]:
                do = ops[d]
                if do["dma"] is not None:
                    continue
                if do["eng"] == o["eng"] and (o["eng"] == "pe" or not SAME_SYNC):
                    continue
                needed.add(d)
        ordinal = {}
        cnt = {e: 0 for e in compute_engs + ("sp",)}
        for o in ops:
            if o["dma"] is None and o["gi"] in needed:
                cnt[o["eng"]] += 1
                ordinal[o["gi"]] = cnt[o["eng"]]
        eng_sem = {e: es.enter_context(nc.semaphore("sem_" + e)) for e in compute_engs + ("sp",)}
        dma_sem = {k: es.enter_context(nc.semaphore("dsem_" + k)) for k in self.dma_sem_ops}
        per_eng = {e: [] for e in compute_engs + ("sp",)}
        for o in ops:
            per_eng[o["eng"]].append(o)

        def replay(engname, e):
            waited = {}
            for o in per_eng[engname]:
                wl = {}
                for d in o["deps"]:
                    do = ops[d]
                    if do["dma"] is not None:
                        lst = self.dma_sem_ops[do["dma"]]
                        n = bisect.bisect_left(lst, o["gi"])
                        key = ("d", do["dma"])
                        val = 16 * n
                    else:
                        if do["eng"] == engname and (engname == "pe" or not SAME_SYNC):
                            continue
                        key = ("e", do["eng"])
                        val = ordinal[d]
                    if val > wl.get(key, 0):
                        wl[key] = val
                for key, val in wl.items():
                    if waited.get(key, 0) >= val:
                        continue
                    waited[key] = val
                    sem = dma_sem[key[1]] if key[0] == "d" else eng_sem[key[1]]
                    e.wait_ge(sem, val)
                ins = o["fn"](e)
                if o["dma"] is not None:
                    ins.then_inc(dma_sem[o["dma"]], 16)
                elif o["gi"] in needed:
                    ins.then_inc(eng_sem[engname], 1)
            if engname == "sp":
                for k, lst in self.dma_sem_ops.items():
                    e.wait_ge(dma_sem[k], 16 * len(lst))

        block = es.enter_context(nc.Block())

        @block.tensor
        def _(e):
            replay("pe", e)

        @block.scalar
        def _(e):
            replay("act", e)

        @block.vector
        def _(e):
            replay("dve", e)

        @block.gpsimd
        def _(e):
            replay("pool", e)

        @block.sync
        def _(e):
            replay("sp", e)


def build_program(stop=99):
    nc = bass.Bass("TRN2", target_bir_lowering=False)

    def din(name, shape):
        return nc.dram_tensor(name, list(shape), F32, kind="ExternalInput").ap()

    def dout(name, shape):
        return nc.dram_tensor(name, list(shape), F32, kind="ExternalOutput").ap()

    x_pre = din("x_pre", [1024, D])
    x_main = din("x_main", [1024, D])
    x_smp = din("x_smp", [16, D])
    sconv = din("sconv", [16, 2048])
    shgrn = din("shgrn", [16, 8, 128, 128])
    gm_d = din("gm", [128, 16])
    gf_d = din("gf", [128, 16])
    cw_d = din("cw", [128, 24])
    lbl_d = din("lbl", [128, 16])
    og_d = din("ogg", [128, 1])
    nfin_d = din("nfin", [128, D])
    ident_d = din("ident", [128, 128])
    mask2_d = din("mask2", [128, 128])
    rmask_d = din("rmask", [128, 512])
    w_in = din("w_in", [D, 11264])
    w_a = din("w_branch_a", [1024, D])
    w_b = din("w_branch_b", [1024, D])
    w_out = din("w_out", [D, D])
    w_up = din("w_up", [D, 8192])
    w_down = din("w_down", [8192, D])

    y_main = dout("y_main", [1024, D])
    y_smp = dout("y_smp", [16, D])
    o_cp = dout("o_cp", [2, 1024])
    o_hp = dout("o_hp", [8, 128, 128])
    o_cs = dout("o_cs", [16, 2, 1024])
    o_hs = dout("o_hs", [16, 8, 128, 128])

    S = Sched()
    with ExitStack() as es:
        def sb(name, shape, dtype):
            return es.enter_context(nc.sbuf_tensor("s_" + name, list(shape), dtype))

        R1 = sb("R1", [128, 16 * TT], BF16)
        R2 = sb("R2", [128, 18 * D], BF16)
        R3 = sb("R3", [128, 16 * TT], BF16)
        W = [sb("W0", [128, 16, 512], BF16), sb("W1", [128, 16, 512], BF16)]
        E_BYTES = 30 * 1024
        EA = sb("EA", [128, E_BYTES // 2], BF16)
        idb = sb("idb", [128, 128], BF16)
        idf = sb("idf", [128, 128], F32)
        ones_b = sb("ones_b", [128, 128], BF16)
        mask2 = sb("mask2", [128, 128], F32)
        rmask = sb("rmask", [128, 512], F32)
        gm = sb("gm", [128, 16], F32)
        gf = sb("gf", [128, 16], F32)
        cw = sb("cw", [128, 24], F32)
        lbl = sb("lbl", [128, 16], F32)
        lbm = sb("lbm", [128, 8], F32)
        oml = sb("oml", [128, 8], F32)
        ogg = sb("ogg", [128, 1], F32)
        ssb = sb("ssb", [128, 8], F32)
        scT = sb("scT", [128, 256], F32)
        uo = sb("uo", [128, 8 * 18], F32)
        dec = sb("dec", [128, 32], F32)
        hxPt = sb("hxPt", [128, 16, 16], BF16)
        S.scratch = ssb[:, 6:7]
        PS = [es.enter_context(nc.psum_tensor("ps%d" % i, [128, 512], F32)) for i in range(8)]

        bank_ctr = [0]
        bank_set = [list(range(8))]

        def bank():
            bs_ = bank_set[0]
            b = bs_[bank_ctr[0] % len(bs_)]
            bank_ctr[0] += 1
            return b

        def view(raw, off, shape, dtype):
            n = 1
            for s_ in shape[1:]:
                n *= s_
            esz = 2 if dtype == BF16 else 4
            a = off // 2
            ln = n * esz // 2
            assert off % 4 == 0 and a + ln <= raw.shape[1], (off, shape, raw.shape)
            ap = raw[:, a:a + ln]
            if dtype != BF16:
                ap = ap.bitcast(dtype)
            if len(shape) == 3:
                ap = ap.rearrange("p (a b) -> p a b", a=shape[1])
            return ap

        class Arena:
            def __init__(self, raws):
                self.raws = [(r, 0, n) for (r, n) in raws]
                self.nbase = len(self.raws)
                self.reset()

            def reset(self):
                self.raws = self.raws[:self.nbase]
                self.cur = [0 for _ in self.raws]

            def add_region(self, raw, base, nbytes):
                self.raws.append((raw, base, nbytes))
                self.cur.append(0)

            def alloc(self, shape, dtype):
                n = 1
                for s_ in shape[1:]:
                    n *= s_
                nb = n * (2 if dtype == BF16 else 4)
                nb = (nb + 31) // 32 * 32
                for i, (raw, base, tot) in enumerate(self.raws):
                    if self.cur[i] + nb <= tot:
                        v = view(raw, base + self.cur[i], shape, dtype)
                        ALLOC_LOG.append((tuple(shape), str(dtype), i, self.cur[i]))
                        self.cur[i] += nb
                        return v
                raise RuntimeError("arena overflow %s" % (shape,))

        ar = Arena([(R3, 16 * TT * 2), (EA, E_BYTES)])

        hxM = R1[:, :].rearrange("p (k t) -> p k t", k=16)
        R2b = R2[:, :]
        aT = R2b[:, 0:8 * TT].rearrange("p (c t) -> p c t", c=8)
        oT = R2b[:, 8 * TT:16 * TT].rearrange("p (c t) -> p c t", c=8)
        hxP = R2b[:, 16 * TT:16 * TT + 16 * 1024].rearrange("p (k t) -> p k t", k=16)
        x1v = R2[:, :].bitcast(F32).rearrange("p (t c) -> p t c", t=9)
        mixT = R3[:, :].rearrange("p (k t) -> p k t", k=16)
        actT = mixT
        h2T = hxM

        def psb(b):
            return PS[b][:, :].bitcast(BF16)

        XR = []

        def ncols_of(ap):
            n = 1
            for d_ in tuple(ap.shape)[1:]:
                n *= int(d_)
            return n

        def dma(eng, out, in_, sem, reads=(), writes=()):
            nbytes = ncols_of(out) * int(tuple(out.shape)[0]) * 4
            S.op(eng, lambda e: e.dma_start(out=out, in_=in_), reads=list(reads) + XR, writes=writes, dma=sem,
                 cost=(1.3 if eng == "pool" else 0.3), lat=2.0 + nbytes / 300e3)

        def mm_group(out_ap, pairs, reads, writes):
            def fn(e):
                n = len(pairs)
                ins = None
                for i, (l, r) in enumerate(pairs):
                    ins = e.matmul(out_ap, lhsT=l, rhs=r, start=(i == 0), stop=(i == n - 1))
                return ins
            cost = sum(max(ncols_of(r), 64) / 2400.0 + 0.01 for (_, r) in pairs)
            S.op("pe", fn, reads=list(reads) + XR, writes=writes, cost=cost)

        def act(out, in_, func, reads, writes, **kw):
            tbl = "sig" if func == AF.Sigmoid else ("lnexp" if func in (AF.Ln, AF.Exp) else None)
            S.op("act", lambda e: e.activation(out=out, in_=in_, func=func, **kw), reads=list(reads) + XR, writes=writes,
                 cost=0.22 + ncols_of(out) / 1200.0, tbl=tbl)

        def vcost(eng, out):
            return (0.25 + ncols_of(out) / 480.0) if eng == "pool" else (0.12 + ncols_of(out) / 960.0)

        def tt_op(out, in0, in1, op, reads, writes, eng="dve"):
            S.op(eng, lambda e: e.tensor_tensor(out=out, in0=in0, in1=in1, op=op), reads=list(reads) + XR, writes=writes,
                 cost=vcost(eng, out))

        def cp_op(out, in_, reads, writes, eng="dve"):
            S.op(eng, lambda e: e.tensor_copy(out=out, in_=in_), reads=list(reads) + XR, writes=writes, cost=vcost(eng, out))

        def ts_op(out, in0, s1, s2, op0, op1, reads, writes, eng="dve"):
            if op1 is None:
                S.op(eng, lambda e: e.tensor_scalar(out=out, in0=in0, scalar1=s1, scalar2=None, op0=op0),
                     reads=list(reads) + XR, writes=writes, cost=vcost(eng, out))
            else:
                S.op(eng, lambda e: e.tensor_scalar(out=out, in0=in0, scalar1=s1, scalar2=s2, op0=op0, op1=op1),
                     reads=list(reads) + XR, writes=writes, cost=vcost(eng, out))

        def stt_op(out, in0, scalar, in1, op0, op1, reads, writes, eng="dve"):
            S.op(eng, lambda e: e.scalar_tensor_tensor(out=out, in0=in0, scalar=scalar, in1=in1, op0=op0, op1=op1),
                 reads=list(reads) + XR, writes=writes, cost=vcost(eng, out))

        WK = lambda s: [("w", s, q) for q in range(4)]

        def wcols(src, c0, n, nkc=16):
            return src[:, c0:c0 + n].rearrange("(kc p) c -> p kc c", p=128)

        dma("pool", idb[:, :], ident_d, "constp", writes=[("c", "idb")])
        for t_, d_, nm in ((idf, ident_d, "idf"), (mask2, mask2_d, "mask2"), (rmask, rmask_d, "rmask"),
                           (gm, gm_d, "gm"), (gf, gf_d, "gf"), (cw, cw_d, "cw"), (lbl, lbl_d, "lbl"),
                           (ogg, og_d, "ogg")):
            dma("sp", t_[:, :], d_, "const", writes=[("c", nm)])
        S.op("dve", lambda e: e.memset(ones_b[:, :], 1.0), writes=[("c", "ones")])
        tt_op(lbm[:, :], lbl[:, 0:8], lbl[:, 8:16], ALU.subtract, [("c", "lbl")], [("c", "lbm")])
        act(lbm[:, :], lbm[:, :], AF.Sigmoid, [("c", "lbm")], [("c", "lbm")])
        ts_op(oml[:, :], lbm[:, :], -1.0, 1.0, ALU.mult, ALU.add, [("c", "lbm")], [("c", "oml")])

        sct = ar.alloc([128, 2048], F32)
        dma("sp", sct[0:16, :], sconv, "misc", writes=[("t", "sct")])
        dma("sp", o_cs[:, 0, :], sct[0:16, 1024:2048], "out", reads=[("t", "sct")])
        b0 = bank()
        def fn_sct(e):
            ins = None
            for j in range(16):
                ins = e.matmul(PS[b0][:, j * 16:(j + 1) * 16], lhsT=sct[0:16, j * 128:(j + 1) * 128],
                               rhs=idf[0:16, 0:16], start=True, stop=True)
            return ins
        S.op("pe", fn_sct, reads=[("t", "sct"), ("c", "idf")], writes=[("ps", b0)], cost=0.6)
        S.op("act", lambda e: e.copy(out=scT[:, :], in_=PS[b0][:, 0:256]), reads=[("ps", b0)], writes=[("c", "scT")])
        scTv = scT[:, :].rearrange("p (r c s) -> p r c s", r=2, c=8)

        fills = []

        def add_fill(fn):
            fills.append(fn)

        def fill_C(c):
            def f(s):
                for blk, base in enumerate((0, 1024, 2048)):
                    dma("pool", W[s][:, :, blk * 128:(blk + 1) * 128], wcols(w_in, base + c * 128, 128),
                        "w%db%d" % (s, blk), writes=[("w", s, blk)])
            return f

        def fill_H(h):
            def f(s):
                for blk, base in enumerate((3072, 4096, 5120, 6144)):
                    dma("pool", W[s][:, :, blk * 128:(blk + 1) * 128], wcols(w_in, base + h * 128, 128),
                        "w%db%d" % (s, blk), writes=[("w", s, blk)])
            return f

        def fill_J(j):
            def f(s):
                dma("pool", W[s][:, :, 0:128], wcols(w_in, 7168 + j * 128, 128), "w%db0" % s, writes=[("w", s, 0)])
                dma("pool", W[s][:, :, 128:256], wcols(w_in, 9216 + j * 128, 128), "w%db1" % s, writes=[("w", s, 1)])
                dma("pool", W[s][:, 0:8, 256:384], wcols(w_a, j * 128, 128), "w%db2" % s, writes=[("w", s, 2)])
                dma("pool", W[s][:, 0:8, 384:512], wcols(w_b, j * 128, 128), "w%db3" % s, writes=[("w", s, 3)])
            return f

        def fill_full(src, r0, c0):
            def f(s):
                dma("pool", W[s][:, :, :], src[r0:r0 + 2048, c0:c0 + 512].rearrange("(kc p) c -> p kc c", p=128),
                    "w%df" % s, writes=WK(s))
            return f

        def fill_P(h):
            def f(s):
                dma("pool", W[s][:, :, 128:256], wcols(w_in, 4096 + h * 128, 128), "w%db1" % s, writes=[("w", s, 1)])
                dma("pool", W[s][:, :, 256:384], wcols(w_in, 5120 + h * 128, 128), "w%db2" % s, writes=[("w", s, 2)])
            return f

        for h in range(8):
            add_fill(fill_P(h))
        for c in range(8):
            add_fill(fill_C(c))
            add_fill(fill_H(c))
        for j in range(16):
            add_fill(fill_J(j))
        for n in range(4):
            add_fill(fill_full(w_out, 0, n * 512))
        for g in range(4):
            for q4 in range(4):
                add_fill(fill_full(w_up, 0, g * 2048 + q4 * 512))
            for n in range(4):
                add_fill(fill_full(w_down, g * 2048, n * 512))
        fill_i = [0]

        fills[0](0)
        fills[1](1)

        def next_slot():
            k = fill_i[0]
            if k >= 1 and k + 1 < len(fills):
                fills[k + 1]((k + 1) % 2)
            fill_i[0] += 1
            return k % 2

        def norm_transpose(tiles, gvec, gname, xs_bufs, junk, dst_fn, tag):
            for (src, r, rkeys, i) in tiles:
                sl = i % 2
                ssv = ssb[:, sl * 2:sl * 2 + 2]
                kss = ("ss", sl)
                S.op("pool", lambda e, ssv=ssv: e.memset(ssv, 0.0), writes=[kss])
                act(junk[0:r, :], src, AF.Square, rkeys + [kss], [("t", "junk"), kss], accum_out=ssv[0:r, 0:1])
                act(ssv[0:r, 1:2], ssv[0:r, 0:1], AF.Ln, [kss], [kss], scale=1.0 / D, bias=EPS)
                act(ssv[0:r, 1:2], ssv[0:r, 1:2], AF.Exp, [kss], [kss], scale=-0.5)
                xs = xs_bufs[sl]
                kxs0 = ("t", tag + "xs", sl, 0)
                kxs1 = ("t", tag + "xs", sl, 1)
                act(xs[0:r, 0:1024], src[:, 0:1024], AF.Copy, rkeys + [kss], [kxs0], scale=ssv[0:r, 1:2])
                ts_op(xs[0:r, 1024:2048], src[:, 1024:2048], ssv[0:r, 1:2], None, ALU.mult, None, rkeys + [kss], [kxs1])
                for hh in range(2):
                    kxs = kxs0 if hh == 0 else kxs1
                    b = bank()
                    pb = psb(b)

                    def fn(e, pb=pb, xs=xs, r=r, hh=hh):
                        ins = None
                        for k in range(8):
                            kc = hh * 8 + k
                            ins = e.transpose(out=pb[:, k * 128:k * 128 + r], in_=xs[0:r, kc * 128:(kc + 1) * 128],
                                              identity=idb[0:r, 0:r])
                        return ins
                    S.op("pe", fn, reads=[kxs, ("c", "idb")], writes=[("ps", b)], cost=0.65)
                    pv = pb[:, 0:1024].rearrange("p (k t) -> p k t", k=8)[:, :, 0:r]
                    gb = gvec[:, hh * 8:(hh + 1) * 8].unsqueeze(2).to_broadcast([128, 8, r])
                    dst, dkeys = dst_fn(i, hh)
                    tt_op(dst, pv, gb, ALU.mult, [("ps", b), ("c", gname)], dkeys)

        def hx_keys(i0, i1):
            return [("hx", i, hh) for i in range(i0, i1) for hh in range(2)]
        hx_keys_early = hx_keys

        xt = [ar.alloc([128, D], F32) for _ in range(3)]
        xs0 = [ar.alloc([128, D], BF16), ar.alloc([128, D], BF16)]
        junk0 = ar.alloc([128, D], BF16)
        tiles0 = []
        for i in range(17):
            if i < 8:
                src, r = x_pre[i * 128:(i + 1) * 128, :], 128
            elif i < 16:
                src, r = x_main[(i - 8) * 128:(i - 7) * 128, :], 128
            else:
                src, r = x_smp, 16
            tiles0.append((i, src, r))

        def dst0(i, hh):
            if i < 8:
                return hxP[:, hh * 8:(hh + 1) * 8, i * 128:(i + 1) * 128], [("hx", i, hh)]
            if i < 16:
                return hxM[:, hh * 8:(hh + 1) * 8, (i - 8) * 128:(i - 7) * 128], [("hx", i, hh)]
            return hxM[:, hh * 8:(hh + 1) * 8, 1024:1040], [("hx", i, hh)]

        tl = []
        for (i, src, r) in tiles0:
            sl = i % 3
            dma("sp", xt[sl][0:r, :], src, "xt%d" % sl, writes=[("t", "xt", sl)])
            tl = [(xt[sl][0:r, :], r, [("t", "xt", sl)], i)]
            norm_transpose(tl, gm, "gm", xs0, junk0, dst0, "s0")
        cp_op(hxPt[:, :, :], hxP[:, :, 1008:1024], hx_keys_early(7, 8), [("hxPt",)])
        S.fence(sched=True)
        P_LO = len(S.ops)
        if stop == 0:
            S.emit(nc, es)
            return nc
        ar.reset()


        R2SP = (16 * TT + 16 * 1024) * 2
        S_init = view(R2, R2SP, [128, 8, 128], F32)
        ar.add_region(R2, R2SP + 4096, (18 * D * 2 - R2SP) - 4096)
        t_f = [ar.alloc([128, 512], F32) for _ in range(2)]
        t_q = [ar.alloc([128, 512], F32) for _ in range(2)]
        t_v = [ar.alloc([128, 512], BF16) for _ in range(2)]
        t_ke = [ar.alloc([128, 512], BF16) for _ in range(2)]
        t_lf = ar.alloc([128, 512], F32)
        t_b = ar.alloc([128, 512], F32)
        t_eb = ar.alloc([128, 512], F32)
        t_enb = ar.alloc([128, 512], F32)
        k_inT = ar.alloc([128, 1024], BF16)
        q_inT = ar.alloc([128, 1024], BF16)
        ke_tok = ar.alloc([128, 16, 128], BF16)
        v_tok = ar.alloc([128, 16, 128], BF16)
        sog = ar.alloc([128, TT], F32)
        S_pp = ar.alloc([128, 2, 128], F32)
        S_bf = ar.alloc([128, 16, 128], BF16)
        S0b = [ar.alloc([128, 8, 128], F32) for _ in range(2)]
        S_bfs = ar.alloc([128, 8, 128], BF16)
        vm = ar.alloc([128, 16, 128], BF16)
        fS = ar.alloc([128, 16], F32)
        fS2 = ar.alloc([128, 16], F32)
        sogS = [ar.alloc([128, 16], F32) for _ in range(2)]
        kS_b = ar.alloc([128, 16], BF16)
        vS_b = ar.alloc([128, 16], BF16)
        qS_b = ar.alloc([128, 16], BF16)
        ktok_s = ar.alloc([128, 128], BF16)
        vtok_s = ar.alloc([128, 128], BF16)
        scm = [ar.alloc([128, 128], BF16), ar.alloc([128, 128], BF16)]
        osq = ar.alloc([128, 512], BF16)
        rstd_t = ar.alloc([128, 512], F32)
        on_t = ar.alloc([128, 512], F32)
        K = lambda *n: ("t",) + n
        BOS, BO2 = 5, [6]
        nt_ctr = [0]

        NTS = [
            ("P", lambda kc: hxP[:, kc, 0:512], hx_keys(0, 4), 0),
            ("P", lambda kc: hxP[:, kc, 512:1024], hx_keys(4, 8), 4),
            ("M", lambda kc: hxM[:, kc, 0:512], hx_keys(8, 12), 8),
            ("M", lambda kc: hxM[:, kc, 512:1024], hx_keys(12, 16), 12),
        ]

        def head_ops(h, s):
            Wq = [W[s][:, kc, 0:128] for kc in range(16)]
            Wf = [W[s][:, kc, 128:256] for kc in range(16)]
            Wi = [W[s][:, kc, 256:384] for kc in range(16)]
            Wo = [W[s][:, kc, 384:512] for kc in range(16)]
            lb_h, oml_h = lbm[:, h:h + 1], oml[:, h:h + 1]
            st = {}

            HV = ((0, 256), (256, 512))

            def front(nt):
                kind, hsrc, hk, tb = NTS[nt]
                p = nt_ctr[0] % 2
                nt_ctr[0] += 1
                st[nt] = p
                mcol = (tb - 8) * 128
                tf, tq, tv = t_f[p], t_q[p], t_v[p]
                bf_ = bank()
                mm_group(PS[bf_][:, :], [(Wf[kc], hsrc(kc)) for kc in range(16)], [("w", s, 1)] + hk, [("ps", bf_)])
                bi = bank()
                mm_group(PS[bi][:, :], [(Wi[kc], hsrc(kc)) for kc in range(16)], [("w", s, 2)] + hk, [("ps", bi)])
                for hv, (a, b_) in enumerate(HV):
                    act(tf[:, a:b_], PS[bf_][:, a:b_], AF.Sigmoid, [("ps", bf_)], [K("f", p, hv)])
                act(tv[:, :], PS[bi][:, :], AF.Copy, [("ps", bi)], [K("v", p)])
                if kind == "M":
                    bq = bank()
                    mm_group(PS[bq][:, :], [(Wq[kc], hsrc(kc)) for kc in range(16)], [("w", s, 0)] + hk, [("ps", bq)])
                    bo = bank()
                    mm_group(PS[bo][:, :], [(Wo[kc], hsrc(kc)) for kc in range(16)], [("w", s, 3)] + hk, [("ps", bo)])
                    act(tq[:, :], PS[bq][:, :], AF.Sigmoid, [("ps", bq)], [K("q", p)])
                    act(sog[:, mcol:mcol + 512], PS[bo][:, :], AF.Sigmoid, [("ps", bo)], [K("sog", nt)])
                    tt_op(tq[:, :], tq[:, :], PS[bq][:, :], ALU.mult, [K("q", p), ("ps", bq)], [K("q", p)])
                    tt_op(sog[:, mcol:mcol + 512], sog[:, mcol:mcol + 512], PS[bo][:, :], ALU.mult,
                          [K("sog", nt), ("ps", bo)], [K("sog", nt)])

            def chain(nt):
                kind, hsrc, hk, tb = NTS[nt]
                p = st[nt]
                mcol = (tb - 8) * 128
                tf, tq, tke = t_f[p], t_q[p], t_ke[p]
                kq = K("q", p)
                n0 = tb * 2
                c3 = lambda ap: ap.rearrange("p (c l) -> p c l", l=64)
                steps = []
                for hv, (a, b_) in enumerate(HV):
                    kf, klf, kb, keb, kenb = K("f", p, hv), K("lf", hv), K("b", hv), K("eb", hv), K("enb", hv)
                    kke = K("ke", p) if False else K("ke", p, hv)
                    sl = slice(a, b_)
                    ebv = c3(t_eb[:, sl])
                    ops = []
                    ops.append(lambda kf=kf, sl=sl: ts_op(tf[:, sl], tf[:, sl], oml_h, lb_h, ALU.mult, ALU.add,
                                                          [kf, ("c", "oml"), ("c", "lbm")], [kf]))
                    ops.append(lambda kf=kf, klf=klf, sl=sl: act(t_lf[:, sl], tf[:, sl], AF.Ln, [kf], [klf]))
                    ops.append(lambda klf=klf, kb=kb, sl=sl: S.op(
                        "dve", lambda e: e.tensor_tensor_scan(out=t_b[:, sl], data0=rmask[:, sl], data1=t_lf[:, sl],
                                                              initial=0.0, op0=ALU.mult, op1=ALU.add),
                        reads=[klf, ("c", "rmask")], writes=[kb], cost=0.7))
                    ops.append(lambda kb=kb, keb=keb, sl=sl: act(t_eb[:, sl], t_b[:, sl], AF.Exp, [kb], [keb]))
                    ops.append(lambda kb=kb, kenb=kenb, sl=sl: act(t_enb[:, sl], t_b[:, sl], AF.Exp, [kb], [kenb], scale=-1.0))
                    ops.append(lambda kf=kf, sl=sl: ts_op(tf[:, sl], tf[:, sl], -1.0, 1.0, ALU.mult, ALU.add, [kf], [kf],
                                                          eng="pool"))
                    ops.append(lambda keb=keb, ebv=ebv, hv=hv: act(dec[:, n0 + 4 * hv:n0 + 4 * hv + 4], ebv[:, :, 63], AF.Copy,
                                                                   [keb], [K("dec", nt)]))
                    if kind == "M":
                        ops.append(lambda kf=kf, kenb=kenb, sl=sl, a=a, b_=b_: tt_op(
                            k_inT[:, mcol + a:mcol + b_], tf[:, sl], t_enb[:, sl], ALU.mult, [kf, kenb], [K("kin", nt)]))
                        ops.append(lambda keb=keb, sl=sl, a=a, b_=b_: tt_op(
                            q_inT[:, mcol + a:mcol + b_], tq[:, sl], t_eb[:, sl], ALU.mult, [kq, keb], [K("qin", nt)]))
                    ops.append(lambda kenb=kenb, keb=keb, klf=klf, sl=sl, ebv=ebv: tt_op(
                        c3(t_lf[:, sl]), c3(t_enb[:, sl]), ebv[:, :, 63:64].to_broadcast([128, 4, 64]), ALU.mult,
                        [kenb, keb, klf], [klf]))
                    ops.append(lambda klf=klf, kf=kf, sl=sl: tt_op(tke[:, sl], t_lf[:, sl], tf[:, sl], ALU.mult,
                                                                   [klf, kf], [K("ke", p)]))
                    steps.append(ops)
                for i in range(len(steps[0])):
                    steps[0][i]()
                    steps[1][i]()

            def trans(nt):
                kind, hsrc, hk, tb = NTS[nt]
                p = st[nt]
                for (src, ksrc, dst, kdst, use_act) in ((t_v[p], K("v", p), v_tok, K("vtok", nt), False),
                                                        (t_ke[p], K("ke", p), ke_tok, K("ketok", nt), True)):
                    bt = bank()
                    pbt = psb(bt)

                    def fn_t(e, pbt=pbt, src=src):
                        ins = None
                        for k in range(4):
                            ins = e.transpose(out=pbt[:, k * 128:(k + 1) * 128], in_=src[:, k * 128:(k + 1) * 128],
                                              identity=idb[:, :])
                        return ins
                    S.op("pe", fn_t, reads=[ksrc, ("c", "idb")], writes=[("ps", bt)], cost=0.35)
                    pv = pbt[:, 0:512].rearrange("p (k t) -> p k t", k=4)
                    if use_act:
                        act(dst[:, tb:tb + 4, :], pv, AF.Copy, [("ps", bt)], [kdst])
                    else:
                        cp_op(dst[:, tb:tb + 4, :], pv, [("ps", bt)], [kdst])

            def scan(part):
                if part == 0:
                    S.op("dve", lambda e: e.memset(S_pp[:, 0, :], 0.0), writes=[K("Spp", 0)])
                else:
                    act(S_pp[:, 0, :], S_init[:, h, :], AF.Copy, [("Sinit", h)], [K("Spp", 0)])
                for grp in (part * 2, part * 2 + 1):
                    bdp = [bank(), bank()]

                    def fn_d(e, grp=grp, bdp=bdp):
                        ins = None
                        for j in range(4):
                            for a_ in range(2):
                                tt_ = grp * 4 + j
                                r0 = a_ * 64
                                ins = e.matmul(PS[bdp[a_]][:, j * 128:(j + 1) * 128], lhsT=ke_tok[r0:r0 + 64, tt_, :],
                                               rhs=v_tok[r0:r0 + 64, tt_, :], start=True, stop=True)
                        return ins
                    S.op("pe", fn_d, reads=[K("ketok", grp), K("vtok", grp)], writes=[("ps", bdp[0]), ("ps", bdp[1])], cost=0.6)
                    for j in range(4):
                        for a_ in range(2):
                            n = grp * 8 + j * 2 + a_
                            cur, nxt = n % 2, (n + 1) % 2
                            if n >= 16:
                                act(S_bf[:, n - 16, :], S_pp[:, cur, :], AF.Copy, [K("Spp", cur)], [K("Sbf", (n - 16) // 2)])
                            stt_op(S_pp[:, nxt, :], S_pp[:, cur, :], dec[:, n:n + 1],
                                   PS[bdp[a_]][:, j * 128:(j + 1) * 128], ALU.mult, ALU.add,
                                   [K("Spp", cur), K("dec", grp), ("ps", bdp[a_])], [K("Spp", nxt)])
                if part == 1:
                    dma("sp", o_hp[h, :, :], S_pp[:, 0, :], "ohp", reads=[K("Spp", 0)])

            def o_main(half):
                pend = None
                for tt_ in range(half * 4, half * 4 + 4):
                    nt = 2 + tt_ // 4
                    bsc = bank()
                    mm_group(PS[bsc][:, 0:128],
                             [(k_inT[:, tt_ * 128:(tt_ + 1) * 128], q_inT[:, tt_ * 128:(tt_ + 1) * 128])],
                             [K("kin", nt), K("qin", nt)], [("ps", bsc)])
                    sc_ = scm[tt_ % 2]
                    ksc = K("scm", tt_ % 2)
                    tt_op(sc_[:, :], PS[bsc][:, 0:128], mask2[:, :], ALU.mult, [("ps", bsc), ("c", "mask2")], [ksc])
                    pob = PS[BO2[0]]
                    c0 = (tt_ % 4) * 128

                    def fn_o(e, tt_=tt_, pob=pob, c0=c0, sc_=sc_):
                        ins = None
                        for a_ in range(2):
                            oc = c0 + 64 * a_
                            e.matmul(pob[:, oc:oc + 64], lhsT=S_bf[:, 2 * tt_ + a_, :],
                                     rhs=q_inT[:, tt_ * 128 + 64 * a_:tt_ * 128 + 64 * a_ + 64], start=True, stop=False)
                            ins = e.matmul(pob[:, oc:oc + 64], lhsT=v_tok[:, 8 + tt_, :], rhs=sc_[:, 64 * a_:64 * a_ + 64],
                                           start=False, stop=True)
                        return ins

                    def emit_o(fn_o=fn_o, tt_=tt_, nt=nt, ksc=ksc):
                        S.op("pe", fn_o, reads=[K("vtok", nt), ksc, K("Sbf", tt_), K("qin", nt)]
                             + ([("ps", BO2[0])] if tt_ % 4 else []), writes=[("ps", BO2[0])])
                    if pend is not None:
                        pend()
                    pend = emit_o
                pend()

            def norm(piece):
                for (pb_, ncol, dcol, ksog) in (((BO2[0], 512, 0, K("sog", 2)), (BO2[0], 512, 512, K("sog", 3)),
                                                 (BOS, 16, 1024, ksgS))[piece],):
                    if pb_ == BOS:
                        po_ap = PS[pb_][:, 0:272].rearrange("p (a b) -> p a b", b=17)[:, :, 0]
                    else:
                        po_ap = PS[pb_][:, 0:ncol]
                    act(osq[:, 0:ncol], po_ap, AF.Square, [("ps", pb_)], [K("osq")])
                    bss = bank()
                    mm_group(PS[bss][:, 0:ncol], [(ones_b[:, :], osq[:, 0:ncol])], [K("osq"), ("c", "ones")], [("ps", bss)])
                    act(rstd_t[:, 0:ncol], PS[bss][:, 0:ncol], AF.Ln, [("ps", bss)], [K("rstd")], scale=1.0 / 128, bias=EPS)
                    act(rstd_t[:, 0:ncol], rstd_t[:, 0:ncol], AF.Exp, [K("rstd")], [K("rstd")], scale=-0.5)
                    tt_op(on_t[:, 0:ncol], po_ap, rstd_t[:, 0:ncol], ALU.mult, [("ps", pb_), K("rstd")], [K("on")])
                    sg_ap = sgS[:, :] if pb_ == BOS else sog[:, dcol:dcol + ncol]
                    stt_op(oT[:, h, dcol:dcol + ncol], on_t[:, 0:ncol], ogg[:, 0:1], sg_ap,
                           ALU.mult, ALU.mult, [K("on"), ksog, ("c", "ogg")], [("oT",)])

            sgS = sogS[h % 2]
            ksgS = K("sogS", h % 2)
            sv = {}

            def s1():
                bs = bank()

                def fn_s(e, bs=bs):
                    ins = None
                    for blk, Wx in enumerate((Wq, Wf, Wi, Wo)):
                        for kc in range(16):
                            ins = e.matmul(PS[bs][:, blk * 16:(blk + 1) * 16], lhsT=Wx[kc], rhs=hxM[:, kc, 1024:1040],
                                           start=(kc == 0), stop=(kc == 15))
                    return ins
                S.op("pe", fn_s, reads=WK(s) + hx_keys(16, 17), writes=[("ps", bs)], cost=2.0)
                act(fS[:, :], PS[bs][:, 16:32], AF.Sigmoid, [("ps", bs)], [K("fS")])
                ts_op(fS[:, :], fS[:, :], oml_h, lb_h, ALU.mult, ALU.add, [K("fS"), ("c", "oml"), ("c", "lbm")], [K("fS")])
                ts_op(kS_b[:, :], fS[:, :], -1.0, 1.0, ALU.mult, ALU.add, [K("fS")], [K("kS")])
                act(vS_b[:, :], PS[bs][:, 32:48], AF.Copy, [("ps", bs)], [K("vS")])
                act(fS2[:, :], PS[bs][:, 0:16], AF.Sigmoid, [("ps", bs)], [K("fS2")])
                tt_op(qS_b[:, :], fS2[:, :], PS[bs][:, 0:16], ALU.mult, [K("fS2"), ("ps", bs)], [K("qS")])
                act(sgS[:, :], PS[bs][:, 48:64], AF.Sigmoid, [("ps", bs)], [ksgS])
                tt_op(sgS[:, :], sgS[:, :], PS[bs][:, 48:64], ALU.mult, [ksgS, ("ps", bs)], [ksgS])

            def s2():
                bts = bank()
                pbts = psb(bts)

                def fn_ts(e, pbts=pbts):
                    e.transpose(out=pbts[0:16, 0:128], in_=kS_b[:, :], identity=idb[:, :])
                    return e.transpose(out=pbts[0:16, 128:256], in_=vS_b[:, :], identity=idb[:, :])
                S.op("pe", fn_ts, reads=[K("kS"), K("vS"), ("c", "idb")], writes=[("ps", bts)])
                act(ktok_s[0:16, :], pbts[0:16, 0:128], AF.Copy, [("ps", bts)], [K("ktoks")])
                act(vtok_s[0:16, :], pbts[0:16, 128:256], AF.Copy, [("ps", bts)], [K("vtoks")])
                tt_op(vm[0:16, :, :], vtok_s[0:16, :].unsqueeze(1).to_broadcast([16, 16, 128]),
                      idb[0:16, 0:16].unsqueeze(2).to_broadcast([16, 16, 128]), ALU.mult,
                      [K("vtoks"), ("c", "idb")], [K("vm")])

            def s3():
                for hf in range(2):
                    S0 = S0b[hf]
                    kS0 = K("S0", hf)
                    bd = [bank(), bank()]
                    for q_ in range(2):
                        mm_group(PS[bd[q_]][:, :],
                                 [(ktok_s[0:16, :], vm[0:16, hf * 8 + q_ * 4:hf * 8 + q_ * 4 + 4, :])],
                                 [K("ktoks"), K("vm")], [("ps", bd[q_])])
                    tt_op(S0[:, :, :], S0[:, :, :], fS[:, hf * 8:(hf + 1) * 8].unsqueeze(2).to_broadcast([128, 8, 128]),
                          ALU.mult, [kS0, K("fS")], [kS0])
                    for q_ in range(2):
                        tt_op(S0[:, q_ * 4:(q_ + 1) * 4, :], S0[:, q_ * 4:(q_ + 1) * 4, :],
                              PS[bd[q_]][:, :].rearrange("p (b e) -> p b e", b=4), ALU.add,
                              [kS0, ("ps", bd[q_])], [kS0])
                    dma("sp", o_hs[hf * 8:(hf + 1) * 8, h, :, :].rearrange("b d e -> d b e"), S0[:, :, :],
                        "s0out%d" % hf, reads=[kS0])

            def s4():
                for hf in range(2):
                    S0 = S0b[hf]
                    kS0 = K("S0", hf)
                    act(S_bfs[:, :, :], S0[:, :, :], AF.Copy, [kS0], [K("Sbfs")])

                    def fn_os(e, hf=hf):
                        ins = None
                        for b_ in range(8):
                            col = hf * 8 + b_
                            ins = e.matmul(PS[BOS][:, col * 16:(col + 1) * 16], lhsT=S_bfs[:, b_, :], rhs=qS_b[:, 0:16],
                                           start=True, stop=True)
                        return ins
                    S.op("pe", fn_os, reads=[K("Sbfs"), K("qS")] + ([("ps", BOS)] if hf else []), writes=[("ps", BOS)])

            def prefetch():
                for hf in range(2):
                    dma("sp", S0b[hf][:, :, :], shgrn[hf * 8:(hf + 1) * 8, h, :, :].rearrange("b d e -> d b e"),
                        "s0ld%d" % hf, writes=[K("S0", hf)])

            def save_init():
                act(S_init[:, h, :], S_pp[:, 0, :], AF.Copy, [K("Spp", 0)], [("Sinit", h)])

            return dict(front=front, chain=chain, trans=trans, scan=scan, o_main=o_main, norm=norm, save_init=save_init,
                        s1=s1, s2=s2, s3=s3, s4=s4, prefetch=prefetch)


        bank_set[0] = list(range(8))
        prevP = None
        for h in range(8):
            s = next_slot()
            Hp = head_ops(h, s)
            Hp["front"](0)
            Hp["front"](1)
            if prevP is not None:
                prevP["scan"](0)
                prevP["save_init"]()
            Hp["chain"](0)
            Hp["trans"](0)
            Hp["chain"](1)
            Hp["trans"](1)
            prevP = Hp
        prevP["scan"](0)
        prevP["save_init"]()
        S.barrier_op(P_LO, len(S.ops), ("bar", "P"))
        bank_set[0] = [0, 1, 2, 3, 4, 7]
        arC = Arena([])
        arC.add_region(R2, 16 * TT * 2, 16 * 1024 * 2)
        cbuf = [[arC.alloc([128, 1042], F32) for _ in range(3)] for _ in range(2)]
        accs_b = [arC.alloc([128, 16], F32) for _ in range(2)]
        uov = uo[:, :].rearrange("p (c t) -> p c t", c=8)
        cwv = cw[:, :].rearrange("p (c k) -> p c k", c=8)

        def C_blocks(c, s):
            par = c % 2
            hcs, ubuf, accb = cbuf[par]
            kh, ku, ka = ("t", "hcs", par), ("t", "ubuf", par), ("t", "acc", par)
            accs = accs_b[par]
            kas = ("t", "accs", par)

            def cgroups(blk):
                Wl = [W[s][:, kc, blk * 128:(blk + 1) * 128] for kc in range(16)]
                bx, by, bz = bank(), bank(), bank()

                def fnx(e, Wl=Wl, bx=bx):
                    ins = None
                    for kc in range(16):
                        ins = e.matmul(PS[bx][:, 0:16], lhsT=Wl[kc], rhs=hxM[:, kc, 1024:1040],
                                       start=(kc == 0), stop=(kc == 15))
                    for kc in range(16):
                        ins = e.matmul(PS[bx][:, 16:32], lhsT=Wl[kc], rhs=hxPt[:, kc, :],
                                       start=(kc == 0), stop=(kc == 15))
                    return ins
                S.op("pe", fnx, reads=[("w", s, blk), ("hxPt",)] + hx_keys(16, 17), writes=[("ps", bx)], cost=1.0)
                mm_group(PS[by][:, :], [(Wl[kc], hxM[:, kc, 0:512]) for kc in range(16)],
                         [("w", s, blk)] + hx_keys(8, 12), [("ps", by)])
                mm_group(PS[bz][:, :], [(Wl[kc], hxM[:, kc, 512:1024]) for kc in range(16)],
                         [("w", s, blk)] + hx_keys(12, 16), [("ps", bz)])
                return bx, by, bz

            def b0():
                bx, by, bz = cgroups(0)
                act(hcs[:, 0:16], PS[bx][:, 0:16], AF.Copy, [("ps", bx)], [kh])
                act(hcs[:, 16:18], PS[bx][:, 30:32], AF.Copy, [("ps", bx)], [kh])
                act(hcs[:, 18:530], PS[by][:, :], AF.Copy, [("ps", by)], [kh])
                act(hcs[:, 530:1042], PS[bz][:, :], AF.Copy, [("ps", bz)], [kh])

            def b1():
                bx, by, bz = cgroups(2)
                tt_op(ubuf[:, 0:16], PS[bx][:, 0:16], hcs[:, 0:16], ALU.mult, [("ps", bx), kh], [ku])
                tt_op(ubuf[:, 16:18], PS[bx][:, 30:32], hcs[:, 16:18], ALU.mult, [("ps", bx), kh], [ku])
                tt_op(ubuf[:, 18:530], PS[by][:, :], hcs[:, 18:530], ALU.mult, [("ps", by), kh], [ku])
                tt_op(ubuf[:, 530:1042], PS[bz][:, :], hcs[:, 530:1042], ALU.mult, [("ps", bz), kh], [ku])
                ts_op(accb[:, 0:1024], ubuf[:, 16:1040], cwv[:, c, 0:1], None, ALU.mult, None, [ku, ("c", "cw")], [ka])
                stt_op(accb[:, 0:1024], ubuf[:, 17:1041], cwv[:, c, 1:2], accb[:, 0:1024], ALU.mult, ALU.add,
                       [ku, ka, ("c", "cw")], [ka])
                stt_op(accb[:, 0:1024], ubuf[:, 18:1042], cwv[:, c, 2:3], accb[:, 0:1024], ALU.mult, ALU.add,
                       [ku, ka, ("c", "cw")], [ka])
                ts_op(accs[:, :], scTv[:, 0, c, :], cwv[:, c, 0:1], None, ALU.mult, None,
                      [("c", "scT"), ("c", "cw")], [kas])
                stt_op(accs[:, :], scTv[:, 1, c, :], cwv[:, c, 1:2], accs[:, :], ALU.mult, ALU.add,
                       [("c", "scT"), kas, ("c", "cw")], [kas])
                stt_op(accs[:, :], ubuf[:, 0:16], cwv[:, c, 2:3], accs[:, :], ALU.mult, ALU.add,
                       [ku, kas, ("c", "cw")], [kas])
                act(uov[:, c, 0:2], ubuf[:, 1040:1042], AF.Copy, [ku], [("c", "uo")])
                act(uov[:, c, 2:18], ubuf[:, 0:16], AF.Copy, [ku], [("c", "uo")])

            def b2():
                bx, by, bz = cgroups(1)
                tt_op(aT[:, c, 0:512], PS[by][:, :], accb[:, 0:512], ALU.mult, [("ps", by), ka], [("aT",)])
                tt_op(aT[:, c, 512:1024], PS[bz][:, :], accb[:, 512:1024], ALU.mult, [("ps", bz), ka], [("aT",)])
                tt_op(aT[:, c, 1024:1040], PS[bx][:, 0:16], accs[:, :], ALU.mult, [("ps", bx), kas], [("aT",)])

            return b0, b1, b2

        assert fill_i[0] == 8
        fills[9](1)
        prev = None
        for h in range(8):
            cb0, cb1, cb2 = C_blocks(h, 0)
            s = 1
            H_ = head_ops(h, s)
            H_["prefetch"]()
            if prev is not None:
                prev["scan"](1)
                prev["o_main"](0)
                prev["norm"](0)
                prev["o_main"](1)
                prev["norm"](1)
                prev["norm"](2)
            H_["front"](2)
            H_["s1"]()
            XR.append(("bar", "P")); cb0(); XR.pop()
            H_["chain"](2)
            H_["front"](3)
            if h < 7:
                fills[9 + 2 * (h + 1)](1)
            H_["trans"](2)
            H_["s2"]()
            XR.append(("bar", "P")); cb1(); XR.pop()
            H_["chain"](3)
            H_["s3"]()
            H_["trans"](3)
            XR.append(("bar", "P")); cb2(); XR.pop()
            fills[8 + 2 * (h + 1)](0)
            H_["s4"]()
            prev = H_
        fill_i[0] = 24
        prev["scan"](1)
        prev["o_main"](0)
        prev["norm"](0)
        prev["o_main"](1)
        prev["norm"](1)
        prev["norm"](2)
        b1, b2 = bank(), bank()
        for half, bb in ((0, b1), (1, b2)):
            def fn_u(e, half=half, bb=bb):
                ins = None
                for k in range(4):
                    c = half * 4 + k
                    ins = e.matmul(PS[bb][0:18, k * 128:(k + 1) * 128], lhsT=uov[:, c, :], rhs=idf[:, :],
                                   start=True, stop=True)
                return ins
            S.op("pe", fn_u, reads=[("c", "uo"), ("c", "idf")], writes=[("ps", bb)])
        S.mark(sched=True)
        S.barrier_op(P_LO, len(S.ops), ("bar", "H"))
        XR.append(("bar", "H"))
        uot = view(EA, 26624, [128, 1024], F32)
        act(uot[0:18, 0:512], PS[b1][0:18, :], AF.Copy, [("ps", b1)], [("t", "uot")])
        act(uot[0:18, 512:1024], PS[b2][0:18, :], AF.Copy, [("ps", b2)], [("t", "uot")])
        dma("sp", o_cp, uot[0:2, :], "out", reads=[("t", "uot")])
        dma("sp", o_cs[:, 1, :], uot[2:18, :], "out", reads=[("t", "uot")])
        bank_set[0] = list(range(8))
        ar.reset()

        t1 = [ar.alloc([128, 512], F32) for _ in range(2)]
        t2 = [ar.alloc([128, 512], F32) for _ in range(2)]
        ar.cur[0] = 16 * TT * 2
        ar.cur[1] = 0
        t1 = [ar.alloc([128, 512], F32) for _ in range(2)]
        t2 = [ar.alloc([128, 512], F32) for _ in range(2)]
        it = 0
        for j in range(16):
            s = next_slot()
            for (c0, ncol, hk) in ((0, 352, hx_keys(8, 11)), (352, 352, hx_keys(10, 14)), (704, 336, hx_keys(13, 17))):
                p_ = it % 2
                it += 1
                bga, bgb, bA, bB = bank(), bank(), bank(), bank()
                mm_group(PS[bga][:, 0:ncol], [(W[s][:, kc, 0:128], hxM[:, kc, c0:c0 + ncol]) for kc in range(16)],
                         [("w", s, 0)] + hk, [("ps", bga)])
                mm_group(PS[bgb][:, 0:ncol], [(W[s][:, kc, 128:256], hxM[:, kc, c0:c0 + ncol]) for kc in range(16)],
                         [("w", s, 1)] + hk, [("ps", bgb)])
                mm_group(PS[bA][:, 0:ncol], [(W[s][:, kc, 256:384], aT[:, kc, c0:c0 + ncol]) for kc in range(8)],
                         [("w", s, 2), ("aT",)], [("ps", bA)])
                mm_group(PS[bB][:, 0:ncol], [(W[s][:, kc, 384:512], oT[:, kc, c0:c0 + ncol]) for kc in range(8)],
                         [("w", s, 3), ("oT",)], [("ps", bB)])
                k1, k2 = ("t", "t1", p_), ("t", "t2", p_)
                act(t1[p_][:, 0:ncol], PS[bga][:, 0:ncol], AF.Sigmoid, [("ps", bga)], [k1])
                act(t2[p_][:, 0:ncol], PS[bgb][:, 0:ncol], AF.Sigmoid, [("ps", bgb)], [k2])
                tt_op(t1[p_][:, 0:ncol], t1[p_][:, 0:ncol], PS[bA][:, 0:ncol], ALU.mult, [k1, ("ps", bA)], [k1])
                tt_op(t2[p_][:, 0:ncol], t2[p_][:, 0:ncol], PS[bB][:, 0:ncol], ALU.mult, [k2, ("ps", bB)], [k2])
                tt_op(mixT[:, j, c0:c0 + ncol], t1[p_][:, 0:ncol], t2[p_][:, 0:ncol], ALU.add, [k1, k2], [("mix",)])
        if stop == 3:
            S.fence()
            S.emit(nc, es)
            return nc

        xsrc = [(x_main[t * 128:(t + 1) * 128, :], 128) for t in range(8)] + [(x_smp, 16)]
        for t, (src, r) in enumerate(xsrc):
            dma("sp", x1v[0:r, t, :], src, "x1ld", writes=[("x1", t, n) for n in range(4)] + [("aT",), ("oT",)])
        for n in range(4):
            s = next_slot()
            for t, (src, r) in enumerate(xsrc):
                c0 = t * 128
                b = bank()
                mm_group(PS[b][0:r, :], [(mixT[:, kc, c0:c0 + r], W[s][:, kc, :]) for kc in range(16)],
                         WK(s) + [("mix",)], [("ps", b)])
                tt_op(x1v[0:r, t, n * 512:(n + 1) * 512], x1v[0:r, t, n * 512:(n + 1) * 512], PS[b][0:r, :], ALU.add,
                      [("ps", b), ("x1", t, n)], [("x1", t, n)])
        if stop == 4:
            S.fence()
            S.emit(nc, es)
            return nc

        xs4 = [ar.alloc([128, D], BF16), ar.alloc([128, D], BF16)]
        junk4 = ar.alloc([128, D], BF16)

        def dst4(i, hh):
            r = 128 if i < 8 else 16
            return (h2T[:, hh * 8:(hh + 1) * 8, i * 128:i * 128 + r],
                    [("h2", i, hh), ("hx", (8 + i) if i < 8 else 16, hh)])

        tiles4 = [(x1v[0:(128 if t < 8 else 16), t, :], 128 if t < 8 else 16, [("x1", t, n) for n in range(4)], t)
                  for t in range(9)]
        norm_transpose(tiles4, gf, "gf", xs4, junk4, dst4, "s4")
        if stop == 5:
            S.fence()
            S.emit(nc, es)
            return nc

        tr_ = [ar.alloc([128, 512], F32) for _ in range(3)]
        h2_keys = lambda t0, t1_: [("h2", i, hh) for i in range(t0, t1_) for hh in range(2)]
        it = 0
        for g in range(4):
            for q4 in range(4):
                s = next_slot()
                for fc in range(4):
                    fidx = q4 * 4 + fc
                    for (c0, ncol, hk) in ((0, 352, h2_keys(0, 3)), (352, 352, h2_keys(2, 6)), (704, 336, h2_keys(5, 9))):
                        b = bank()
                        mm_group(PS[b][:, 0:ncol],
                                 [(W[s][:, kc, fc * 128:(fc + 1) * 128], h2T[:, kc, c0:c0 + ncol]) for kc in range(16)],
                                 WK(s) + hk, [("ps", b)])
                        p_ = it % 3
                        it += 1
                        kt = ("t", "relu", p_)
                        act(tr_[p_][:, 0:ncol], PS[b][:, 0:ncol], AF.Relu, [("ps", b)], [kt])
                        tt_op(actT[:, fidx, c0:c0 + ncol], tr_[p_][:, 0:ncol], tr_[p_][:, 0:ncol], ALU.mult,
                              [kt], [("actT",)] + ([("mix",)] if g == 0 else []))
            for n in range(4):
                s = next_slot()
                for t, (src, r) in enumerate(xsrc):
                    c0 = t * 128
                    b = bank()
                    mm_group(PS[b][0:r, :], [(actT[:, fc, c0:c0 + r], W[s][:, fc, :]) for fc in range(16)],
                             WK(s) + [("actT",)], [("ps", b)])
                    tt_op(x1v[0:r, t, n * 512:(n + 1) * 512], x1v[0:r, t, n * 512:(n + 1) * 512], PS[b][0:r, :],
                          ALU.add, [("ps", b), ("x1", t, n)], [("x1", t, n)])

        h2_all = [("h2", i, hh) for i in range(9) for hh in range(2)]
        yt = [view(R1, 0, [128, D], F32), view(R1, D * 4, [128, D], F32)]
        junk6 = view(R1, D * 8, [128, D], BF16)
        gfin = view(R1, D * 8 + D * 2, [128, D], F32)
        dma("sp", gfin, nfin_d, "misc", writes=[("c", "gfin")] + h2_all)
        for t, (src, r) in enumerate(xsrc):
            sl = t % 2
            ssv = ssb[:, sl * 2:sl * 2 + 2]
            kss = ("ss", sl)
            xk = [("x1", t, n) for n in range(4)]
            S.op("pool", lambda e, ssv=ssv: e.memset(ssv, 0.0), writes=[kss])
            act(junk6[0:r, :], x1v[0:r, t, :], AF.Square, xk + [kss, ("c", "gfin")], [("t", "junk6"), kss],
                accum_out=ssv[0:r, 0:1])
            act(ssv[0:r, 1:2], ssv[0:r, 0:1], AF.Ln, [kss], [kss], scale=1.0 / D, bias=EPS)
            act(ssv[0:r, 1:2], ssv[0:r, 1:2], AF.Exp, [kss], [kss], scale=-0.5)
            ky = ("t", "yt", sl)
            stt_op(yt[sl][0:r, :], x1v[0:r, t, :], ssv[0:r, 1:2], gfin[0:r, :], ALU.mult, ALU.mult,
                   xk + [kss, ("c", "gfin")], [ky])
            dst = y_main[t * 128:(t + 1) * 128, :] if t < 8 else y_smp
            dma("sp", dst, yt[sl][0:r, :], "yout%d" % sl, reads=[ky])

        if os.environ.get('DBG_MEM'):
            print('SBUF remaining', nc.sbuf_bytes_remaining)
        S.emit(nc, es)
    return nc


_CACHE = {}


def _consts():
    ident = np.eye(128, dtype=np.float32)
    s_idx = np.arange(128)[:, None]
    l_idx = np.arange(128)[None, :]
    mask2 = ((s_idx // 64 == l_idx // 64) & (l_idx >= s_idx)).astype(np.float32)
    rmask = np.ones((128, 512), np.float32)
    rmask[:, ::64] = 0.0
    return ident, mask2, rmask


def kernel(x_prompt, x_sample, state_conv, state_hgrn, norm_mix, w_in, conv_w, lb_logits, onorm_g,
           w_branch_a, w_branch_b, w_out, norm_ffn, w_up, w_down, norm_final):
    f32 = lambda a: np.ascontiguousarray(np.asarray(a, dtype=np.float32))
    x_prompt, x_sample, state_conv, state_hgrn = f32(x_prompt), f32(x_sample), f32(state_conv), f32(state_hgrn)
    if "nc" not in _CACHE:
        _CACHE["nc"] = build_program()
    nc = _CACHE["nc"]
    ident, mask2, rmask = _consts()
    shared = {
        "gm": f32(np.asarray(norm_mix)[0].reshape(16, 128).T),
        "gf": f32(np.asarray(norm_ffn)[0].reshape(16, 128).T),
        "cw": f32(np.asarray(conv_w)[0].reshape(3, 8, 128).transpose(2, 1, 0).reshape(128, 24)),
        "lbl": f32(np.asarray(lb_logits).reshape(2, 8, 128).transpose(2, 0, 1).reshape(128, 16)),
        "ogg": f32(np.asarray(onorm_g)[0].reshape(128, 1)),
        "nfin": f32(np.broadcast_to(np.asarray(norm_final).reshape(1, D), (128, D))),
        "ident": ident, "mask2": mask2, "rmask": rmask,
        "w_in": f32(np.asarray(w_in)[0]), "w_branch_a": f32(np.asarray(w_branch_a)[0]),
        "w_branch_b": f32(np.asarray(w_branch_b)[0]), "w_out": f32(np.asarray(w_out)[0]),
        "w_up": f32(np.asarray(w_up)[0]), "w_down": f32(np.asarray(w_down)[0]),
    }
    in_maps = []
    for c in range(N_CORES):
        sq, hf = c // 2, c % 2
        m = dict(shared)
        m["x_main"] = f32(x_prompt[sq, hf * 1024:(hf + 1) * 1024])
        m["x_pre"] = f32(x_prompt[sq, 0:1024]) if hf == 1 else np.zeros((1024, D), np.float32)
        m["x_smp"] = f32(x_sample[c * 16:(c + 1) * 16, 0])
        m["sconv"] = f32(state_conv[0, c * 16:(c + 1) * 16].reshape(16, 2048))
        m["shgrn"] = f32(state_hgrn[0, c * 16:(c + 1) * 16])
        in_maps.append(m)
    if _CACHE.get('debug_return_maps'):
        return in_maps
    res = run_bass_kernel_spmd(nc, in_maps, core_ids=list(range(N_CORES)))
    R = res.results
    yp = np.zeros((4, 2048, D), np.float32)
    ys = np.zeros((128, 1, D), np.float32)
    ncp = np.zeros((1, 4, 2, 1024), np.float32)
    nhp = np.zeros((1, 4, 8, 128, 128), np.float32)
    ncs = np.zeros((1, 128, 2, 1024), np.float32)
    nhs = np.zeros((1, 128, 8, 128, 128), np.float32)
    for c in range(N_CORES):
        sq, hf = c // 2, c % 2
        r = R[c]
        yp[sq, hf * 1024:(hf + 1) * 1024] = np.asarray(r["y_main"])
        ys[c * 16:(c + 1) * 16, 0] = np.asarray(r["y_smp"])
        ncs[0, c * 16:(c + 1) * 16] = np.asarray(r["o_cs"])
        nhs[0, c * 16:(c + 1) * 16] = np.asarray(r["o_hs"])
        if hf == 1:
            ncp[0, sq] = np.asarray(r["o_cp"])
            nhp[0, sq] = np.asarray(r["o_hp"])
    return yp, ys, ncp, nhp, ncs, nhs
```

```python
import bisect
import os
from contextlib import ExitStack
import numpy as np
import concourse.bass as bass
import concourse.mybir as mybir
from concourse.bass_utils import run_bass_kernel_spmd

F32 = mybir.dt.float32
BF16 = mybir.dt.bfloat16
ALU = mybir.AluOpType
AF = mybir.ActivationFunctionType

D = 2048
NK = 16
TM = 1024
TSM = 16
TT = TM + TSM
EPS = 1e-6
SAME_SYNC = True
SCHEDULE = True
LOOKAHEAD = 12
SCHED_SEGS = (1, 2)
TBL_COST = 1.3
PRIO = 1
SLACK = 0.5
N_CORES = 8
ALLOC_LOG = []


class Sched:
    ENGS = ("pe", "act", "dve", "pool", "sp")

    def __init__(self):
        self.ops = []
        self.keys = {}
        self.bounds = []
        self.fences = set()
        self.region_sched = {}
        self.scratch = None
        self.dma_sem_ops = {}

    def op(self, eng, fn, reads=(), writes=(), dma=None, cost=0.3, lat=0.0, tbl=None):
        gi = len(self.ops)
        deps = set()
        ps_r = [k for k in reads if k and k[0] == "ps"]
        if ps_r:
            reads = [k for k in reads if not (k and k[0] == "ps")]
            writes = list(writes) + ps_r
        for k in reads:
            st = self.keys.get(k)
            if st is not None and st[0] is not None:
                deps.add(st[0])
        for k in writes:
            st = self.keys.get(k)
            if st is not None:
                if st[0] is not None:
                    deps.add(st[0])
                deps.update(st[1])
        for k in reads:
            self.keys.setdefault(k, [None, []])[1].append(gi)
        for k in writes:
            self.keys[k] = [gi, []]
        deps.discard(gi)
        self.ops.append(dict(eng=eng, fn=fn, deps=deps, dma=dma, gi=gi, cost=cost, lat=lat, tbl=tbl))
        return gi

    def fence(self, sched=False):
        self.bounds.append(len(self.ops))
        self.fences.add(len(self.ops))
        self.region_sched[len(self.ops)] = sched

    def mark(self, sched=False):
        self.bounds.append(len(self.ops))
        self.region_sched[len(self.ops)] = sched

    def barrier_op(self, lo, hi, key):
        gi = self.op("dve", lambda e: e.memset(self.scratch, 0.0), writes=[key], cost=0.1)
        self.ops[gi]["deps"] |= set(range(lo, hi))
        return gi

    def finalize(self):
        ops = self.ops
        bounds = [0] + [b for b in self.bounds if 0 < b < len(ops)] + [len(ops)]
        bounds = sorted(set(bounds))
        order = []
        t_eng = {e: 0.0 for e in self.ENGS}
        finish = {}
        for si in range(len(bounds) - 1):
            lo, hi = bounds[si], bounds[si + 1]
            seg = range(lo, hi)
            t0 = max(t_eng.values())
            for e in self.ENGS:
                t_eng[e] = t0
            if (not SCHEDULE) or (not self.region_sched.get(lo, False)):
                order.extend(seg)
                continue
            ndep = {}
            users = {}
            for i in seg:
                c = 0
                for d in ops[i]["deps"]:
                    if d >= lo:
                        c += 1
                        users.setdefault(d, []).append(i)
                ndep[i] = c
            blevel = {}
            for i in reversed(seg):
                m = 0.0
                for u in users.get(i, ()):
                    if blevel[u] > m:
                        m = blevel[u]
                blevel[i] = ops[i]["cost"] + ops[i]["lat"] + m
            ready = {e: [] for e in self.ENGS}
            import heapq
            for i in seg:
                if ndep[i] == 0:
                    heapq.heappush(ready[ops[i]["eng"]], i)
            nleft = hi - lo
            cur_tbl = [None]
            while nleft:
                best = None
                for e in self.ENGS:
                    cand = heapq.nsmallest(LOOKAHEAD, ready[e])
                    for i in cand:
                        o = ops[i]
                        est = t_eng[e]
                        for d in o["deps"]:
                            if d >= lo:
                                fd = finish[d] + (0.15 if ops[d]["eng"] != e else 0.1)
                                if fd > est:
                                    est = fd
                        pen = TBL_COST if (o["tbl"] is not None and o["tbl"] != cur_tbl[0]) else 0.0
                        if PRIO == 0:
                            key = (est + pen, i)
                        elif PRIO == 1:
                            key = (est + pen, -blevel[i], i)
                        else:
                            key = (round((est + pen) / SLACK), -blevel[i], i)
                        if best is None or key < best[0]:
                            best = (key, i, e, est, pen)
                _, i, e, est, pen = best
                ready[e].remove(i)
                heapq.heapify(ready[e])
                o = ops[i]
                if o["tbl"] is not None:
                    cur_tbl[0] = o["tbl"]
                t_eng[e] = est + pen + o["cost"]
                finish[i] = est + pen + o["cost"] + o["lat"]
                order.append(i)
                nleft -= 1
                for u in users.get(i, ()):
                    ndep[u] -= 1
                    if ndep[u] == 0:
                        heapq.heappush(ready[ops[u]["eng"]], u)
        self.est_total = max(t_eng.values())
        if os.environ.get("DBG_SCHED"):
            te = {e: 0.0 for e in self.ENGS}
            fin = {}
            seg_of = lambda i: bisect.bisect_right(bounds, i) - 1
            cur = 0
            seg_start = 0.0
            busy = {e: 0.0 for e in self.ENGS}
            for i in order + [None]:
                sg = seg_of(i) if i is not None else -1
                if sg != cur:
                    t0 = max(te.values())
                    print("SCHED seg %d: %.1f us  busy %s" % (cur, t0 - seg_start,
                          " ".join("%s=%.0f" % (e, busy[e]) for e in self.ENGS)))
                    seg_start = t0
                    busy = {e: 0.0 for e in self.ENGS}
                    for e in self.ENGS:
                        te[e] = t0
                    cur = sg
                if i is None:
                    break
                o = ops[i]
                e = o["eng"]
                est = te[e]
                for d in o["deps"]:
                    if d in fin:
                        fd = fin[d] + (0.15 if ops[d]["eng"] != e else 0.1)
                        est = max(est, fd)
                pen = 0.0
                if o["tbl"] is not None:
                    if o["tbl"] != getattr(self, "_dbg_tbl", None):
                        pen = TBL_COST
                    self._dbg_tbl = o["tbl"]
                te[e] = est + pen + o["cost"]
                busy[e] += o["cost"] + pen
                fin[i] = est + pen + o["cost"] + o["lat"]
            print("SCHED model total %.1f us, nops %d" % (max(te.values()), len(ops)))
        fence_list = sorted(self.fences)
        newpos = {old: new for new, old in enumerate(order)}
        new_ops = []
        for new, old in enumerate(order):
            o = ops[old]
            o["deps"] = set(newpos[d] for d in o["deps"])
            o["gi"] = new
            o["seg"] = bisect.bisect_right(fence_list, old)
            new_ops.append(o)
        self.ops = ops = new_ops
        last_on_eng = {}
        last_dma = {}
        pending = {}
        cur_seg = 0
        seg_last_eng, seg_last_dma = {}, {}
        for o in ops:
            if o["seg"] != cur_seg:
                deps = set(last_on_eng.values()) | set(last_dma.values())
                for e in self.ENGS:
                    pending[e] = pending.get(e, set()) | deps
                cur_seg = o["seg"]
            fd = pending.pop(o["eng"], None)
            if fd:
                o["deps"] |= fd
                o["deps"].discard(o["gi"])
            last_on_eng[o["eng"]] = o["gi"]
            if o["dma"] is not None:
                last_dma[o["dma"]] = o["gi"]
        self.dma_sem_ops = {}
        for o in ops:
            if o["dma"] is not None:
                self.dma_sem_ops.setdefault(o["dma"], []).append(o["gi"])

    def emit(self, nc, es):
        self.finalize()
        ops = self.ops
        compute_engs = ("pe", "act", "dve", "pool")
        needed = set()
        for o in ops:
            for d in o["deps"]:
                do = ops[d]
                if do["dma"] is not None:
                    continue
                if do["eng"] == o["eng"] and (o["eng"] == "pe" or not SAME_SYNC):
                    continue
                needed.add(d)
        ordinal = {}
        cnt = {e: 0 for e in compute_engs + ("sp",)}
        for o in ops:
            if o["dma"] is None and o["gi"] in needed:
                cnt[o["eng"]] += 1
                ordinal[o["gi"]] = cnt[o["eng"]]
        eng_sem = {e: es.enter_context(nc.semaphore("sem_" + e)) for e in compute_engs + ("sp",)}
        dma_sem = {k: es.enter_context(nc.semaphore("dsem_" + k)) for k in self.dma_sem_ops}
        per_eng = {e: [] for e in compute_engs + ("sp",)}
        for o in ops:
            per_eng[o["eng"]].append(o)

        def replay(engname, e):
            waited = {}
            for o in per_eng[engname]:
                wl = {}
                for d in o["deps"]:
                    do = ops[d]
                    if do["dma"] is not None:
                        lst = self.dma_sem_ops[do["dma"]]
                        n = bisect.bisect_left(lst, o["gi"])
                        key = ("d", do["dma"])
                        val = 16 * n
                    else:
                        if do["eng"] == engname and (engname == "pe" or not SAME_SYNC):
                            continue
                        key = ("e", do["eng"])
                        val = ordinal[d]
                    if val > wl.get(key, 0):
                        wl[key] = val
                for key, val in wl.items():
                    if waited.get(key, 0) >= val:
                        continue
                    waited[key] = val
                    sem = dma_sem[key[1]] if key[0] == "d" else eng_sem[key[1]]
                    e.wait_ge(sem, val)
                ins = o["fn"](e)
                if o["dma"] is not None:
                    ins.then_inc(dma_sem[o["dma"]], 16)
                elif o["gi"] in needed:
                    ins.then_inc(eng_sem[engname], 1)
            if engname == "sp":
                for k, lst in self.dma_sem_ops.items():
                    e.wait_ge(dma_sem[k], 16 * len(lst))

        block = es.enter_context(nc.Block())

        @block.tensor
        def _(e):
            replay("pe", e)

        @block.scalar
        def _(e):
            replay("act", e)

        @block.vector
        def _(e):
            replay("dve", e)

        @block.gpsimd
        def _(e):
            replay("pool", e)

        @block.sync
        def _(e):
            replay("sp", e)


def build_program(stop=99):
    nc = bass.Bass("TRN2", target_bir_lowering=False)

    def din(name, shape):
        return nc.dram_tensor(name, list(shape), F32, kind="ExternalInput").ap()

    def dout(name, shape):
        return nc.dram_tensor(name, list(shape), F32, kind="ExternalOutput").ap()

    x_pre = din("x_pre", [1024, D])
    x_main = din("x_main", [1024, D])
    x_smp = din("x_smp", [16, D])
    sconv = din("sconv", [16, 2048])
    shgrn = din("shgrn", [16, 8, 128, 128])
    gm_d = din("gm", [128, 16])
    gf_d = din("gf", [128, 16])
    cw_d = din("cw", [128, 24])
    lbl_d = din("lbl", [128, 16])
    og_d = din("ogg", [128, 1])
    nfin_d = din("nfin", [128, D])
    ident_d = din("ident", [128, 128])
    sel_d = din("sel", [128, 16])
    mask2_d = din("mask2", [128, 128])
    rmask_d = din("rmask", [128, 512])
    w_in = din("w_in", [D, 11264])
    w_a = din("w_branch_a", [1024, D])
    w_b = din("w_branch_b", [1024, D])
    w_out = din("w_out", [D, D])
    w_up = din("w_up", [D, 8192])
    w_down = din("w_down", [8192, D])

    y_main = dout("y_main", [1024, D])
    y_smp = dout("y_smp", [16, D])
    o_cp = dout("o_cp", [2, 1024])
    o_hp = dout("o_hp", [8, 128, 128])
    o_cs = dout("o_cs", [16, 2, 1024])
    o_hs = dout("o_hs", [16, 8, 128, 128])

    S = Sched()
    with ExitStack() as es:
        def sb(name, shape, dtype):
            return es.enter_context(nc.sbuf_tensor("s_" + name, list(shape), dtype))

        R1 = sb("R1", [128, 16 * TT], BF16)
        R2 = sb("R2", [128, 18 * D], BF16)
        R3 = sb("R3", [128, 16 * TT + 32], BF16)
        W = [sb("W0", [128, 16, 512], BF16), sb("W1", [128, 16, 512], BF16)]
        E_BYTES = 30 * 1024
        EA = sb("EA", [128, E_BYTES // 2], BF16)
        idb = sb("idb", [128, 128], BF16)
        idf = sb("idf", [128, 128], F32)
        selT = sb("selT", [128, 16], F32)
        ones_b = sb("ones_b", [128, 128], BF16)
        mask2 = sb("mask2", [128, 128], F32)
        rmask = sb("rmask", [128, 512], F32)
        gm = sb("gm", [128, 16], F32)
        gf = sb("gf", [128, 16], F32)
        cw = sb("cw", [128, 24], F32)
        lbl = sb("lbl", [128, 16], F32)
        lbm = sb("lbm", [128, 8], F32)
        oml = sb("oml", [128, 8], F32)
        ogg = sb("ogg", [128, 1], F32)
        ssb = sb("ssb", [128, 8], F32)
        scT = sb("scT", [128, 256], F32)
        uo = sb("uo", [128, 8 * 18], F32)
        dec = sb("dec", [128, 32], F32)
        hxPt = sb("hxPt", [128, 16, 16], BF16)
        S.scratch = ssb[:, 6:7]
        PS = [es.enter_context(nc.psum_tensor("ps%d" % i, [128, 512], F32)) for i in range(8)]

        bank_ctr = [0]
        bank_set = [list(range(8))]

        def bank():
            bs_ = bank_set[0]
            b = bs_[bank_ctr[0] % len(bs_)]
            bank_ctr[0] += 1
            return b

        def view(raw, off, shape, dtype):
            n = 1
            for s_ in shape[1:]:
                n *= s_
            esz = 2 if dtype == BF16 else 4
            a = off // 2
            ln = n * esz // 2
            assert off % 4 == 0 and a + ln <= raw.shape[1], (off, shape, raw.shape)
            ap = raw[:, a:a + ln]
            if dtype != BF16:
                ap = ap.bitcast(dtype)
            if len(shape) == 3:
                ap = ap.rearrange("p (a b) -> p a b", a=shape[1])
            return ap

        class Arena:
            def __init__(self, raws):
                self.raws = [(r, 0, n) for (r, n) in raws]
                self.nbase = len(self.raws)
                self.reset()

            def reset(self):
                self.raws = self.raws[:self.nbase]
                self.cur = [0 for _ in self.raws]

            def add_region(self, raw, base, nbytes):
                self.raws.append((raw, base, nbytes))
                self.cur.append(0)

            def alloc(self, shape, dtype):
                n = 1
                for s_ in shape[1:]:
                    n *= s_
                nb = n * (2 if dtype == BF16 else 4)
                nb = (nb + 31) // 32 * 32
                for i, (raw, base, tot) in enumerate(self.raws):
                    if self.cur[i] + nb <= tot:
                        v = view(raw, base + self.cur[i], shape, dtype)
                        ALLOC_LOG.append((tuple(shape), str(dtype), i, self.cur[i]))
                        self.cur[i] += nb
                        return v
                raise RuntimeError("arena overflow %s" % (shape,))

        ar = Arena([(R3, 16 * TT * 2), (EA, E_BYTES)])

        hxM = R1[:, :].rearrange("p (k t) -> p k t", k=16)
        R2b = R2[:, :]
        aT = R2b[:, 0:8 * TT].rearrange("p (c t) -> p c t", c=8)
        oT = R2b[:, 8 * TT:16 * TT].rearrange("p (c t) -> p c t", c=8)
        hxP = R2b[:, 16 * TT:16 * TT + 16 * 1024].rearrange("p (k t) -> p k t", k=16)
        x1v = R2[:, :].bitcast(F32).rearrange("p (t c) -> p t c", t=9)
        mixT = R3[:, 0:16 * TT].rearrange("p (k t) -> p k t", k=16)
        actT = mixT
        h2T = hxM

        def psb(b):
            return PS[b][:, :].bitcast(BF16)

        XR = []

        def ncols_of(ap):
            n = 1
            for d_ in tuple(ap.shape)[1:]:
                n *= int(d_)
            return n

        def dma(eng, out, in_, sem, reads=(), writes=()):
            nbytes = ncols_of(out) * int(tuple(out.shape)[0]) * 4
            S.op(eng, lambda e: e.dma_start(out=out, in_=in_), reads=list(reads) + XR, writes=writes, dma=sem,
                 cost=(1.3 if eng == "pool" else 0.3), lat=2.0 + nbytes / 300e3)

        def mm_group(out_ap, pairs, reads, writes):
            def fn(e):
                n = len(pairs)
                ins = None
                for i, (l, r) in enumerate(pairs):
                    ins = e.matmul(out_ap, lhsT=l, rhs=r, start=(i == 0), stop=(i == n - 1))
                return ins
            cost = sum(max(ncols_of(r), 64) / 2400.0 + 0.01 for (_, r) in pairs)
            S.op("pe", fn, reads=list(reads) + XR, writes=writes, cost=cost)

        def act(out, in_, func, reads, writes, **kw):
            tbl = "sig" if func == AF.Sigmoid else ("lnexp" if func in (AF.Ln, AF.Exp) else None)
            S.op("act", lambda e: e.activation(out=out, in_=in_, func=func, **kw), reads=list(reads) + XR, writes=writes,
                 cost=0.22 + ncols_of(out) / 1200.0, tbl=tbl)

        def vcost(eng, out):
            return (0.25 + ncols_of(out) / 480.0) if eng == "pool" else (0.12 + ncols_of(out) / 960.0)

        def tt_op(out, in0, in1, op, reads, writes, eng="dve"):
            S.op(eng, lambda e: e.tensor_tensor(out=out, in0=in0, in1=in1, op=op), reads=list(reads) + XR, writes=writes,
                 cost=vcost(eng, out))

        def cp_op(out, in_, reads, writes, eng="dve"):
            S.op(eng, lambda e: e.tensor_copy(out=out, in_=in_), reads=list(reads) + XR, writes=writes, cost=vcost(eng, out))

        def ts_op(out, in0, s1, s2, op0, op1, reads, writes, eng="dve"):
            if op1 is None:
                S.op(eng, lambda e: e.tensor_scalar(out=out, in0=in0, scalar1=s1, scalar2=None, op0=op0),
                     reads=list(reads) + XR, writes=writes, cost=vcost(eng, out))
            else:
                S.op(eng, lambda e: e.tensor_scalar(out=out, in0=in0, scalar1=s1, scalar2=s2, op0=op0, op1=op1),
                     reads=list(reads) + XR, writes=writes, cost=vcost(eng, out))

        def stt_op(out, in0, scalar, in1, op0, op1, reads, writes, eng="dve"):
            S.op(eng, lambda e: e.scalar_tensor_tensor(out=out, in0=in0, scalar=scalar, in1=in1, op0=op0, op1=op1),
                 reads=list(reads) + XR, writes=writes, cost=vcost(eng, out))

        WK = lambda s: [("w", s, q) for q in range(4)]

        def wcols(src, c0, n, nkc=16):
            return src[:, c0:c0 + n].rearrange("(kc p) c -> p kc c", p=128)

        dma("pool", idb[:, :], ident_d, "constp", writes=[("c", "idb")])
        S.op("dve", lambda e: e.memset(R3[:, 16 * TT:16 * TT + 32], 0.0), writes=[("c", "r3pad")], cost=0.1)
        for t_, d_, nm in ((selT, sel_d, "sel"), (idf, ident_d, "idf"), (mask2, mask2_d, "mask2"), (rmask, rmask_d, "rmask"),
                           (gm, gm_d, "gm"), (gf, gf_d, "gf"), (cw, cw_d, "cw"), (lbl, lbl_d, "lbl"),
                           (ogg, og_d, "ogg")):
            dma("sp", t_[:, :], d_, "const", writes=[("c", nm)])
        S.op("dve", lambda e: e.memset(ones_b[:, :], 1.0), writes=[("c", "ones")])
        tt_op(lbm[:, :], lbl[:, 0:8], lbl[:, 8:16], ALU.subtract, [("c", "lbl")], [("c", "lbm")])
        act(lbm[:, :], lbm[:, :], AF.Sigmoid, [("c", "lbm")], [("c", "lbm")])
        ts_op(oml[:, :], lbm[:, :], -1.0, 1.0, ALU.mult, ALU.add, [("c", "lbm")], [("c", "oml")])

        sct = ar.alloc([128, 2048], F32)
        dma("sp", sct[0:16, :], sconv, "misc", writes=[("t", "sct")])
        dma("sp", o_cs[:, 0, :], sct[0:16, 1024:2048], "out", reads=[("t", "sct")])
        b0 = bank()
        def fn_sct(e):
            ins = None
            for j in range(16):
                ins = e.matmul(PS[b0][:, j * 16:(j + 1) * 16], lhsT=sct[0:16, j * 128:(j + 1) * 128],
                               rhs=idf[0:16, 0:16], start=True, stop=True)
            return ins
        S.op("pe", fn_sct, reads=[("t", "sct"), ("c", "idf")], writes=[("ps", b0)], cost=0.6)
        S.op("act", lambda e: e.copy(out=scT[:, :], in_=PS[b0][:, 0:256]), reads=[("ps", b0)], writes=[("c", "scT")])
        scTv = scT[:, :].rearrange("p (r c s) -> p r c s", r=2, c=8)

        fills = []

        def add_fill(fn):
            fills.append(fn)

        def fill_C(c):
            def f(s):
                for blk, base in enumerate((0, 1024, 2048)):
                    dma("pool", W[s][:, :, blk * 128:(blk + 1) * 128], wcols(w_in, base + c * 128, 128),
                        "w%db%d" % (s, blk), writes=[("w", s, blk)])
            return f

        def fill_H(h):
            def f(s):
                for blk, base in enumerate((3072, 4096, 5120, 6144)):
                    dma("pool", W[s][:, :, blk * 128:(blk + 1) * 128], wcols(w_in, base + h * 128, 128),
                        "w%db%d" % (s, blk), writes=[("w", s, blk)])
            return f

        def fill_J(j):
            def f(s):
                dma("pool", W[s][:, :, 0:128], wcols(w_in, 7168 + j * 128, 128), "w%db0" % s, writes=[("w", s, 0)])
                dma("pool", W[s][:, :, 128:256], wcols(w_in, 9216 + j * 128, 128), "w%db1" % s, writes=[("w", s, 1)])
                dma("pool", W[s][:, 0:8, 256:384], wcols(w_a, j * 128, 128), "w%db2" % s, writes=[("w", s, 2)])
                dma("pool", W[s][:, 0:8, 384:512], wcols(w_b, j * 128, 128), "w%db3" % s, writes=[("w", s, 3)])
            return f

        def fill_full(src, r0, c0):
            def f(s):
                dma("pool", W[s][:, :, :], src[r0:r0 + 2048, c0:c0 + 512].rearrange("(kc p) c -> p kc c", p=128),
                    "w%df" % s, writes=WK(s))
            return f

        def fill_P(h):
            def f(s):
                dma("pool", W[s][:, :, 128:256], wcols(w_in, 4096 + h * 128, 128), "w%db1" % s, writes=[("w", s, 1)])
                dma("pool", W[s][:, :, 256:384], wcols(w_in, 5120 + h * 128, 128), "w%db2" % s, writes=[("w", s, 2)])
            return f

        for h in range(8):
            add_fill(fill_P(h))
        for c in range(8):
            add_fill(fill_C(c))
            add_fill(fill_H(c))
        for j in range(16):
            add_fill(fill_J(j))
        for n in range(4):
            add_fill(fill_full(w_out, 0, n * 512))
        for g in range(4):
            for q4 in range(4):
                add_fill(fill_full(w_up, 0, g * 2048 + q4 * 512))
            for n in range(4):
                add_fill(fill_full(w_down, g * 2048, n * 512))
        fill_i = [0]

        fills[0](0)
        fills[1](1)

        def next_slot():
            k = fill_i[0]
            if k >= 1 and k + 1 < len(fills):
                fills[k + 1]((k + 1) % 2)
            fill_i[0] += 1
            return k % 2

        def norm_transpose(tiles, gvec, gname, xs_bufs, junk, dst_fn, tag):
            for (src, r, rkeys, i) in tiles:
                sl = i % 2
                ssv = ssb[:, sl * 2:sl * 2 + 2]
                kss = ("ss", sl)
                S.op("pool", lambda e, ssv=ssv: e.memset(ssv, 0.0), writes=[kss])
                act(junk[0:r, :], src, AF.Square, rkeys + [kss], [("t", "junk"), kss], accum_out=ssv[0:r, 0:1])
                act(ssv[0:r, 1:2], ssv[0:r, 0:1], AF.Ln, [kss], [kss], scale=1.0 / D, bias=EPS)
                act(ssv[0:r, 1:2], ssv[0:r, 1:2], AF.Exp, [kss], [kss], scale=-0.5)
                xs = xs_bufs[sl]
                kxs0 = ("t", tag + "xs", sl, 0)
                kxs1 = ("t", tag + "xs", sl, 1)
                act(xs[0:r, 0:1024], src[:, 0:1024], AF.Copy, rkeys + [kss], [kxs0], scale=ssv[0:r, 1:2])
                ts_op(xs[0:r, 1024:2048], src[:, 1024:2048], ssv[0:r, 1:2], None, ALU.mult, None, rkeys + [kss], [kxs1])
                for hh in range(2):
                    kxs = kxs0 if hh == 0 else kxs1
                    b = bank()
                    pb = psb(b)

                    def fn(e, pb=pb, xs=xs, r=r, hh=hh):
                        ins = None
                        for k in range(8):
                            kc = hh * 8 + k
                            ins = e.transpose(out=pb[:, k * 128:k * 128 + r], in_=xs[0:r, kc * 128:(kc + 1) * 128],
                                              identity=idb[0:r, 0:r])
                        return ins
                    S.op("pe", fn, reads=[kxs, ("c", "idb")], writes=[("ps", b)], cost=0.65)
                    pv = pb[:, 0:1024].rearrange("p (k t) -> p k t", k=8)[:, :, 0:r]
                    gb = gvec[:, hh * 8:(hh + 1) * 8].unsqueeze(2).to_broadcast([128, 8, r])
                    dst, dkeys = dst_fn(i, hh)
                    tt_op(dst, pv, gb, ALU.mult, [("ps", b), ("c", gname)], dkeys)

        def hx_keys(i0, i1):
            return [("hx", i, hh) for i in range(i0, i1) for hh in range(2)]
        hx_keys_early = hx_keys

        xt = [ar.alloc([128, D], F32) for _ in range(3)]
        xs0 = [ar.alloc([128, D], BF16), ar.alloc([128, D], BF16)]
        junk0 = ar.alloc([128, D], BF16)
        tiles0 = []
        for i in range(17):
            if i < 8:
                src, r = x_pre[i * 128:(i + 1) * 128, :], 128
            elif i < 16:
                src, r = x_main[(i - 8) * 128:(i - 7) * 128, :], 128
            else:
                src, r = x_smp, 16
            tiles0.append((i, src, r))

        def dst0(i, hh):
            if i < 8:
                return hxP[:, hh * 8:(hh + 1) * 8, i * 128:(i + 1) * 128], [("hx", i, hh)]
            if i < 16:
                return hxM[:, hh * 8:(hh + 1) * 8, (i - 8) * 128:(i - 7) * 128], [("hx", i, hh)]
            return hxM[:, hh * 8:(hh + 1) * 8, 1024:1040], [("hx", i, hh)]

        tl = []
        for (i, src, r) in tiles0:
            sl = i % 3
            dma("sp", xt[sl][0:r, :], src, "xt%d" % sl, writes=[("t", "xt", sl)])
            tl = [(xt[sl][0:r, :], r, [("t", "xt", sl)], i)]
            norm_transpose(tl, gm, "gm", xs0, junk0, dst0, "s0")
        cp_op(hxPt[:, :, :], hxP[:, :, 1008:1024], hx_keys_early(7, 8), [("hxPt",)])
        S.fence(sched=True)
        P_LO = len(S.ops)
        if stop == 0:
            S.emit(nc, es)
            return nc
        ar.reset()


        R2SP = (16 * TT + 16 * 1024) * 2
        S_init = view(R2, R2SP, [128, 8, 128], F32)
        ar.add_region(R2, R2SP + 4096, (18 * D * 2 - R2SP) - 4096)
        t_f = [ar.alloc([128, 512], F32) for _ in range(2)]
        t_q = [ar.alloc([128, 512], F32) for _ in range(2)]
        t_v = [ar.alloc([128, 512], BF16) for _ in range(2)]
        t_ke = [ar.alloc([128, 512], BF16) for _ in range(2)]
        t_lf = ar.alloc([128, 512], F32)
        t_b = ar.alloc([128, 512], F32)
        t_eb = ar.alloc([128, 512], F32)
        t_enb = ar.alloc([128, 512], F32)
        k_inT = ar.alloc([128, 1024], BF16)
        q_inT = ar.alloc([128, 1024], BF16)
        ke_tok = ar.alloc([128, 16, 128], BF16)
        v_tok = ar.alloc([128, 16, 128], BF16)
        sog = ar.alloc([128, TT], F32)
        S_pp = ar.alloc([128, 2, 128], F32)
        S_bf = ar.alloc([128, 16, 128], BF16)
        S0b = [ar.alloc([128, 8, 128], F32) for _ in range(2)]
        S_bfs = ar.alloc([128, 8, 128], BF16)
        vm = ar.alloc([128, 16, 128], BF16)
        fS = ar.alloc([128, 16], F32)
        fS2 = ar.alloc([128, 16], F32)
        sogS = [ar.alloc([128, 16], F32) for _ in range(2)]
        kS_b = ar.alloc([128, 16], BF16)
        vS_b = ar.alloc([128, 16], BF16)
        qS_b = ar.alloc([128, 16], BF16)
        ktok_s = ar.alloc([128, 128], BF16)
        vtok_s = ar.alloc([128, 128], BF16)
        scm = [ar.alloc([128, 128], BF16), ar.alloc([128, 128], BF16)]
        osq = ar.alloc([128, 512], BF16)
        rstd_t = ar.alloc([128, 512], F32)
        on_t = ar.alloc([128, 512], F32)
        K = lambda *n: ("t",) + n
        BOS, BO2 = 5, [6]
        nt_ctr = [0]

        NTS = [
            ("P", lambda kc: hxP[:, kc, 0:512], hx_keys(0, 4), 0),
            ("P", lambda kc: hxP[:, kc, 512:1024], hx_keys(4, 8), 4),
            ("M", lambda kc: hxM[:, kc, 0:512], hx_keys(8, 12), 8),
            ("M", lambda kc: hxM[:, kc, 512:1024], hx_keys(12, 16), 12),
        ]

        def head_ops(h, s):
            Wq = [W[s][:, kc, 0:128] for kc in range(16)]
            Wf = [W[s][:, kc, 128:256] for kc in range(16)]
            Wi = [W[s][:, kc, 256:384] for kc in range(16)]
            Wo = [W[s][:, kc, 384:512] for kc in range(16)]
            lb_h, oml_h = lbm[:, h:h + 1], oml[:, h:h + 1]
            st = {}

            HV = ((0, 256), (256, 512))

            def front(nt):
                kind, hsrc, hk, tb = NTS[nt]
                p = nt_ctr[0] % 2
                nt_ctr[0] += 1
                st[nt] = p
                mcol = (tb - 8) * 128
                tf, tq, tv = t_f[p], t_q[p], t_v[p]
                bf_ = bank()
                mm_group(PS[bf_][:, :], [(Wf[kc], hsrc(kc)) for kc in range(16)], [("w", s, 1)] + hk, [("ps", bf_)])
                bi = bank()
                mm_group(PS[bi][:, :], [(Wi[kc], hsrc(kc)) for kc in range(16)], [("w", s, 2)] + hk, [("ps", bi)])
                for hv, (a, b_) in enumerate(HV):
                    act(tf[:, a:b_], PS[bf_][:, a:b_], AF.Sigmoid, [("ps", bf_)], [K("f", p, hv)])
                act(tv[:, :], PS[bi][:, :], AF.Copy, [("ps", bi)], [K("v", p)])
                if kind == "M":
                    bq = bank()
                    mm_group(PS[bq][:, :], [(Wq[kc], hsrc(kc)) for kc in range(16)], [("w", s, 0)] + hk, [("ps", bq)])
                    bo = bank()
                    mm_group(PS[bo][:, :], [(Wo[kc], hsrc(kc)) for kc in range(16)], [("w", s, 3)] + hk, [("ps", bo)])
                    act(tq[:, :], PS[bq][:, :], AF.Sigmoid, [("ps", bq)], [K("q", p)])
                    act(sog[:, mcol:mcol + 512], PS[bo][:, :], AF.Sigmoid, [("ps", bo)], [K("sog", nt)])
                    tt_op(tq[:, :], tq[:, :], PS[bq][:, :], ALU.mult, [K("q", p), ("ps", bq)], [K("q", p)])
                    tt_op(sog[:, mcol:mcol + 512], sog[:, mcol:mcol + 512], PS[bo][:, :], ALU.mult,
                          [K("sog", nt), ("ps", bo)], [K("sog", nt)])

            def chain(nt):
                kind, hsrc, hk, tb = NTS[nt]
                p = st[nt]
                mcol = (tb - 8) * 128
                tf, tq, tke = t_f[p], t_q[p], t_ke[p]
                kq = K("q", p)
                n0 = tb * 2
                c3 = lambda ap: ap.rearrange("p (c l) -> p c l", l=64)
                steps = []
                for hv, (a, b_) in enumerate(HV):
                    kf, klf, kb, keb, kenb = K("f", p, hv), K("lf", hv), K("b", hv), K("eb", hv), K("enb", hv)
                    kke = K("ke", p) if False else K("ke", p, hv)
                    sl = slice(a, b_)
                    ebv = c3(t_eb[:, sl])
                    ops = []
                    ops.append(lambda kf=kf, sl=sl: ts_op(tf[:, sl], tf[:, sl], oml_h, lb_h, ALU.mult, ALU.add,
                                                          [kf, ("c", "oml"), ("c", "lbm")], [kf]))
                    ops.append(lambda kf=kf, klf=klf, sl=sl: act(t_lf[:, sl], tf[:, sl], AF.Ln, [kf], [klf]))
                    ops.append(lambda klf=klf, kb=kb, sl=sl: S.op(
                        "dve", lambda e: e.tensor_tensor_scan(out=t_b[:, sl], data0=rmask[:, sl], data1=t_lf[:, sl],
                                                              initial=0.0, op0=ALU.mult, op1=ALU.add),
                        reads=[klf, ("c", "rmask")], writes=[kb], cost=0.7))
                    ops.append(lambda kb=kb, keb=keb, sl=sl: act(t_eb[:, sl], t_b[:, sl], AF.Exp, [kb], [keb]))
                    ops.append(lambda kb=kb, kenb=kenb, sl=sl: act(t_enb[:, sl], t_b[:, sl], AF.Exp, [kb], [kenb], scale=-1.0))
                    ops.append(lambda kf=kf, sl=sl: ts_op(tf[:, sl], tf[:, sl], -1.0, 1.0, ALU.mult, ALU.add, [kf], [kf],
                                                          eng="pool"))
                    ops.append(lambda keb=keb, ebv=ebv, hv=hv: act(dec[:, n0 + 4 * hv:n0 + 4 * hv + 4], ebv[:, :, 63], AF.Copy,
                                                                   [keb], [K("dec", nt)]))
                    if kind == "M":
                        ops.append(lambda kf=kf, kenb=kenb, sl=sl, a=a, b_=b_: tt_op(
                            k_inT[:, mcol + a:mcol + b_], tf[:, sl], t_enb[:, sl], ALU.mult, [kf, kenb], [K("kin", nt)]))
                        ops.append(lambda keb=keb, sl=sl, a=a, b_=b_: tt_op(
                            q_inT[:, mcol + a:mcol + b_], tq[:, sl], t_eb[:, sl], ALU.mult, [kq, keb], [K("qin", nt)]))
                    ops.append(lambda kenb=kenb, keb=keb, klf=klf, sl=sl, ebv=ebv: tt_op(
                        c3(t_lf[:, sl]), c3(t_enb[:, sl]), ebv[:, :, 63:64].to_broadcast([128, 4, 64]), ALU.mult,
                        [kenb, keb, klf], [klf]))
                    ops.append(lambda klf=klf, kf=kf, sl=sl: tt_op(tke[:, sl], t_lf[:, sl], tf[:, sl], ALU.mult,
                                                                   [klf, kf], [K("ke", p)]))
                    steps.append(ops)
                for i in range(len(steps[0])):
                    steps[0][i]()
                    steps[1][i]()

            def trans(nt):
                kind, hsrc, hk, tb = NTS[nt]
                p = st[nt]
                for (src, ksrc, dst, kdst, use_act) in ((t_v[p], K("v", p), v_tok, K("vtok", nt), False),
                                                        (t_ke[p], K("ke", p), ke_tok, K("ketok", nt), True)):
                    bt = bank()
                    pbt = psb(bt)

                    def fn_t(e, pbt=pbt, src=src):
                        ins = None
                        for k in range(4):
                            ins = e.transpose(out=pbt[:, k * 128:(k + 1) * 128], in_=src[:, k * 128:(k + 1) * 128],
                                              identity=idb[:, :])
                        return ins
                    S.op("pe", fn_t, reads=[ksrc, ("c", "idb")], writes=[("ps", bt)], cost=0.35)
                    pv = pbt[:, 0:512].rearrange("p (k t) -> p k t", k=4)
                    if use_act:
                        act(dst[:, tb:tb + 4, :], pv, AF.Copy, [("ps", bt)], [kdst])
                    else:
                        cp_op(dst[:, tb:tb + 4, :], pv, [("ps", bt)], [kdst])

            def scan(part):
                if part == 0:
                    S.op("dve", lambda e: e.memset(S_pp[:, 0, :], 0.0), writes=[K("Spp", 0)])
                else:
                    act(S_pp[:, 0, :], S_init[:, h, :], AF.Copy, [("Sinit", h)], [K("Spp", 0)])
                for grp in (part * 2, part * 2 + 1):
                    bdp = [bank(), bank()]

                    def fn_d(e, grp=grp, bdp=bdp):
                        ins = None
                        for j in range(4):
                            for a_ in range(2):
                                tt_ = grp * 4 + j
                                r0 = a_ * 64
                                ins = e.matmul(PS[bdp[a_]][:, j * 128:(j + 1) * 128], lhsT=ke_tok[r0:r0 + 64, tt_, :],
                                               rhs=v_tok[r0:r0 + 64, tt_, :], start=True, stop=True)
                        return ins
                    S.op("pe", fn_d, reads=[K("ketok", grp), K("vtok", grp)], writes=[("ps", bdp[0]), ("ps", bdp[1])], cost=0.6)
                    for j in range(4):
                        for a_ in range(2):
                            n = grp * 8 + j * 2 + a_
                            cur, nxt = n % 2, (n + 1) % 2
                            if n >= 16:
                                act(S_bf[:, n - 16, :], S_pp[:, cur, :], AF.Copy, [K("Spp", cur)], [K("Sbf", (n - 16) // 2)])
                            stt_op(S_pp[:, nxt, :], S_pp[:, cur, :], dec[:, n:n + 1],
                                   PS[bdp[a_]][:, j * 128:(j + 1) * 128], ALU.mult, ALU.add,
                                   [K("Spp", cur), K("dec", grp), ("ps", bdp[a_])], [K("Spp", nxt)])
                if part == 1:
                    dma("sp", o_hp[h, :, :], S_pp[:, 0, :], "ohp", reads=[K("Spp", 0)])

            def o_main(half):
                pend = None
                for tt_ in range(half * 4, half * 4 + 4):
                    nt = 2 + tt_ // 4
                    bsc = bank()
                    mm_group(PS[bsc][:, 0:128],
                             [(k_inT[:, tt_ * 128:(tt_ + 1) * 128], q_inT[:, tt_ * 128:(tt_ + 1) * 128])],
                             [K("kin", nt), K("qin", nt)], [("ps", bsc)])
                    sc_ = scm[tt_ % 2]
                    ksc = K("scm", tt_ % 2)
                    tt_op(sc_[:, :], PS[bsc][:, 0:128], mask2[:, :], ALU.mult, [("ps", bsc), ("c", "mask2")], [ksc])
                    pob = PS[BO2[0]]
                    c0 = (tt_ % 4) * 128

                    def fn_o(e, tt_=tt_, pob=pob, c0=c0, sc_=sc_):
                        ins = None
                        for a_ in range(2):
                            oc = c0 + 64 * a_
                            e.matmul(pob[:, oc:oc + 64], lhsT=S_bf[:, 2 * tt_ + a_, :],
                                     rhs=q_inT[:, tt_ * 128 + 64 * a_:tt_ * 128 + 64 * a_ + 64], start=True, stop=False)
                            ins = e.matmul(pob[:, oc:oc + 64], lhsT=v_tok[:, 8 + tt_, :], rhs=sc_[:, 64 * a_:64 * a_ + 64],
                                           start=False, stop=True)
                        return ins

                    def emit_o(fn_o=fn_o, tt_=tt_, nt=nt, ksc=ksc):
                        S.op("pe", fn_o, reads=[K("vtok", nt), ksc, K("Sbf", tt_), K("qin", nt)]
                             + ([("ps", BO2[0])] if tt_ % 4 else []), writes=[("ps", BO2[0])])
                    if pend is not None:
                        pend()
                    pend = emit_o
                pend()

            def norm(piece):
                for (pb_, ncol, dcol, ksog) in (((BO2[0], 512, 0, K("sog", 2)), (BO2[0], 512, 512, K("sog", 3)),
                                                 (BOS, 16, 1024, ksgS))[piece],):
                    if pb_ == BOS:
                        po_ap = PS[pb_][:, 0:272].rearrange("p (a b) -> p a b", b=17)[:, :, 0]
                    else:
                        po_ap = PS[pb_][:, 0:ncol]
                    act(osq[:, 0:ncol], po_ap, AF.Square, [("ps", pb_)], [K("osq")])
                    bss = bank()
                    mm_group(PS[bss][:, 0:ncol], [(ones_b[:, :], osq[:, 0:ncol])], [K("osq"), ("c", "ones")], [("ps", bss)])
                    act(rstd_t[:, 0:ncol], PS[bss][:, 0:ncol], AF.Ln, [("ps", bss)], [K("rstd")], scale=1.0 / 128, bias=EPS)
                    act(rstd_t[:, 0:ncol], rstd_t[:, 0:ncol], AF.Exp, [K("rstd")], [K("rstd")], scale=-0.5)
                    tt_op(on_t[:, 0:ncol], po_ap, rstd_t[:, 0:ncol], ALU.mult, [("ps", pb_), K("rstd")], [K("on")])
                    sg_ap = sgS[:, :] if pb_ == BOS else sog[:, dcol:dcol + ncol]
                    stt_op(oT[:, h, dcol:dcol + ncol], on_t[:, 0:ncol], ogg[:, 0:1], sg_ap,
                           ALU.mult, ALU.mult, [K("on"), ksog, ("c", "ogg")], [("oT",)])

            sgS = sogS[h % 2]
            ksgS = K("sogS", h % 2)
            sv = {}

            def s1():
                bs = bank()

                def fn_s(e, bs=bs):
                    ins = None
                    for blk, Wx in enumerate((Wq, Wf, Wi, Wo)):
                        for kc in range(16):
                            ins = e.matmul(PS[bs][:, blk * 16:(blk + 1) * 16], lhsT=Wx[kc], rhs=hxM[:, kc, 1024:1040],
                                           start=(kc == 0), stop=(kc == 15))
                    return ins
                S.op("pe", fn_s, reads=WK(s) + hx_keys(16, 17), writes=[("ps", bs)], cost=2.0)
                act(fS[:, :], PS[bs][:, 16:32], AF.Sigmoid, [("ps", bs)], [K("fS")])
                ts_op(fS[:, :], fS[:, :], oml_h, lb_h, ALU.mult, ALU.add, [K("fS"), ("c", "oml"), ("c", "lbm")], [K("fS")])
                ts_op(kS_b[:, :], fS[:, :], -1.0, 1.0, ALU.mult, ALU.add, [K("fS")], [K("kS")])
                act(vS_b[:, :], PS[bs][:, 32:48], AF.Copy, [("ps", bs)], [K("vS")])
                act(fS2[:, :], PS[bs][:, 0:16], AF.Sigmoid, [("ps", bs)], [K("fS2")])
                tt_op(qS_b[:, :], fS2[:, :], PS[bs][:, 0:16], ALU.mult, [K("fS2"), ("ps", bs)], [K("qS")])
                act(sgS[:, :], PS[bs][:, 48:64], AF.Sigmoid, [("ps", bs)], [ksgS])
                tt_op(sgS[:, :], sgS[:, :], PS[bs][:, 48:64], ALU.mult, [ksgS, ("ps", bs)], [ksgS])

            def s2():
                bts = bank()
                pbts = psb(bts)

                def fn_ts(e, pbts=pbts):
                    e.transpose(out=pbts[0:16, 0:128], in_=kS_b[:, :], identity=idb[:, :])
                    return e.transpose(out=pbts[0:16, 128:256], in_=vS_b[:, :], identity=idb[:, :])
                S.op("pe", fn_ts, reads=[K("kS"), K("vS"), ("c", "idb")], writes=[("ps", bts)])
                act(ktok_s[0:16, :], pbts[0:16, 0:128], AF.Copy, [("ps", bts)], [K("ktoks")])
                act(vtok_s[0:16, :], pbts[0:16, 128:256], AF.Copy, [("ps", bts)], [K("vtoks")])
                tt_op(vm[0:16, :, :], vtok_s[0:16, :].unsqueeze(1).to_broadcast([16, 16, 128]),
                      idb[0:16, 0:16].unsqueeze(2).to_broadcast([16, 16, 128]), ALU.mult,
                      [K("vtoks"), ("c", "idb")], [K("vm")])

            def s3():
                for hf in range(2):
                    S0 = S0b[hf]
                    kS0 = K("S0", hf)
                    bd = [bank(), bank()]
                    for q_ in range(2):
                        mm_group(PS[bd[q_]][:, :],
                                 [(ktok_s[0:16, :], vm[0:16, hf * 8 + q_ * 4:hf * 8 + q_ * 4 + 4, :])],
                                 [K("ktoks"), K("vm")], [("ps", bd[q_])])
                    tt_op(S0[:, :, :], S0[:, :, :], fS[:, hf * 8:(hf + 1) * 8].unsqueeze(2).to_broadcast([128, 8, 128]),
                          ALU.mult, [kS0, K("fS")], [kS0])
                    for q_ in range(2):
                        tt_op(S0[:, q_ * 4:(q_ + 1) * 4, :], S0[:, q_ * 4:(q_ + 1) * 4, :],
                              PS[bd[q_]][:, :].rearrange("p (b e) -> p b e", b=4), ALU.add,
                              [kS0, ("ps", bd[q_])], [kS0])
                    dma("sp", o_hs[hf * 8:(hf + 1) * 8, h, :, :].rearrange("b d e -> d b e"), S0[:, :, :],
                        "s0out%d" % hf, reads=[kS0])

            def s4():
                for hf in range(2):
                    S0 = S0b[hf]
                    kS0 = K("S0", hf)
                    act(S_bfs[:, :, :], S0[:, :, :], AF.Copy, [kS0], [K("Sbfs")])

                    def fn_os(e, hf=hf):
                        ins = None
                        for b_ in range(8):
                            col = hf * 8 + b_
                            ins = e.matmul(PS[BOS][:, col * 16:(col + 1) * 16], lhsT=S_bfs[:, b_, :], rhs=qS_b[:, 0:16],
                                           start=True, stop=True)
                        return ins
                    S.op("pe", fn_os, reads=[K("Sbfs"), K("qS")] + ([("ps", BOS)] if hf else []), writes=[("ps", BOS)])

            def prefetch():
                for hf in range(2):
                    dma("sp", S0b[hf][:, :, :], shgrn[hf * 8:(hf + 1) * 8, h, :, :].rearrange("b d e -> d b e"),
                        "s0ld%d" % hf, writes=[K("S0", hf)])

            def save_init():
                act(S_init[:, h, :], S_pp[:, 0, :], AF.Copy, [K("Spp", 0)], [("Sinit", h)])

            return dict(front=front, chain=chain, trans=trans, scan=scan, o_main=o_main, norm=norm, save_init=save_init,
                        s1=s1, s2=s2, s3=s3, s4=s4, prefetch=prefetch)


        bank_set[0] = list(range(8))
        prevP = None
        for h in range(8):
            s = next_slot()
            Hp = head_ops(h, s)
            Hp["front"](0)
            Hp["front"](1)
            if prevP is not None:
                prevP["scan"](0)
                prevP["save_init"]()
            Hp["chain"](0)
            Hp["trans"](0)
            Hp["chain"](1)
            Hp["trans"](1)
            prevP = Hp
        prevP["scan"](0)
        prevP["save_init"]()
        S.barrier_op(P_LO, len(S.ops), ("bar", "P"))
        bank_set[0] = [0, 1, 2, 3, 4, 7]
        arC = Arena([])
        arC.add_region(R2, 16 * TT * 2, 16 * 1024 * 2)
        cbuf = [[arC.alloc([128, 1042], F32) for _ in range(3)] for _ in range(2)]
        accs_b = [arC.alloc([128, 16], F32) for _ in range(2)]
        uov = uo[:, :].rearrange("p (c t) -> p c t", c=8)
        cwv = cw[:, :].rearrange("p (c k) -> p c k", c=8)

        def C_blocks(c, s):
            par = c % 2
            hcs, ubuf, accb = cbuf[par]
            kh, ku, ka = ("t", "hcs", par), ("t", "ubuf", par), ("t", "acc", par)
            accs = accs_b[par]
            kas = ("t", "accs", par)

            def cgroups(blk):
                Wl = [W[s][:, kc, blk * 128:(blk + 1) * 128] for kc in range(16)]
                bx, by, bz = bank(), bank(), bank()

                def fnx(e, Wl=Wl, bx=bx):
                    ins = None
                    for kc in range(16):
                        ins = e.matmul(PS[bx][:, 0:16], lhsT=Wl[kc], rhs=hxM[:, kc, 1024:1040],
                                       start=(kc == 0), stop=(kc == 15))
                    for kc in range(16):
                        ins = e.matmul(PS[bx][:, 16:32], lhsT=Wl[kc], rhs=hxPt[:, kc, :],
                                       start=(kc == 0), stop=(kc == 15))
                    return ins
                S.op("pe", fnx, reads=[("w", s, blk), ("hxPt",)] + hx_keys(16, 17), writes=[("ps", bx)], cost=1.0)
                mm_group(PS[by][:, :], [(Wl[kc], hxM[:, kc, 0:512]) for kc in range(16)],
                         [("w", s, blk)] + hx_keys(8, 12), [("ps", by)])
                mm_group(PS[bz][:, :], [(Wl[kc], hxM[:, kc, 512:1024]) for kc in range(16)],
                         [("w", s, blk)] + hx_keys(12, 16), [("ps", bz)])
                return bx, by, bz

            def b0():
                bx, by, bz = cgroups(0)
                act(hcs[:, 0:16], PS[bx][:, 0:16], AF.Copy, [("ps", bx)], [kh])
                act(hcs[:, 16:18], PS[bx][:, 30:32], AF.Copy, [("ps", bx)], [kh])
                act(hcs[:, 18:530], PS[by][:, :], AF.Copy, [("ps", by)], [kh])
                act(hcs[:, 530:1042], PS[bz][:, :], AF.Copy, [("ps", bz)], [kh])

            def b1():
                bx, by, bz = cgroups(2)
                tt_op(ubuf[:, 0:16], PS[bx][:, 0:16], hcs[:, 0:16], ALU.mult, [("ps", bx), kh], [ku])
                tt_op(ubuf[:, 16:18], PS[bx][:, 30:32], hcs[:, 16:18], ALU.mult, [("ps", bx), kh], [ku])
                tt_op(ubuf[:, 18:530], PS[by][:, :], hcs[:, 18:530], ALU.mult, [("ps", by), kh], [ku])
                tt_op(ubuf[:, 530:1042], PS[bz][:, :], hcs[:, 530:1042], ALU.mult, [("ps", bz), kh], [ku])
                ts_op(accb[:, 0:1024], ubuf[:, 16:1040], cwv[:, c, 0:1], None, ALU.mult, None, [ku, ("c", "cw")], [ka])
                stt_op(accb[:, 0:1024], ubuf[:, 17:1041], cwv[:, c, 1:2], accb[:, 0:1024], ALU.mult, ALU.add,
                       [ku, ka, ("c", "cw")], [ka])
                stt_op(accb[:, 0:1024], ubuf[:, 18:1042], cwv[:, c, 2:3], accb[:, 0:1024], ALU.mult, ALU.add,
                       [ku, ka, ("c", "cw")], [ka])
                ts_op(accs[:, :], scTv[:, 0, c, :], cwv[:, c, 0:1], None, ALU.mult, None,
                      [("c", "scT"), ("c", "cw")], [kas])
                stt_op(accs[:, :], scTv[:, 1, c, :], cwv[:, c, 1:2], accs[:, :], ALU.mult, ALU.add,
                       [("c", "scT"), kas, ("c", "cw")], [kas])
                stt_op(accs[:, :], ubuf[:, 0:16], cwv[:, c, 2:3], accs[:, :], ALU.mult, ALU.add,
                       [ku, kas, ("c", "cw")], [kas])
                act(uov[:, c, 0:2], ubuf[:, 1040:1042], AF.Copy, [ku], [("c", "uo")])
                act(uov[:, c, 2:18], ubuf[:, 0:16], AF.Copy, [ku], [("c", "uo")])

            def b2():
                bx, by, bz = cgroups(1)
                tt_op(aT[:, c, 0:512], PS[by][:, :], accb[:, 0:512], ALU.mult, [("ps", by), ka], [("aT",)])
                tt_op(aT[:, c, 512:1024], PS[bz][:, :], accb[:, 512:1024], ALU.mult, [("ps", bz), ka], [("aT",)])
                tt_op(aT[:, c, 1024:1040], PS[bx][:, 0:16], accs[:, :], ALU.mult, [("ps", bx), kas], [("aT",)])

            return b0, b1, b2

        assert fill_i[0] == 8
        fills[9](1)
        prev = None
        for h in range(8):
            cb0, cb1, cb2 = C_blocks(h, 0)
            s = 1
            H_ = head_ops(h, s)
            H_["prefetch"]()
            if prev is not None:
                prev["scan"](1)
                prev["o_main"](0)
                prev["norm"](0)
                prev["o_main"](1)
                prev["norm"](1)
                prev["norm"](2)
            H_["front"](2)
            H_["s1"]()
            XR.append(("bar", "P")); cb0(); XR.pop()
            H_["chain"](2)
            H_["front"](3)
            if h < 7:
                fills[9 + 2 * (h + 1)](1)
            H_["trans"](2)
            H_["s2"]()
            XR.append(("bar", "P")); cb1(); XR.pop()
            H_["chain"](3)
            H_["s3"]()
            H_["trans"](3)
            XR.append(("bar", "P")); cb2(); XR.pop()
            fills[8 + 2 * (h + 1)](0)
            H_["s4"]()
            prev = H_
        fill_i[0] = 24
        prev["scan"](1)
        prev["o_main"](0)
        prev["norm"](0)
        prev["o_main"](1)
        prev["norm"](1)
        prev["norm"](2)
        b1, b2 = bank(), bank()
        for half, bb in ((0, b1), (1, b2)):
            def fn_u(e, half=half, bb=bb):
                ins = None
                for k in range(4):
                    c = half * 4 + k
                    ins = e.matmul(PS[bb][0:18, k * 128:(k + 1) * 128], lhsT=uov[:, c, :], rhs=idf[:, :],
                                   start=True, stop=True)
                return ins
            S.op("pe", fn_u, reads=[("c", "uo"), ("c", "idf")], writes=[("ps", bb)])
        S.mark(sched=True)
        S.barrier_op(P_LO, len(S.ops), ("bar", "H"))
        XR.append(("bar", "H"))
        uot = view(EA, 26624, [128, 1024], F32)
        act(uot[0:18, 0:512], PS[b1][0:18, :], AF.Copy, [("ps", b1)], [("t", "uot")])
        act(uot[0:18, 512:1024], PS[b2][0:18, :], AF.Copy, [("ps", b2)], [("t", "uot")])
        dma("sp", o_cp, uot[0:2, :], "out", reads=[("t", "uot")])
        dma("sp", o_cs[:, 1, :], uot[2:18, :], "out", reads=[("t", "uot")])
        bank_set[0] = list(range(8))
        ar.reset()

        t1 = [ar.alloc([128, 512], F32) for _ in range(2)]
        t2 = [ar.alloc([128, 512], F32) for _ in range(2)]
        ar.cur[0] = 16 * TT * 2
        ar.cur[1] = 0
        t1 = [ar.alloc([128, 512], F32) for _ in range(2)]
        t2 = [ar.alloc([128, 512], F32) for _ in range(2)]
        it = 0
        for j in range(16):
            s = next_slot()
            for (c0, ncol, hk) in ((0, 352, hx_keys(8, 11)), (352, 352, hx_keys(10, 14)), (704, 336, hx_keys(13, 17))):
                p_ = it % 2
                it += 1
                bga, bgb, bA, bB = bank(), bank(), bank(), bank()
                mm_group(PS[bga][:, 0:ncol], [(W[s][:, kc, 0:128], hxM[:, kc, c0:c0 + ncol]) for kc in range(16)],
                         [("w", s, 0)] + hk, [("ps", bga)])
                mm_group(PS[bgb][:, 0:ncol], [(W[s][:, kc, 128:256], hxM[:, kc, c0:c0 + ncol]) for kc in range(16)],
                         [("w", s, 1)] + hk, [("ps", bgb)])
                mm_group(PS[bA][:, 0:ncol], [(W[s][:, kc, 256:384], aT[:, kc, c0:c0 + ncol]) for kc in range(8)],
                         [("w", s, 2), ("aT",)], [("ps", bA)])
                mm_group(PS[bB][:, 0:ncol], [(W[s][:, kc, 384:512], oT[:, kc, c0:c0 + ncol]) for kc in range(8)],
                         [("w", s, 3), ("oT",)], [("ps", bB)])
                k1, k2 = ("t", "t1", p_), ("t", "t2", p_)
                act(t1[p_][:, 0:ncol], PS[bga][:, 0:ncol], AF.Sigmoid, [("ps", bga)], [k1])
                act(t2[p_][:, 0:ncol], PS[bgb][:, 0:ncol], AF.Sigmoid, [("ps", bgb)], [k2])
                tt_op(t1[p_][:, 0:ncol], t1[p_][:, 0:ncol], PS[bA][:, 0:ncol], ALU.mult, [k1, ("ps", bA)], [k1])
                tt_op(t2[p_][:, 0:ncol], t2[p_][:, 0:ncol], PS[bB][:, 0:ncol], ALU.mult, [k2, ("ps", bB)], [k2])
                tt_op(mixT[:, j, c0:c0 + ncol], t1[p_][:, 0:ncol], t2[p_][:, 0:ncol], ALU.add, [k1, k2], [("mix",)])
        if stop == 3:
            S.fence()
            S.emit(nc, es)
            return nc


        def smp_group(b, s, src_keys):
            def fn(e):
                ins = None
                for r_ in range(4):
                    for j in range(4):
                        kc = r_ * 4 + j
                        ins = e.matmul(PS[b][32 * j:32 * j + 32, :], lhsT=R3[:, kc * TT + 1024:kc * TT + 1056],
                                       rhs=W[s][:, kc, :], start=(r_ == 0), stop=(r_ == 3),
                                       tile_position=(0, 32 * j))
                return ins
            S.op("pe", fn, reads=WK(s) + list(src_keys) + [("c", "r3pad")] + XR, writes=[("ps", b)], cost=1.1)

        def smp_reduce(zero_rest):
            for n in range(4):
                b = bank()
                cols = slice(n * 512, (n + 1) * 512)
                S.op("pe", lambda e, b=b, cols=cols: e.matmul(PS[b][0:16, :], lhsT=selT[:, :], rhs=x1v[:, 8, cols],
                                                              start=True, stop=True),
                     reads=[("x1", 8, n), ("c", "sel")] + XR, writes=[("ps", b)], cost=0.9)
                cp_op(x1v[0:16, 8, cols], PS[b][0:16, :], [("ps", b)], [("x1", 8, n)])
            if zero_rest:
                S.op("dve", lambda e: e.memset(x1v[32:64, 8, :], 0.0), reads=list(XR),
                     writes=[("x1", 8, n) for n in range(4)], cost=2.2)
                S.op("dve", lambda e: e.memset(x1v[64:128, 8, :], 0.0), reads=list(XR),
                     writes=[("x1", 8, n) for n in range(4)], cost=2.2)

        xsrc = [(x_main[t * 128:(t + 1) * 128, :], 128) for t in range(8)] + [(x_smp, 16)]
        S.op("dve", lambda e: e.memset(x1v[:, 8, :], 0.0), reads=list(XR),
             writes=[("x1", 8, n) for n in range(4)] + [("aT",), ("oT",)], cost=2.2)
        for t, (src, r) in enumerate(xsrc):
            dma("sp", x1v[0:r, t, :], src, "x1ld", writes=[("x1", t, n) for n in range(4)] + [("aT",), ("oT",)])
        for n in range(4):
            s = next_slot()
            for t, (src, r) in enumerate(xsrc):
                c0 = t * 128
                b = bank()
                if t == 8:
                    smp_group(b, s, [("mix",)])
                    tt_op(x1v[:, 8, n * 512:(n + 1) * 512], x1v[:, 8, n * 512:(n + 1) * 512], PS[b][:, :], ALU.add,
                          [("ps", b), ("x1", t, n)], [("x1", t, n)])
                    continue
                mm_group(PS[b][0:r, :], [(mixT[:, kc, c0:c0 + r], W[s][:, kc, :]) for kc in range(16)],
                         WK(s) + [("mix",)], [("ps", b)])
                tt_op(x1v[0:r, t, n * 512:(n + 1) * 512], x1v[0:r, t, n * 512:(n + 1) * 512], PS[b][0:r, :], ALU.add,
                      [("ps", b), ("x1", t, n)], [("x1", t, n)])
        smp_reduce(True)
        if stop == 4:
            S.fence()
            S.emit(nc, es)
            return nc

        xs4 = [ar.alloc([128, D], BF16), ar.alloc([128, D], BF16)]
        junk4 = ar.alloc([128, D], BF16)

        def dst4(i, hh):
            r = 128 if i < 8 else 16
            return (h2T[:, hh * 8:(hh + 1) * 8, i * 128:i * 128 + r],
                    [("h2", i, hh), ("hx", (8 + i) if i < 8 else 16, hh)])

        tiles4 = [(x1v[0:(128 if t < 8 else 16), t, :], 128 if t < 8 else 16, [("x1", t, n) for n in range(4)], t)
                  for t in range(9)]
        norm_transpose(tiles4, gf, "gf", xs4, junk4, dst4, "s4")
        if stop == 5:
            S.fence()
            S.emit(nc, es)
            return nc

        tr_ = [ar.alloc([128, 512], F32) for _ in range(3)]
        h2_keys = lambda t0, t1_: [("h2", i, hh) for i in range(t0, t1_) for hh in range(2)]
        it = 0
        for g in range(4):
            for q4 in range(4):
                s = next_slot()
                for fc in range(4):
                    fidx = q4 * 4 + fc
                    for (c0, ncol, hk) in ((0, 352, h2_keys(0, 3)), (352, 352, h2_keys(2, 6)), (704, 336, h2_keys(5, 9))):
                        b = bank()
                        mm_group(PS[b][:, 0:ncol],
                                 [(W[s][:, kc, fc * 128:(fc + 1) * 128], h2T[:, kc, c0:c0 + ncol]) for kc in range(16)],
                                 WK(s) + hk, [("ps", b)])
                        p_ = it % 3
                        it += 1
                        kt = ("t", "relu", p_)
                        act(tr_[p_][:, 0:ncol], PS[b][:, 0:ncol], AF.Relu, [("ps", b)], [kt])
                        tt_op(actT[:, fidx, c0:c0 + ncol], tr_[p_][:, 0:ncol], tr_[p_][:, 0:ncol], ALU.mult,
                              [kt], [("actT",)] + ([("mix",)] if g == 0 else []))
            for n in range(4):
                s = next_slot()
                for t, (src, r) in enumerate(xsrc):
                    c0 = t * 128
                    b = bank()
                    if t == 8:
                        smp_group(b, s, [("actT",)])
                        tt_op(x1v[:, 8, n * 512:(n + 1) * 512], x1v[:, 8, n * 512:(n + 1) * 512], PS[b][:, :],
                              ALU.add, [("ps", b), ("x1", t, n)], [("x1", t, n)])
                        continue
                    mm_group(PS[b][0:r, :], [(actT[:, fc, c0:c0 + r], W[s][:, fc, :]) for fc in range(16)],
                             WK(s) + [("actT",)], [("ps", b)])
                    tt_op(x1v[0:r, t, n * 512:(n + 1) * 512], x1v[0:r, t, n * 512:(n + 1) * 512], PS[b][0:r, :],
                          ALU.add, [("ps", b), ("x1", t, n)], [("x1", t, n)])

        smp_reduce(False)
        h2_all = [("h2", i, hh) for i in range(9) for hh in range(2)]
        yt = [view(R1, 0, [128, D], F32), view(R1, D * 4, [128, D], F32)]
        junk6 = view(R1, D * 8, [128, D], BF16)
        gfin = view(R1, D * 8 + D * 2, [128, D], F32)
        dma("sp", gfin, nfin_d, "misc", writes=[("c", "gfin")] + h2_all)
        for t, (src, r) in enumerate(xsrc):
            sl = t % 2
            ssv = ssb[:, sl * 2:sl * 2 + 2]
            kss = ("ss", sl)
            xk = [("x1", t, n) for n in range(4)]
            S.op("pool", lambda e, ssv=ssv: e.memset(ssv, 0.0), writes=[kss])
            act(junk6[0:r, :], x1v[0:r, t, :], AF.Square, xk + [kss, ("c", "gfin")], [("t", "junk6"), kss],
                accum_out=ssv[0:r, 0:1])
            act(ssv[0:r, 1:2], ssv[0:r, 0:1], AF.Ln, [kss], [kss], scale=1.0 / D, bias=EPS)
            act(ssv[0:r, 1:2], ssv[0:r, 1:2], AF.Exp, [kss], [kss], scale=-0.5)
            ky = ("t", "yt", sl)
            stt_op(yt[sl][0:r, :], x1v[0:r, t, :], ssv[0:r, 1:2], gfin[0:r, :], ALU.mult, ALU.mult,
                   xk + [kss, ("c", "gfin")], [ky])
            dst = y_main[t * 128:(t + 1) * 128, :] if t < 8 else y_smp
            dma("sp", dst, yt[sl][0:r, :], "yout%d" % sl, reads=[ky])

        if os.environ.get('DBG_MEM'):
            print('SBUF remaining', nc.sbuf_bytes_remaining)
        S.emit(nc, es)
    return nc


_CACHE = {}


def _consts():
    ident = np.eye(128, dtype=np.float32)
    s_idx = np.arange(128)[:, None]
    l_idx = np.arange(128)[None, :]
    mask2 = ((s_idx // 64 == l_idx // 64) & (l_idx >= s_idx)).astype(np.float32)
    rmask = np.ones((128, 512), np.float32)
    rmask[:, ::64] = 0.0
    sel = np.zeros((128, 16), np.float32)
    for p in range(128):
        if p % 32 < 16:
            sel[p, p % 32] = 1.0
    return ident, mask2, rmask, sel


def kernel(x_prompt, x_sample, state_conv, state_hgrn, norm_mix, w_in, conv_w, lb_logits, onorm_g,
           w_branch_a, w_branch_b, w_out, norm_ffn, w_up, w_down, norm_final):
    f32 = lambda a: np.ascontiguousarray(np.asarray(a, dtype=np.float32))
    x_prompt, x_sample, state_conv, state_hgrn = f32(x_prompt), f32(x_sample), f32(state_conv), f32(state_hgrn)
    if "nc" not in _CACHE:
        _CACHE["nc"] = build_program()
    nc = _CACHE["nc"]
    ident, mask2, rmask, sel = _consts()
    shared = {
        "gm": f32(np.asarray(norm_mix)[0].reshape(16, 128).T),
        "gf": f32(np.asarray(norm_ffn)[0].reshape(16, 128).T),
        "cw": f32(np.asarray(conv_w)[0].reshape(3, 8, 128).transpose(2, 1, 0).reshape(128, 24)),
        "lbl": f32(np.asarray(lb_logits).reshape(2, 8, 128).transpose(2, 0, 1).reshape(128, 16)),
        "ogg": f32(np.asarray(onorm_g)[0].reshape(128, 1)),
        "nfin": f32(np.broadcast_to(np.asarray(norm_final).reshape(1, D), (128, D))),
        "ident": ident, "mask2": mask2, "rmask": rmask, "sel": sel,
        "w_in": f32(np.asarray(w_in)[0]), "w_branch_a": f32(np.asarray(w_branch_a)[0]),
        "w_branch_b": f32(np.asarray(w_branch_b)[0]), "w_out": f32(np.asarray(w_out)[0]),
        "w_up": f32(np.asarray(w_up)[0]), "w_down": f32(np.asarray(w_down)[0]),
    }
    in_maps = []
    for c in range(N_CORES):
        sq, hf = c // 2, c % 2
        m = dict(shared)
        m["x_main"] = f32(x_prompt[sq, hf * 1024:(hf + 1) * 1024])
        m["x_pre"] = f32(x_prompt[sq, 0:1024]) if hf == 1 else np.zeros((1024, D), np.float32)
        m["x_smp"] = f32(x_sample[c * 16:(c + 1) * 16, 0])
        m["sconv"] = f32(state_conv[0, c * 16:(c + 1) * 16].reshape(16, 2048))
        m["shgrn"] = f32(state_hgrn[0, c * 16:(c + 1) * 16])
        in_maps.append(m)
    if _CACHE.get('debug_return_maps'):
        return in_maps
    res = run_bass_kernel_spmd(nc, in_maps, core_ids=list(range(N_CORES)))
    R = res.results
    yp = np.zeros((4, 2048, D), np.float32)
    ys = np.zeros((128, 1, D), np.float32)
    ncp = np.zeros((1, 4, 2, 1024), np.float32)
    nhp = np.zeros((1, 4, 8, 128, 128), np.float32)
    ncs = np.zeros((1, 128, 2, 1024), np.float32)
    nhs = np.zeros((1, 128, 8, 128, 128), np.float32)
    for c in range(N_CORES):
        sq, hf = c // 2, c % 2
        r = R[c]
        yp[sq, hf * 1024:(hf + 1) * 1024] = np.asarray(r["y_main"])
        ys[c * 16:(c + 1) * 16, 0] = np.asarray(r["y_smp"])
        ncs[0, c * 16:(c + 1) * 16] = np.asarray(r["o_cs"])
        nhs[0, c * 16:(c + 1) * 16] = np.asarray(r["o_hs"])
        if hf == 1:
            ncp[0, sq] = np.asarray(r["o_cp"])
            nhp[0, sq] = np.asarray(r["o_hp"])
    return yp, ys, ncp, nhp, ncs, nhs
```

```python
import bisect
import os
from contextlib import ExitStack
import numpy as np
import concourse.bass as bass
import concourse.mybir as mybir
from concourse.bass_utils import run_bass_kernel_spmd

F32 = mybir.dt.float32
BF16 = mybir.dt.bfloat16
ALU = mybir.AluOpType
AF = mybir.ActivationFunctionType

D = 2048
NK = 16
TM = 1024
TSM = 16
TT = TM + TSM
EPS = 1e-6
SAME_SYNC = True
SCHEDULE = True
LOOKAHEAD = 12
SCHED_SEGS = (1, 2)
TBL_COST = 1.3
PRIO = 1
SLACK = 0.5
N_CORES = 8
ALLOC_LOG = []


class Sched:
    ENGS = ("pe", "act", "dve", "pool", "sp")

    def __init__(self):
        self.ops = []
        self.keys = {}
        self.bounds = []
        self.fences = set()
        self.region_sched = {}
        self.scratch = None
        self.dma_sem_ops = {}

    def op(self, eng, fn, reads=(), writes=(), dma=None, cost=0.3, lat=0.0, tbl=None):
        gi = len(self.ops)
        deps = set()
        ps_r = [k for k in reads if k and k[0] == "ps"]
        if ps_r:
            reads = [k for k in reads if not (k and k[0] == "ps")]
            writes = list(writes) + ps_r
        for k in reads:
            st = self.keys.get(k)
            if st is not None and st[0] is not None:
                deps.add(st[0])
        for k in writes:
            st = self.keys.get(k)
            if st is not None:
                if st[0] is not None:
                    deps.add(st[0])
                deps.update(st[1])
        for k in reads:
            self.keys.setdefault(k, [None, []])[1].append(gi)
        for k in writes:
            self.keys[k] = [gi, []]
        deps.discard(gi)
        self.ops.append(dict(eng=eng, fn=fn, deps=deps, dma=dma, gi=gi, cost=cost, lat=lat, tbl=tbl))
        return gi

    def fence(self, sched=False):
        self.bounds.append(len(self.ops))
        self.fences.add(len(self.ops))
        self.region_sched[len(self.ops)] = sched

    def mark(self, sched=False):
        self.bounds.append(len(self.ops))
        self.region_sched[len(self.ops)] = sched

    def barrier_op(self, lo, hi, key):
        gi = self.op("dve", lambda e: e.memset(self.scratch, 0.0), writes=[key], cost=0.1)
        self.ops[gi]["deps"] |= set(range(lo, hi))
        return gi

    def finalize(self):
        ops = self.ops
        bounds = [0] + [b for b in self.bounds if 0 < b < len(ops)] + [len(ops)]
        bounds = sorted(set(bounds))
        order = []
        t_eng = {e: 0.0 for e in self.ENGS}
        finish = {}
        for si in range(len(bounds) - 1):
            lo, hi = bounds[si], bounds[si + 1]
            seg = range(lo, hi)
            t0 = max(t_eng.values())
            for e in self.ENGS:
                t_eng[e] = t0
            if (not SCHEDULE) or (not self.region_sched.get(lo, False)):
                order.extend(seg)
                continue
            ndep = {}
            users = {}
            for i in seg:
                c = 0
                for d in ops[i]["deps"]:
                    if d >= lo:
                        c += 1
                        users.setdefault(d, []).append(i)
                ndep[i] = c
            blevel = {}
            for i in reversed(seg):
                m = 0.0
                for u in users.get(i, ()):
                    if blevel[u] > m:
                        m = blevel[u]
                blevel[i] = ops[i]["cost"] + ops[i]["lat"] + m
            ready = {e: [] for e in self.ENGS}
            import heapq
            for i in seg:
                if ndep[i] == 0:
                    heapq.heappush(ready[ops[i]["eng"]], i)
            nleft = hi - lo
            cur_tbl = [None]
            while nleft:
                best = None
                for e in self.ENGS:
                    cand = heapq.nsmallest(LOOKAHEAD, ready[e])
                    for i in cand:
                        o = ops[i]
                        est = t_eng[e]
                        for d in o["deps"]:
                            if d >= lo:
                                fd = finish[d] + (0.15 if ops[d]["eng"] != e else 0.1)
                                if fd > est:
                                    est = fd
                        pen = TBL_COST if (o["tbl"] is not None and o["tbl"] != cur_tbl[0]) else 0.0
                        if PRIO == 0:
                            key = (est + pen, i)
                        elif PRIO == 1:
                            key = (est + pen, -blevel[i], i)
                        else:
                            key = (round((est + pen) / SLACK), -blevel[i], i)
                        if best is None or key < best[0]:
                            best = (key, i, e, est, pen)
                _, i, e, est, pen = best
                ready[e].remove(i)
                heapq.heapify(ready[e])
                o = ops[i]
                if o["tbl"] is not None:
                    cur_tbl[0] = o["tbl"]
                t_eng[e] = est + pen + o["cost"]
                finish[i] = est + pen + o["cost"] + o["lat"]
                order.append(i)
                nleft -= 1
                for u in users.get(i, ()):
                    ndep[u] -= 1
                    if ndep[u] == 0:
                        heapq.heappush(ready[ops[u]["eng"]], u)
        self.est_total = max(t_eng.values())
        if os.environ.get("DBG_SCHED"):
            te = {e: 0.0 for e in self.ENGS}
            fin = {}
            seg_of = lambda i: bisect.bisect_right(bounds, i) - 1
            cur = 0
            seg_start = 0.0
            busy = {e: 0.0 for e in self.ENGS}
            for i in order + [None]:
                sg = seg_of(i) if i is not None else -1
                if sg != cur:
                    t0 = max(te.values())
                    print("SCHED seg %d: %.1f us  busy %s" % (cur, t0 - seg_start,
                          " ".join("%s=%.0f" % (e, busy[e]) for e in self.ENGS)))
                    seg_start = t0
                    busy = {e: 0.0 for e in self.ENGS}
                    for e in self.ENGS:
                        te[e] = t0
                    cur = sg
                if i is None:
                    break
                o = ops[i]
                e = o["eng"]
                est = te[e]
                for d in o["deps"]:
                    if d in fin:
                        fd = fin[d] + (0.15 if ops[d]["eng"] != e else 0.1)
                        est = max(est, fd)
                pen = 0.0
                if o["tbl"] is not None:
                    if o["tbl"] != getattr(self, "_dbg_tbl", None):
                        pen = TBL_COST
                    self._dbg_tbl = o["tbl"]
                te[e] = est + pen + o["cost"]
                busy[e] += o["cost"] + pen
                fin[i] = est + pen + o["cost"] + o["lat"]
            print("SCHED model total %.1f us, nops %d" % (max(te.values()), len(ops)))
        fence_list = sorted(self.fences)
        newpos = {old: new for new, old in enumerate(order)}
        new_ops = []
        for new, old in enumerate(order):
            o = ops[old]
            o["deps"] = set(newpos[d] for d in o["deps"])
            o["gi"] = new
            o["seg"] = bisect.bisect_right(fence_list, old)
            new_ops.append(o)
        self.ops = ops = new_ops
        last_on_eng = {}
        last_dma = {}
        pending = {}
        cur_seg = 0
        seg_last_eng, seg_last_dma = {}, {}
        for o in ops:
            if o["seg"] != cur_seg:
                deps = set(last_on_eng.values()) | set(last_dma.values())
                for e in self.ENGS:
                    pending[e] = pending.get(e, set()) | deps
                cur_seg = o["seg"]
            fd = pending.pop(o["eng"], None)
            if fd:
                o["deps"] |= fd
                o["deps"].discard(o["gi"])
            last_on_eng[o["eng"]] = o["gi"]
            if o["dma"] is not None:
                last_dma[o["dma"]] = o["gi"]
        self.dma_sem_ops = {}
        for o in ops:
            if o["dma"] is not None:
                self.dma_sem_ops.setdefault(o["dma"], []).append(o["gi"])

    def emit(self, nc, es):
        self.finalize()
        ops = self.ops
        compute_engs = ("pe", "act", "dve", "pool")
        needed = set()
        for o in ops:
            for d in o["deps"]:
                do = ops[d]
                if do["dma"] is not None:
                    continue
                if do["eng"] == o["eng"] and (o["eng"] == "pe" or not SAME_SYNC):
                    continue
                needed.add(d)
        ordinal = {}
        cnt = {e: 0 for e in compute_engs + ("sp",)}
        for o in ops:
            if o["dma"] is None and o["gi"] in needed:
                cnt[o["eng"]] += 1
                ordinal[o["gi"]] = cnt[o["eng"]]
        eng_sem = {e: es.enter_context(nc.semaphore("sem_" + e)) for e in compute_engs + ("sp",)}
        dma_sem = {k: es.enter_context(nc.semaphore("dsem_" + k)) for k in self.dma_sem_ops}
        per_eng = {e: [] for e in compute_engs + ("sp",)}
        for o in ops:
            per_eng[o["eng"]].append(o)

        def replay(engname, e):
            waited = {}
            for o in per_eng[engname]:
                wl = {}
                for d in o["deps"]:
                    do = ops[d]
                    if do["dma"] is not None:
                        lst = self.dma_sem_ops[do["dma"]]
                        n = bisect.bisect_left(lst, o["gi"])
                        key = ("d", do["dma"])
                        val = 16 * n
                    else:
                        if do["eng"] == engname and (engname == "pe" or not SAME_SYNC):
                            continue
                        key = ("e", do["eng"])
                        val = ordinal[d]
                    if val > wl.get(key, 0):
                        wl[key] = val
                for key, val in wl.items():
                    if waited.get(key, 0) >= val:
                        continue
                    waited[key] = val
                    sem = dma_sem[key[1]] if key[0] == "d" else eng_sem[key[1]]
                    e.wait_ge(sem, val)
                ins = o["fn"](e)
                if o["dma"] is not None:
                    ins.then_inc(dma_sem[o["dma"]], 16)
                elif o["gi"] in needed:
                    ins.then_inc(eng_sem[engname], 1)
            if engname == "sp":
                for k, lst in self.dma_sem_ops.items():
                    e.wait_ge(dma_sem[k], 16 * len(lst))

        block = es.enter_context(nc.Block())

        @block.tensor
        def _(e):
            replay("pe", e)

        @block.scalar
        def _(e):
            replay("act", e)

        @block.vector
        def _(e):
            replay("dve", e)

        @block.gpsimd
        def _(e):
            replay("pool", e)

        @block.sync
        def _(e):
            replay("sp", e)


def build_program(stop=99):
    nc = bass.Bass("TRN2", target_bir_lowering=False)

    def din(name, shape):
        return nc.dram_tensor(name, list(shape), F32, kind="ExternalInput").ap()

    def dout(name, shape):
        return nc.dram_tensor(name, list(shape), F32, kind="ExternalOutput").ap()

    x_pre = din("x_pre", [1024, D])
    x_main = din("x_main", [1024, D])
    x_smp = din("x_smp", [16, D])
    sconv = din("sconv", [16, 2048])
    shgrn = din("shgrn", [16, 8, 128, 128])
    gm_d = din("gm", [128, 16])
    gf_d = din("gf", [128, 16])
    cw_d = din("cw", [128, 24])
    lbl_d = din("lbl", [128, 16])
    og_d = din("ogg", [128, 1])
    nfin_d = din("nfin", [128, D])
    ident_d = din("ident", [128, 128])
    sel_d = din("sel", [128, 16])
    mask2_d = din("mask2", [128, 128])
    rmask_d = din("rmask", [128, 512])
    w_in = din("w_in", [D, 11264])
    w_a = din("w_branch_a", [1024, D])
    w_b = din("w_branch_b", [1024, D])
    w_out = din("w_out", [D, D])
    w_up = din("w_up", [D, 8192])
    w_down = din("w_down", [8192, D])

    y_main = dout("y_main", [1024, D])
    y_smp = dout("y_smp", [16, D])
    o_cp = dout("o_cp", [2, 1024])
    o_hp = dout("o_hp", [8, 128, 128])
    o_cs = dout("o_cs", [16, 2, 1024])
    o_hs = dout("o_hs", [16, 8, 128, 128])

    S = Sched()
    with ExitStack() as es:
        def sb(name, shape, dtype):
            return es.enter_context(nc.sbuf_tensor("s_" + name, list(shape), dtype))

        R1 = sb("R1", [128, 16 * TT], BF16)
        R2 = sb("R2", [128, 18 * D], BF16)
        R3 = sb("R3", [128, 16 * TT + 32], BF16)
        W = [sb("W0", [128, 16, 512], BF16), sb("W1", [128, 16, 512], BF16)]
        E_BYTES = 30 * 1024
        EA = sb("EA", [128, E_BYTES // 2], BF16)
        idb = sb("idb", [128, 128], BF16)
        idf = sb("idf", [128, 128], F32)
        selT = sb("selT", [128, 16], F32)
        ones_b = sb("ones_b", [128, 128], BF16)
        mask2 = sb("mask2", [128, 128], F32)
        rmask = sb("rmask", [128, 512], F32)
        gm = sb("gm", [128, 16], F32)
        gf = sb("gf", [128, 16], F32)
        cw = sb("cw", [128, 24], F32)
        lbl = sb("lbl", [128, 16], F32)
        lbm = sb("lbm", [128, 8], F32)
        oml = sb("oml", [128, 8], F32)
        ogg = sb("ogg", [128, 1], F32)
        ssb = sb("ssb", [128, 8], F32)
        scT = sb("scT", [128, 256], F32)
        uo = sb("uo", [128, 8 * 18], F32)
        dec = sb("dec", [128, 32], F32)
        hxPt = sb("hxPt", [128, 16, 16], BF16)
        S.scratch = ssb[:, 6:7]
        PS = [es.enter_context(nc.psum_tensor("ps%d" % i, [128, 512], F32)) for i in range(8)]

        bank_ctr = [0]
        bank_set = [list(range(8))]

        def bank():
            bs_ = bank_set[0]
            b = bs_[bank_ctr[0] % len(bs_)]
            bank_ctr[0] += 1
            return b

        def view(raw, off, shape, dtype):
            n = 1
            for s_ in shape[1:]:
                n *= s_
            esz = 2 if dtype == BF16 else 4
            a = off // 2
            ln = n * esz // 2
            assert off % 4 == 0 and a + ln <= raw.shape[1], (off, shape, raw.shape)
            ap = raw[:, a:a + ln]
            if dtype != BF16:
                ap = ap.bitcast(dtype)
            if len(shape) == 3:
                ap = ap.rearrange("p (a b) -> p a b", a=shape[1])
            return ap

        class Arena:
            def __init__(self, raws):
                self.raws = [(r, 0, n) for (r, n) in raws]
                self.nbase = len(self.raws)
                self.reset()

            def reset(self):
                self.raws = self.raws[:self.nbase]
                self.cur = [0 for _ in self.raws]

            def add_region(self, raw, base, nbytes):
                self.raws.append((raw, base, nbytes))
                self.cur.append(0)

            def alloc(self, shape, dtype):
                n = 1
                for s_ in shape[1:]:
                    n *= s_
                nb = n * (2 if dtype == BF16 else 4)
                nb = (nb + 31) // 32 * 32
                for i, (raw, base, tot) in enumerate(self.raws):
                    if self.cur[i] + nb <= tot:
                        v = view(raw, base + self.cur[i], shape, dtype)
                        ALLOC_LOG.append((tuple(shape), str(dtype), i, self.cur[i]))
                        self.cur[i] += nb
                        return v
                raise RuntimeError("arena overflow %s" % (shape,))

        ar = Arena([(R3, 16 * TT * 2), (EA, E_BYTES)])

        hxM = R1[:, :].rearrange("p (k t) -> p k t", k=16)
        R2b = R2[:, :]
        aT = R2b[:, 0:8 * TT].rearrange("p (c t) -> p c t", c=8)
        oT = R2b[:, 8 * TT:16 * TT].rearrange("p (c t) -> p c t", c=8)
        hxP = R2b[:, 16 * TT:16 * TT + 16 * 1024].rearrange("p (k t) -> p k t", k=16)
        x1v = R2[:, :].bitcast(F32).rearrange("p (t c) -> p t c", t=9)
        mixT = R3[:, 0:16 * TT].rearrange("p (k t) -> p k t", k=16)
        actT = mixT
        h2T = hxM

        def psb(b):
            return PS[b][:, :].bitcast(BF16)

        XR = []

        def ncols_of(ap):
            n = 1
            for d_ in tuple(ap.shape)[1:]:
                n *= int(d_)
            return n

        def dma(eng, out, in_, sem, reads=(), writes=()):
            nbytes = ncols_of(out) * int(tuple(out.shape)[0]) * 4
            S.op(eng, lambda e: e.dma_start(out=out, in_=in_), reads=list(reads) + XR, writes=writes, dma=sem,
                 cost=(1.3 if eng == "pool" else 0.3), lat=2.0 + nbytes / 300e3)

        def mm_group(out_ap, pairs, reads, writes):
            def fn(e):
                n = len(pairs)
                ins = None
                for i, (l, r) in enumerate(pairs):
                    ins = e.matmul(out_ap, lhsT=l, rhs=r, start=(i == 0), stop=(i == n - 1))
                return ins
            cost = sum(max(ncols_of(r), 64) / 2400.0 + 0.01 for (_, r) in pairs)
            S.op("pe", fn, reads=list(reads) + XR, writes=writes, cost=cost)

        def act(out, in_, func, reads, writes, **kw):
            tbl = "sig" if func == AF.Sigmoid else ("lnexp" if func in (AF.Ln, AF.Exp) else None)
            S.op("act", lambda e: e.activation(out=out, in_=in_, func=func, **kw), reads=list(reads) + XR, writes=writes,
                 cost=0.22 + ncols_of(out) / 1200.0, tbl=tbl)

        def vcost(eng, out):
            return (0.25 + ncols_of(out) / 480.0) if eng == "pool" else (0.12 + ncols_of(out) / 960.0)

        def tt_op(out, in0, in1, op, reads, writes, eng="dve"):
            S.op(eng, lambda e: e.tensor_tensor(out=out, in0=in0, in1=in1, op=op), reads=list(reads) + XR, writes=writes,
                 cost=vcost(eng, out))

        def cp_op(out, in_, reads, writes, eng="dve"):
            S.op(eng, lambda e: e.tensor_copy(out=out, in_=in_), reads=list(reads) + XR, writes=writes, cost=vcost(eng, out))

        def ts_op(out, in0, s1, s2, op0, op1, reads, writes, eng="dve"):
            if op1 is None:
                S.op(eng, lambda e: e.tensor_scalar(out=out, in0=in0, scalar1=s1, scalar2=None, op0=op0),
                     reads=list(reads) + XR, writes=writes, cost=vcost(eng, out))
            else:
                S.op(eng, lambda e: e.tensor_scalar(out=out, in0=in0, scalar1=s1, scalar2=s2, op0=op0, op1=op1),
                     reads=list(reads) + XR, writes=writes, cost=vcost(eng, out))

        def stt_op(out, in0, scalar, in1, op0, op1, reads, writes, eng="dve"):
            S.op(eng, lambda e: e.scalar_tensor_tensor(out=out, in0=in0, scalar=scalar, in1=in1, op0=op0, op1=op1),
                 reads=list(reads) + XR, writes=writes, cost=vcost(eng, out))

        WK = lambda s: [("w", s, q) for q in range(4)]

        def wcols(src, c0, n, nkc=16):
            return src[:, c0:c0 + n].rearrange("(kc p) c -> p kc c", p=128)

        dma("pool", idb[:, :], ident_d, "constp", writes=[("c", "idb")])
        S.op("dve", lambda e: e.memset(R3[:, 16 * TT:16 * TT + 32], 0.0), writes=[("c", "r3pad")], cost=0.1)
        for t_, d_, nm in ((selT, sel_d, "sel"), (idf, ident_d, "idf"), (mask2, mask2_d, "mask2"), (rmask, rmask_d, "rmask"),
                           (gm, gm_d, "gm"), (gf, gf_d, "gf"), (cw, cw_d, "cw"), (lbl, lbl_d, "lbl"),
                           (ogg, og_d, "ogg")):
            dma("sp", t_[:, :], d_, "const", writes=[("c", nm)])
        S.op("dve", lambda e: e.memset(ones_b[:, :], 1.0), writes=[("c", "ones")])
        tt_op(lbm[:, :], lbl[:, 0:8], lbl[:, 8:16], ALU.subtract, [("c", "lbl")], [("c", "lbm")])
        act(lbm[:, :], lbm[:, :], AF.Sigmoid, [("c", "lbm")], [("c", "lbm")])
        ts_op(oml[:, :], lbm[:, :], -1.0, 1.0, ALU.mult, ALU.add, [("c", "lbm")], [("c", "oml")])

        sct = ar.alloc([128, 2048], F32)
        dma("sp", sct[0:16, :], sconv, "misc", writes=[("t", "sct")])
        dma("sp", o_cs[:, 0, :], sct[0:16, 1024:2048], "out", reads=[("t", "sct")])
        b0 = bank()
        def fn_sct(e):
            ins = None
            for j in range(16):
                ins = e.matmul(PS[b0][:, j * 16:(j + 1) * 16], lhsT=sct[0:16, j * 128:(j + 1) * 128],
                               rhs=idf[0:16, 0:16], start=True, stop=True)
            return ins
        S.op("pe", fn_sct, reads=[("t", "sct"), ("c", "idf")], writes=[("ps", b0)], cost=0.6)
        S.op("act", lambda e: e.copy(out=scT[:, :], in_=PS[b0][:, 0:256]), reads=[("ps", b0)], writes=[("c", "scT")])
        scTv = scT[:, :].rearrange("p (r c s) -> p r c s", r=2, c=8)

        fills = []

        def add_fill(fn):
            fills.append(fn)

        def fill_C(c):
            def f(s):
                for blk, base in enumerate((0, 1024, 2048)):
                    dma("pool", W[s][:, :, blk * 128:(blk + 1) * 128], wcols(w_in, base + c * 128, 128),
                        "w%db%d" % (s, blk), writes=[("w", s, blk)])
            return f

        def fill_H(h):
            def f(s):
                for blk, base in enumerate((3072, 4096, 5120, 6144)):
                    dma("pool", W[s][:, :, blk * 128:(blk + 1) * 128], wcols(w_in, base + h * 128, 128),
                        "w%db%d" % (s, blk), writes=[("w", s, blk)])
            return f

        def fill_J(j):
            def f(s):
                dma("pool", W[s][:, :, 0:128], wcols(w_in, 7168 + j * 128, 128), "w%db0" % s, writes=[("w", s, 0)])
                dma("pool", W[s][:, :, 128:256], wcols(w_in, 9216 + j * 128, 128), "w%db1" % s, writes=[("w", s, 1)])
                dma("pool", W[s][:, 0:8, 256:384], wcols(w_a, j * 128, 128), "w%db2" % s, writes=[("w", s, 2)])
                dma("pool", W[s][:, 0:8, 384:512], wcols(w_b, j * 128, 128), "w%db3" % s, writes=[("w", s, 3)])
            return f

        def fill_full(src, r0, c0):
            def f(s):
                dma("pool", W[s][:, :, :], src[r0:r0 + 2048, c0:c0 + 512].rearrange("(kc p) c -> p kc c", p=128),
                    "w%df" % s, writes=WK(s))
            return f

        def fill_P(h):
            def f(s):
                dma("pool", W[s][:, :, 128:256], wcols(w_in, 4096 + h * 128, 128), "w%db1" % s, writes=[("w", s, 1)])
                dma("pool", W[s][:, :, 256:384], wcols(w_in, 5120 + h * 128, 128), "w%db2" % s, writes=[("w", s, 2)])
            return f

        for h in range(8):
            add_fill(fill_P(h))
        for c in range(8):
            add_fill(fill_C(c))
            add_fill(fill_H(c))
        for j in range(16):
            add_fill(fill_J(j))
        for n in range(4):
            add_fill(fill_full(w_out, 0, n * 512))
        for g in range(4):
            for q4 in range(4):
                add_fill(fill_full(w_up, 0, g * 2048 + q4 * 512))
            for n in range(4):
                add_fill(fill_full(w_down, g * 2048, n * 512))
        fill_i = [0]

        fills[0](0)
        fills[1](1)

        def next_slot():
            k = fill_i[0]
            if k >= 1 and k + 1 < len(fills):
                fills[k + 1]((k + 1) % 2)
            fill_i[0] += 1
            return k % 2

        def norm_transpose(tiles, gvec, gname, xs_bufs, junk, dst_fn, tag):
            for (src, r, rkeys, i) in tiles:
                sl = i % 2
                ssv = ssb[:, sl * 2:sl * 2 + 2]
                kss = ("ss", sl)
                S.op("pool", lambda e, ssv=ssv: e.memset(ssv, 0.0), writes=[kss])
                act(junk[0:r, :], src, AF.Square, rkeys + [kss], [("t", "junk"), kss], accum_out=ssv[0:r, 0:1])
                act(ssv[0:r, 1:2], ssv[0:r, 0:1], AF.Ln, [kss], [kss], scale=1.0 / D, bias=EPS)
                act(ssv[0:r, 1:2], ssv[0:r, 1:2], AF.Exp, [kss], [kss], scale=-0.5)
                xs = xs_bufs[sl]
                kxs0 = ("t", tag + "xs", sl, 0)
                kxs1 = ("t", tag + "xs", sl, 1)
                act(xs[0:r, 0:1024], src[:, 0:1024], AF.Copy, rkeys + [kss], [kxs0], scale=ssv[0:r, 1:2])
                ts_op(xs[0:r, 1024:2048], src[:, 1024:2048], ssv[0:r, 1:2], None, ALU.mult, None, rkeys + [kss], [kxs1])
                for hh in range(2):
                    kxs = kxs0 if hh == 0 else kxs1
                    b = bank()
                    pb = psb(b)

                    def fn(e, pb=pb, xs=xs, r=r, hh=hh):
                        ins = None
                        for k in range(8):
                            kc = hh * 8 + k
                            ins = e.transpose(out=pb[:, k * 128:k * 128 + r], in_=xs[0:r, kc * 128:(kc + 1) * 128],
                                              identity=idb[0:r, 0:r])
                        return ins
                    S.op("pe", fn, reads=[kxs, ("c", "idb")], writes=[("ps", b)], cost=0.65)
                    pv = pb[:, 0:1024].rearrange("p (k t) -> p k t", k=8)[:, :, 0:r]
                    gb = gvec[:, hh * 8:(hh + 1) * 8].unsqueeze(2).to_broadcast([128, 8, r])
                    dst, dkeys = dst_fn(i, hh)
                    tt_op(dst, pv, gb, ALU.mult, [("ps", b), ("c", gname)], dkeys)

        def hx_keys(i0, i1):
            return [("hx", i, hh) for i in range(i0, i1) for hh in range(2)]
        hx_keys_early = hx_keys

        xt = [ar.alloc([128, D], F32) for _ in range(3)]
        xs0 = [ar.alloc([128, D], BF16), ar.alloc([128, D], BF16)]
        junk0 = ar.alloc([128, D], BF16)
        tiles0 = []
        for i in range(17):
            if i < 8:
                src, r = x_pre[i * 128:(i + 1) * 128, :], 128
            elif i < 16:
                src, r = x_main[(i - 8) * 128:(i - 7) * 128, :], 128
            else:
                src, r = x_smp, 16
            tiles0.append((i, src, r))

        def dst0(i, hh):
            if i < 8:
                return hxP[:, hh * 8:(hh + 1) * 8, i * 128:(i + 1) * 128], [("hx", i, hh)]
            if i < 16:
                return hxM[:, hh * 8:(hh + 1) * 8, (i - 8) * 128:(i - 7) * 128], [("hx", i, hh)]
            return hxM[:, hh * 8:(hh + 1) * 8, 1024:1040], [("hx", i, hh)]

        tl = []
        for (i, src, r) in tiles0:
            sl = i % 3
            dma("sp", xt[sl][0:r, :], src, "xt%d" % sl, writes=[("t", "xt", sl)])
            tl = [(xt[sl][0:r, :], r, [("t", "xt", sl)], i)]
            norm_transpose(tl, gm, "gm", xs0, junk0, dst0, "s0")
        cp_op(hxPt[:, :, :], hxP[:, :, 1008:1024], hx_keys_early(7, 8), [("hxPt",)])
        S.fence(sched=True)
        P_LO = len(S.ops)
        if stop == 0:
            S.emit(nc, es)
            return nc
        ar.reset()


        R2SP = (16 * TT + 16 * 1024) * 2
        S_init = view(R2, R2SP, [128, 8, 128], F32)
        ar.add_region(R2, R2SP + 4096, (18 * D * 2 - R2SP) - 4096)
        t_f = [ar.alloc([128, 512], F32) for _ in range(2)]
        t_q = [ar.alloc([128, 512], F32) for _ in range(2)]
        t_v = [ar.alloc([128, 512], BF16) for _ in range(2)]
        t_ke = [ar.alloc([128, 512], BF16) for _ in range(2)]
        t_lf = ar.alloc([128, 512], F32)
        t_b = ar.alloc([128, 512], F32)
        t_eb = ar.alloc([128, 512], F32)
        t_enb = ar.alloc([128, 512], F32)
        k_inT = ar.alloc([128, 1024], BF16)
        q_inT = ar.alloc([128, 1024], BF16)
        ke_tok = ar.alloc([128, 16, 128], BF16)
        v_tok = ar.alloc([128, 16, 128], BF16)
        sog = ar.alloc([128, TT], F32)
        S_pp = ar.alloc([128, 2, 128], F32)
        S_bf = ar.alloc([128, 16, 128], BF16)
        S0b = [ar.alloc([128, 8, 128], F32) for _ in range(2)]
        S_bfs = ar.alloc([128, 8, 128], BF16)
        vm = ar.alloc([128, 16, 128], BF16)
        fS = ar.alloc([128, 16], F32)
        fS2 = ar.alloc([128, 16], F32)
        sogS = [ar.alloc([128, 16], F32) for _ in range(2)]
        kS_b = ar.alloc([128, 16], BF16)
        vS_b = ar.alloc([128, 16], BF16)
        qS_b = ar.alloc([128, 16], BF16)
        ktok_s = ar.alloc([128, 128], BF16)
        vtok_s = ar.alloc([128, 128], BF16)
        scm = [ar.alloc([128, 128], BF16), ar.alloc([128, 128], BF16)]
        osq = ar.alloc([128, 512], BF16)
        rstd_t = ar.alloc([128, 512], F32)
        on_t = ar.alloc([128, 512], F32)
        K = lambda *n: ("t",) + n
        BOS, BO2 = 5, [6]
        nt_ctr = [0]

        NTS = [
            ("P", lambda kc: hxP[:, kc, 0:512], hx_keys(0, 4), 0),
            ("P", lambda kc: hxP[:, kc, 512:1024], hx_keys(4, 8), 4),
            ("M", lambda kc: hxM[:, kc, 0:512], hx_keys(8, 12), 8),
            ("M", lambda kc: hxM[:, kc, 512:1024], hx_keys(12, 16), 12),
        ]

        def head_ops(h, s):
            Wq = [W[s][:, kc, 0:128] for kc in range(16)]
            Wf = [W[s][:, kc, 128:256] for kc in range(16)]
            Wi = [W[s][:, kc, 256:384] for kc in range(16)]
            Wo = [W[s][:, kc, 384:512] for kc in range(16)]
            lb_h, oml_h = lbm[:, h:h + 1], oml[:, h:h + 1]
            st = {}

            HV = ((0, 256), (256, 512))

            def front(nt):
                kind, hsrc, hk, tb = NTS[nt]
                p = nt_ctr[0] % 2
                nt_ctr[0] += 1
                st[nt] = p
                mcol = (tb - 8) * 128
                tf, tq, tv = t_f[p], t_q[p], t_v[p]
                bf_ = bank()
                mm_group(PS[bf_][:, :], [(Wf[kc], hsrc(kc)) for kc in range(16)], [("w", s, 1)] + hk, [("ps", bf_)])
                bi = bank()
                mm_group(PS[bi][:, :], [(Wi[kc], hsrc(kc)) for kc in range(16)], [("w", s, 2)] + hk, [("ps", bi)])
                for hv, (a, b_) in enumerate(HV):
                    act(tf[:, a:b_], PS[bf_][:, a:b_], AF.Sigmoid, [("ps", bf_)], [K("f", p, hv)])
                act(tv[:, :], PS[bi][:, :], AF.Copy, [("ps", bi)], [K("v", p)])
                if kind == "M":
                    bq = bank()
                    mm_group(PS[bq][:, :], [(Wq[kc], hsrc(kc)) for kc in range(16)], [("w", s, 0)] + hk, [("ps", bq)])
                    bo = bank()
                    mm_group(PS[bo][:, :], [(Wo[kc], hsrc(kc)) for kc in range(16)], [("w", s, 3)] + hk, [("ps", bo)])
                    act(tq[:, :], PS[bq][:, :], AF.Sigmoid, [("ps", bq)], [K("q", p)])
                    act(sog[:, mcol:mcol + 512], PS[bo][:, :], AF.Sigmoid, [("ps", bo)], [K("sog", nt)])
                    tt_op(tq[:, :], tq[:, :], PS[bq][:, :], ALU.mult, [K("q", p), ("ps", bq)], [K("q", p)])
                    tt_op(sog[:, mcol:mcol + 512], sog[:, mcol:mcol + 512], PS[bo][:, :], ALU.mult,
                          [K("sog", nt), ("ps", bo)], [K("sog", nt)])

            def chain(nt):
                kind, hsrc, hk, tb = NTS[nt]
                p = st[nt]
                mcol = (tb - 8) * 128
                tf, tq, tke = t_f[p], t_q[p], t_ke[p]
                kq = K("q", p)
                n0 = tb * 2
                c3 = lambda ap: ap.rearrange("p (c l) -> p c l", l=64)
                steps = []
                for hv, (a, b_) in enumerate(HV):
                    kf, klf, kb, keb, kenb = K("f", p, hv), K("lf", hv), K("b", hv), K("eb", hv), K("enb", hv)
                    kke = K("ke", p) if False else K("ke", p, hv)
                    sl = slice(a, b_)
                    ebv = c3(t_eb[:, sl])
                    ops = []
                    ops.append(lambda kf=kf, sl=sl: ts_op(tf[:, sl], tf[:, sl], oml_h, lb_h, ALU.mult, ALU.add,
                                                          [kf, ("c", "oml"), ("c", "lbm")], [kf]))
                    ops.append(lambda kf=kf, klf=klf, sl=sl: act(t_lf[:, sl], tf[:, sl], AF.Ln, [kf], [klf]))
                    ops.append(lambda klf=klf, kb=kb, sl=sl: S.op(
                        "dve", lambda e: e.tensor_tensor_scan(out=t_b[:, sl], data0=rmask[:, sl], data1=t_lf[:, sl],
                                                              initial=0.0, op0=ALU.mult, op1=ALU.add),
                        reads=[klf, ("c", "rmask")], writes=[kb], cost=0.7))
                    ops.append(lambda kb=kb, keb=keb, sl=sl: act(t_eb[:, sl], t_b[:, sl], AF.Exp, [kb], [keb]))
                    ops.append(lambda kb=kb, kenb=kenb, sl=sl: act(t_enb[:, sl], t_b[:, sl], AF.Exp, [kb], [kenb], scale=-1.0))
                    ops.append(lambda kf=kf, sl=sl: ts_op(tf[:, sl], tf[:, sl], -1.0, 1.0, ALU.mult, ALU.add, [kf], [kf],
                                                          eng="pool"))
                    ops.append(lambda keb=keb, ebv=ebv, hv=hv: act(dec[:, n0 + 4 * hv:n0 + 4 * hv + 4], ebv[:, :, 63], AF.Copy,
                                                                   [keb], [K("dec", nt)]))
                    if kind == "M":
                        ops.append(lambda kf=kf, kenb=kenb, sl=sl, a=a, b_=b_: tt_op(
                            k_inT[:, mcol + a:mcol + b_], tf[:, sl], t_enb[:, sl], ALU.mult, [kf, kenb], [K("kin", nt)]))
                        ops.append(lambda keb=keb, sl=sl, a=a, b_=b_: tt_op(
                            q_inT[:, mcol + a:mcol + b_], tq[:, sl], t_eb[:, sl], ALU.mult, [kq, keb], [K("qin", nt)]))
                    ops.append(lambda kenb=kenb, keb=keb, klf=klf, sl=sl, ebv=ebv: tt_op(
                        c3(t_lf[:, sl]), c3(t_enb[:, sl]), ebv[:, :, 63:64].to_broadcast([128, 4, 64]), ALU.mult,
                        [kenb, keb, klf], [klf]))
                    ops.append(lambda klf=klf, kf=kf, sl=sl: tt_op(tke[:, sl], t_lf[:, sl], tf[:, sl], ALU.mult,
                                                                   [klf, kf], [K("ke", p)]))
                    steps.append(ops)
                for i in range(len(steps[0])):
                    steps[0][i]()
                    steps[1][i]()

            def trans(nt):
                kind, hsrc, hk, tb = NTS[nt]
                p = st[nt]
                for (src, ksrc, dst, kdst, use_act) in ((t_v[p], K("v", p), v_tok, K("vtok", nt), False),
                                                        (t_ke[p], K("ke", p), ke_tok, K("ketok", nt), True)):
                    bt = bank()
                    pbt = psb(bt)

                    def fn_t(e, pbt=pbt, src=src):
                        ins = None
                        for k in range(4):
                            ins = e.transpose(out=pbt[:, k * 128:(k + 1) * 128], in_=src[:, k * 128:(k + 1) * 128],
                                              identity=idb[:, :])
                        return ins
                    S.op("pe", fn_t, reads=[ksrc, ("c", "idb")], writes=[("ps", bt)], cost=0.35)
                    pv = pbt[:, 0:512].rearrange("p (k t) -> p k t", k=4)
                    if use_act:
                        act(dst[:, tb:tb + 4, :], pv, AF.Copy, [("ps", bt)], [kdst])
                    else:
                        cp_op(dst[:, tb:tb + 4, :], pv, [("ps", bt)], [kdst])

            def scan(part):
                if part == 0:
                    S.op("dve", lambda e: e.memset(S_pp[:, 0, :], 0.0), writes=[K("Spp", 0)])
                else:
                    act(S_pp[:, 0, :], S_init[:, h, :], AF.Copy, [("Sinit", h)], [K("Spp", 0)])
                for grp in (part * 2, part * 2 + 1):
                    bdp = [bank(), bank()]

                    def fn_d(e, grp=grp, bdp=bdp):
                        ins = None
                        for j in range(4):
                            for a_ in range(2):
                                tt_ = grp * 4 + j
                                r0 = a_ * 64
                                ins = e.matmul(PS[bdp[a_]][:, j * 128:(j + 1) * 128], lhsT=ke_tok[r0:r0 + 64, tt_, :],
                                               rhs=v_tok[r0:r0 + 64, tt_, :], start=True, stop=True)
                        return ins
                    S.op("pe", fn_d, reads=[K("ketok", grp), K("vtok", grp)], writes=[("ps", bdp[0]), ("ps", bdp[1])], cost=0.6)
                    for j in range(4):
                        for a_ in range(2):
                            n = grp * 8 + j * 2 + a_
                            cur, nxt = n % 2, (n + 1) % 2
                            if n >= 16:
                                act(S_bf[:, n - 16, :], S_pp[:, cur, :], AF.Copy, [K("Spp", cur)], [K("Sbf", (n - 16) // 2)])
                            stt_op(S_pp[:, nxt, :], S_pp[:, cur, :], dec[:, n:n + 1],
                                   PS[bdp[a_]][:, j * 128:(j + 1) * 128], ALU.mult, ALU.add,
                                   [K("Spp", cur), K("dec", grp), ("ps", bdp[a_])], [K("Spp", nxt)])
                if part == 1:
                    dma("sp", o_hp[h, :, :], S_pp[:, 0, :], "ohp", reads=[K("Spp", 0)])

            def o_main(half):
                pend = None
                for tt_ in range(half * 4, half * 4 + 4):
                    nt = 2 + tt_ // 4
                    bsc = bank()
                    mm_group(PS[bsc][:, 0:128],
                             [(k_inT[:, tt_ * 128:(tt_ + 1) * 128], q_inT[:, tt_ * 128:(tt_ + 1) * 128])],
                             [K("kin", nt), K("qin", nt)], [("ps", bsc)])
                    sc_ = scm[tt_ % 2]
                    ksc = K("scm", tt_ % 2)
                    tt_op(sc_[:, :], PS[bsc][:, 0:128], mask2[:, :], ALU.mult, [("ps", bsc), ("c", "mask2")], [ksc])
                    pob = PS[BO2[0]]
                    c0 = (tt_ % 4) * 128

                    def fn_o(e, tt_=tt_, pob=pob, c0=c0, sc_=sc_):
                        ins = None
                        for a_ in range(2):
                            oc = c0 + 64 * a_
                            e.matmul(pob[:, oc:oc + 64], lhsT=S_bf[:, 2 * tt_ + a_, :],
                                     rhs=q_inT[:, tt_ * 128 + 64 * a_:tt_ * 128 + 64 * a_ + 64], start=True, stop=False)
                            ins = e.matmul(pob[:, oc:oc + 64], lhsT=v_tok[:, 8 + tt_, :], rhs=sc_[:, 64 * a_:64 * a_ + 64],
                                           start=False, stop=True)
                        return ins

                    def emit_o(fn_o=fn_o, tt_=tt_, nt=nt, ksc=ksc):
                        S.op("pe", fn_o, reads=[K("vtok", nt), ksc, K("Sbf", tt_), K("qin", nt)]
                             + ([("ps", BO2[0])] if tt_ % 4 else []), writes=[("ps", BO2[0])])
                    if pend is not None:
                        pend()
                    pend = emit_o
                pend()

            def norm(piece):
                for (pb_, ncol, dcol, ksog) in (((BO2[0], 512, 0, K("sog", 2)), (BO2[0], 512, 512, K("sog", 3)),
                                                 (BOS, 16, 1024, ksgS))[piece],):
                    if pb_ == BOS:
                        po_ap = PS[pb_][:, 0:272].rearrange("p (a b) -> p a b", b=17)[:, :, 0]
                    else:
                        po_ap = PS[pb_][:, 0:ncol]
                    act(osq[:, 0:ncol], po_ap, AF.Square, [("ps", pb_)], [K("osq")])
                    bss = bank()
                    mm_group(PS[bss][:, 0:ncol], [(ones_b[:, :], osq[:, 0:ncol])], [K("osq"), ("c", "ones")], [("ps", bss)])
                    act(rstd_t[:, 0:ncol], PS[bss][:, 0:ncol], AF.Ln, [("ps", bss)], [K("rstd")], scale=1.0 / 128, bias=EPS)
                    act(rstd_t[:, 0:ncol], rstd_t[:, 0:ncol], AF.Exp, [K("rstd")], [K("rstd")], scale=-0.5)
                    tt_op(on_t[:, 0:ncol], po_ap, rstd_t[:, 0:ncol], ALU.mult, [("ps", pb_), K("rstd")], [K("on")])
                    sg_ap = sgS[:, :] if pb_ == BOS else sog[:, dcol:dcol + ncol]
                    stt_op(oT[:, h, dcol:dcol + ncol], on_t[:, 0:ncol], ogg[:, 0:1], sg_ap,
                           ALU.mult, ALU.mult, [K("on"), ksog, ("c", "ogg")], [("oT",)])

            sgS = sogS[h % 2]
            ksgS = K("sogS", h % 2)
            sv = {}

            def s1():
                bs = bank()

                def fn_s(e, bs=bs):
                    ins = None
                    for blk, Wx in enumerate((Wq, Wf, Wi, Wo)):
                        for kc in range(16):
                            ins = e.matmul(PS[bs][:, blk * 16:(blk + 1) * 16], lhsT=Wx[kc], rhs=hxM[:, kc, 1024:1040],
                                           start=(kc == 0), stop=(kc == 15))
                    return ins
                S.op("pe", fn_s, reads=WK(s) + hx_keys(16, 17), writes=[("ps", bs)], cost=2.0)
                act(fS[:, :], PS[bs][:, 16:32], AF.Sigmoid, [("ps", bs)], [K("fS")])
                ts_op(fS[:, :], fS[:, :], oml_h, lb_h, ALU.mult, ALU.add, [K("fS"), ("c", "oml"), ("c", "lbm")], [K("fS")])
                ts_op(kS_b[:, :], fS[:, :], -1.0, 1.0, ALU.mult, ALU.add, [K("fS")], [K("kS")])
                act(vS_b[:, :], PS[bs][:, 32:48], AF.Copy, [("ps", bs)], [K("vS")])
                act(fS2[:, :], PS[bs][:, 0:16], AF.Sigmoid, [("ps", bs)], [K("fS2")])
                tt_op(qS_b[:, :], fS2[:, :], PS[bs][:, 0:16], ALU.mult, [K("fS2"), ("ps", bs)], [K("qS")])
                act(sgS[:, :], PS[bs][:, 48:64], AF.Sigmoid, [("ps", bs)], [ksgS])
                tt_op(sgS[:, :], sgS[:, :], PS[bs][:, 48:64], ALU.mult, [ksgS, ("ps", bs)], [ksgS])

            def s2():
                bts = bank()
                pbts = psb(bts)

                def fn_ts(e, pbts=pbts):
                    e.transpose(out=pbts[0:16, 0:128], in_=kS_b[:, :], identity=idb[:, :])
                    return e.transpose(out=pbts[0:16, 128:256], in_=vS_b[:, :], identity=idb[:, :])
                S.op("pe", fn_ts, reads=[K("kS"), K("vS"), ("c", "idb")], writes=[("ps", bts)])
                act(ktok_s[0:16, :], pbts[0:16, 0:128], AF.Copy, [("ps", bts)], [K("ktoks")])
                act(vtok_s[0:16, :], pbts[0:16, 128:256], AF.Copy, [("ps", bts)], [K("vtoks")])
                tt_op(vm[0:16, :, :], vtok_s[0:16, :].unsqueeze(1).to_broadcast([16, 16, 128]),
                      idb[0:16, 0:16].unsqueeze(2).to_broadcast([16, 16, 128]), ALU.mult,
                      [K("vtoks"), ("c", "idb")], [K("vm")])

            def s3():
                for hf in range(2):
                    S0 = S0b[hf]
                    kS0 = K("S0", hf)
                    bd = [bank(), bank()]
                    for q_ in range(2):
                        mm_group(PS[bd[q_]][:, :],
                                 [(ktok_s[0:16, :], vm[0:16, hf * 8 + q_ * 4:hf * 8 + q_ * 4 + 4, :])],
                                 [K("ktoks"), K("vm")], [("ps", bd[q_])])
                    tt_op(S0[:, :, :], S0[:, :, :], fS[:, hf * 8:(hf + 1) * 8].unsqueeze(2).to_broadcast([128, 8, 128]),
                          ALU.mult, [kS0, K("fS")], [kS0])
                    for q_ in range(2):
                        tt_op(S0[:, q_ * 4:(q_ + 1) * 4, :], S0[:, q_ * 4:(q_ + 1) * 4, :],
                              PS[bd[q_]][:, :].rearrange("p (b e) -> p b e", b=4), ALU.add,
                              [kS0, ("ps", bd[q_])], [kS0])
                    dma("sp", o_hs[hf * 8:(hf + 1) * 8, h, :, :].rearrange("b d e -> d b e"), S0[:, :, :],
                        "s0out%d" % hf, reads=[kS0])

            def s4():
                for hf in range(2):
                    S0 = S0b[hf]
                    kS0 = K("S0", hf)
                    act(S_bfs[:, :, :], S0[:, :, :], AF.Copy, [kS0], [K("Sbfs")])

                    def fn_os(e, hf=hf):
                        ins = None
                        for b_ in range(8):
                            col = hf * 8 + b_
                            ins = e.matmul(PS[BOS][:, col * 16:(col + 1) * 16], lhsT=S_bfs[:, b_, :], rhs=qS_b[:, 0:16],
                                           start=True, stop=True)
                        return ins
                    S.op("pe", fn_os, reads=[K("Sbfs"), K("qS")] + ([("ps", BOS)] if hf else []), writes=[("ps", BOS)])

            def prefetch():
                for hf in range(2):
                    dma("sp", S0b[hf][:, :, :], shgrn[hf * 8:(hf + 1) * 8, h, :, :].rearrange("b d e -> d b e"),
                        "s0ld%d" % hf, writes=[K("S0", hf)])

            def save_init():
                act(S_init[:, h, :], S_pp[:, 0, :], AF.Copy, [K("Spp", 0)], [("Sinit", h)])

            return dict(front=front, chain=chain, trans=trans, scan=scan, o_main=o_main, norm=norm, save_init=save_init,
                        s1=s1, s2=s2, s3=s3, s4=s4, prefetch=prefetch)


        bank_set[0] = list(range(8))
        prevP = None
        for h in range(8):
            s = next_slot()
            Hp = head_ops(h, s)
            Hp["front"](0)
            Hp["front"](1)
            if prevP is not None:
                prevP["scan"](0)
                prevP["save_init"]()
            Hp["chain"](0)
            Hp["trans"](0)
            Hp["chain"](1)
            Hp["trans"](1)
            prevP = Hp
        prevP["scan"](0)
        prevP["save_init"]()
        S.barrier_op(P_LO, len(S.ops), ("bar", "P"))
        bank_set[0] = [0, 1, 2, 3, 4, 7]
        arC = Arena([])
        arC.add_region(R2, 16 * TT * 2, 16 * 1024 * 2)
        cbuf = [[arC.alloc([128, 1042], F32) for _ in range(3)] for _ in range(2)]
        accs_b = [arC.alloc([128, 16], F32) for _ in range(2)]
        uov = uo[:, :].rearrange("p (c t) -> p c t", c=8)
        cwv = cw[:, :].rearrange("p (c k) -> p c k", c=8)

        def C_blocks(c, s):
            par = c % 2
            hcs, ubuf, accb = cbuf[par]
            kh, ku, ka = ("t", "hcs", par), ("t", "ubuf", par), ("t", "acc", par)
            accs = accs_b[par]
            kas = ("t", "accs", par)

            def cgroups(blk):
                Wl = [W[s][:, kc, blk * 128:(blk + 1) * 128] for kc in range(16)]
                g0, g1, g2 = bank(), bank(), bank()
                mm_group(PS[g0][:, 0:352], [(Wl[kc], hxM[:, kc, 0:352]) for kc in range(16)],
                         [("w", s, blk)] + hx_keys(8, 11), [("ps", g0)])
                mm_group(PS[g1][:, 0:352], [(Wl[kc], hxM[:, kc, 352:704]) for kc in range(16)],
                         [("w", s, blk)] + hx_keys(10, 14), [("ps", g1)])

                def fn2(e, Wl=Wl, g2=g2):
                    ins = None
                    for kc in range(16):
                        ins = e.matmul(PS[g2][:, 0:336], lhsT=Wl[kc], rhs=hxM[:, kc, 704:1040],
                                       start=(kc == 0), stop=(kc == 15))
                    for kc in range(16):
                        ins = e.matmul(PS[g2][:, 336:352], lhsT=Wl[kc], rhs=hxPt[:, kc, :],
                                       start=(kc == 0), stop=(kc == 15))
                    return ins
                S.op("pe", fn2, reads=[("w", s, blk), ("hxPt",)] + hx_keys(13, 17) + XR, writes=[("ps", g2)],
                     cost=16 * (336 / 2400.0 + 0.01) + 16 * 0.037)
                return g0, g1, g2

            def pieces(g0, g1, g2):
                return ((PS[g0][:, 0:352], slice(18, 370), g0), (PS[g1][:, 0:352], slice(370, 722), g1),
                        (PS[g2][:, 0:320], slice(722, 1042), g2), (PS[g2][:, 320:336], slice(0, 16), g2),
                        (PS[g2][:, 350:352], slice(16, 18), g2))

            def b0():
                for (src, dsl, g) in pieces(*cgroups(0)):
                    act(hcs[:, dsl], src, AF.Copy, [("ps", g)], [kh])

            def b1():
                for (src, dsl, g) in pieces(*cgroups(2)):
                    tt_op(ubuf[:, dsl], src, hcs[:, dsl], ALU.mult, [("ps", g), kh], [ku])
                ts_op(accb[:, 0:1024], ubuf[:, 16:1040], cwv[:, c, 0:1], None, ALU.mult, None, [ku, ("c", "cw")], [ka])
                stt_op(accb[:, 0:1024], ubuf[:, 17:1041], cwv[:, c, 1:2], accb[:, 0:1024], ALU.mult, ALU.add,
                       [ku, ka, ("c", "cw")], [ka])
                stt_op(accb[:, 0:1024], ubuf[:, 18:1042], cwv[:, c, 2:3], accb[:, 0:1024], ALU.mult, ALU.add,
                       [ku, ka, ("c", "cw")], [ka])
                ts_op(accs[:, :], scTv[:, 0, c, :], cwv[:, c, 0:1], None, ALU.mult, None,
                      [("c", "scT"), ("c", "cw")], [kas])
                stt_op(accs[:, :], scTv[:, 1, c, :], cwv[:, c, 1:2], accs[:, :], ALU.mult, ALU.add,
                       [("c", "scT"), kas, ("c", "cw")], [kas])
                stt_op(accs[:, :], ubuf[:, 0:16], cwv[:, c, 2:3], accs[:, :], ALU.mult, ALU.add,
                       [ku, kas, ("c", "cw")], [kas])
                act(uov[:, c, 0:2], ubuf[:, 1040:1042], AF.Copy, [ku], [("c", "uo")])
                act(uov[:, c, 2:18], ubuf[:, 0:16], AF.Copy, [ku], [("c", "uo")])

            def b2():
                g0, g1, g2 = cgroups(1)
                tt_op(aT[:, c, 0:352], PS[g0][:, 0:352], accb[:, 0:352], ALU.mult, [("ps", g0), ka], [("aT",)])
                tt_op(aT[:, c, 352:704], PS[g1][:, 0:352], accb[:, 352:704], ALU.mult, [("ps", g1), ka], [("aT",)])
                tt_op(aT[:, c, 704:1024], PS[g2][:, 0:320], accb[:, 704:1024], ALU.mult, [("ps", g2), ka], [("aT",)])
                tt_op(aT[:, c, 1024:1040], PS[g2][:, 320:336], accs[:, :], ALU.mult, [("ps", g2), kas], [("aT",)])

            return b0, b1, b2

        assert fill_i[0] == 8
        fills[9](1)
        prev = None
        for h in range(8):
            cb0, cb1, cb2 = C_blocks(h, 0)
            s = 1
            H_ = head_ops(h, s)
            H_["prefetch"]()
            if prev is not None:
                prev["scan"](1)
                prev["o_main"](0)
                prev["norm"](0)
                prev["o_main"](1)
                prev["norm"](1)
                prev["norm"](2)
            H_["front"](2)
            H_["s1"]()
            XR.append(("bar", "P")); cb0(); XR.pop()
            H_["chain"](2)
            H_["front"](3)
            if h < 7:
                fills[9 + 2 * (h + 1)](1)
            H_["trans"](2)
            H_["s2"]()
            XR.append(("bar", "P")); cb1(); XR.pop()
            H_["chain"](3)
            H_["s3"]()
            H_["trans"](3)
            XR.append(("bar", "P")); cb2(); XR.pop()
            fills[8 + 2 * (h + 1)](0)
            H_["s4"]()
            prev = H_
        fill_i[0] = 24
        prev["scan"](1)
        prev["o_main"](0)
        prev["norm"](0)
        prev["o_main"](1)
        prev["norm"](1)
        prev["norm"](2)
        b1, b2 = bank(), bank()
        for half, bb in ((0, b1), (1, b2)):
            def fn_u(e, half=half, bb=bb):
                ins = None
                for k in range(4):
                    c = half * 4 + k
                    ins = e.matmul(PS[bb][0:18, k * 128:(k + 1) * 128], lhsT=uov[:, c, :], rhs=idf[:, :],
                                   start=True, stop=True)
                return ins
            S.op("pe", fn_u, reads=[("c", "uo"), ("c", "idf")], writes=[("ps", bb)])
        S.mark(sched=True)
        S.barrier_op(P_LO, len(S.ops), ("bar", "H"))
        XR.append(("bar", "H"))
        uot = view(EA, 26624, [128, 1024], F32)
        act(uot[0:18, 0:512], PS[b1][0:18, :], AF.Copy, [("ps", b1)], [("t", "uot")])
        act(uot[0:18, 512:1024], PS[b2][0:18, :], AF.Copy, [("ps", b2)], [("t", "uot")])
        dma("sp", o_cp, uot[0:2, :], "out", reads=[("t", "uot")])
        dma("sp", o_cs[:, 1, :], uot[2:18, :], "out", reads=[("t", "uot")])
        bank_set[0] = list(range(8))
        ar.reset()

        t1 = [ar.alloc([128, 512], F32) for _ in range(2)]
        t2 = [ar.alloc([128, 512], F32) for _ in range(2)]
        ar.cur[0] = 16 * TT * 2
        ar.cur[1] = 0
        t1 = [ar.alloc([128, 512], F32) for _ in range(2)]
        t2 = [ar.alloc([128, 512], F32) for _ in range(2)]
        it = 0
        for j in range(16):
            s = next_slot()
            for (c0, ncol, hk) in ((0, 352, hx_keys(8, 11)), (352, 352, hx_keys(10, 14)), (704, 336, hx_keys(13, 17))):
                p_ = it % 2
                it += 1
                bga, bgb, bA, bB = bank(), bank(), bank(), bank()
                mm_group(PS[bga][:, 0:ncol], [(W[s][:, kc, 0:128], hxM[:, kc, c0:c0 + ncol]) for kc in range(16)],
                         [("w", s, 0)] + hk, [("ps", bga)])
                mm_group(PS[bgb][:, 0:ncol], [(W[s][:, kc, 128:256], hxM[:, kc, c0:c0 + ncol]) for kc in range(16)],
                         [("w", s, 1)] + hk, [("ps", bgb)])
                mm_group(PS[bA][:, 0:ncol], [(W[s][:, kc, 256:384], aT[:, kc, c0:c0 + ncol]) for kc in range(8)],
                         [("w", s, 2), ("aT",)], [("ps", bA)])
                mm_group(PS[bB][:, 0:ncol], [(W[s][:, kc, 384:512], oT[:, kc, c0:c0 + ncol]) for kc in range(8)],
                         [("w", s, 3), ("oT",)], [("ps", bB)])
                k1, k2 = ("t", "t1", p_), ("t", "t2", p_)
                act(t1[p_][:, 0:ncol], PS[bga][:, 0:ncol], AF.Sigmoid, [("ps", bga)], [k1])
                act(t2[p_][:, 0:ncol], PS[bgb][:, 0:ncol], AF.Sigmoid, [("ps", bgb)], [k2])
                tt_op(t1[p_][:, 0:ncol], t1[p_][:, 0:ncol], PS[bA][:, 0:ncol], ALU.mult, [k1, ("ps", bA)], [k1])
                tt_op(t2[p_][:, 0:ncol], t2[p_][:, 0:ncol], PS[bB][:, 0:ncol], ALU.mult, [k2, ("ps", bB)], [k2])
                tt_op(mixT[:, j, c0:c0 + ncol], t1[p_][:, 0:ncol], t2[p_][:, 0:ncol], ALU.add, [k1, k2], [("mix",)])
        if stop == 3:
            S.fence()
            S.emit(nc, es)
            return nc


        def smp_group(b, s, src_keys):
            def fn(e):
                ins = None
                for r_ in range(4):
                    for j in range(4):
                        kc = r_ * 4 + j
                        ins = e.matmul(PS[b][32 * j:32 * j + 32, :], lhsT=R3[:, kc * TT + 1024:kc * TT + 1056],
                                       rhs=W[s][:, kc, :], start=(r_ == 0), stop=(r_ == 3),
                                       tile_position=(0, 32 * j))
                return ins
            S.op("pe", fn, reads=WK(s) + list(src_keys) + [("c", "r3pad")] + XR, writes=[("ps", b)], cost=1.1)

        def smp_reduce(zero_rest):
            for n in range(4):
                b = bank()
                cols = slice(n * 512, (n + 1) * 512)
                S.op("pe", lambda e, b=b, cols=cols: e.matmul(PS[b][0:16, :], lhsT=selT[:, :], rhs=x1v[:, 8, cols],
                                                              start=True, stop=True),
                     reads=[("x1", 8, n), ("c", "sel")] + XR, writes=[("ps", b)], cost=0.9)
                cp_op(x1v[0:16, 8, cols], PS[b][0:16, :], [("ps", b)], [("x1", 8, n)])
            if zero_rest:
                S.op("dve", lambda e: e.memset(x1v[32:64, 8, :], 0.0), reads=list(XR),
                     writes=[("x1", 8, n) for n in range(4)], cost=2.2)
                S.op("dve", lambda e: e.memset(x1v[64:128, 8, :], 0.0), reads=list(XR),
                     writes=[("x1", 8, n) for n in range(4)], cost=2.2)

        xsrc = [(x_main[t * 128:(t + 1) * 128, :], 128) for t in range(8)] + [(x_smp, 16)]
        S.op("dve", lambda e: e.memset(x1v[:, 8, :], 0.0), reads=list(XR),
             writes=[("x1", 8, n) for n in range(4)] + [("aT",), ("oT",)], cost=2.2)
        for t, (src, r) in enumerate(xsrc):
            dma("sp", x1v[0:r, t, :], src, "x1ld", writes=[("x1", t, n) for n in range(4)] + [("aT",), ("oT",)])
        for n in range(4):
            s = next_slot()
            for t, (src, r) in enumerate(xsrc):
                c0 = t * 128
                b = bank()
                if t == 8:
                    smp_group(b, s, [("mix",)])
                    tt_op(x1v[:, 8, n * 512:(n + 1) * 512], x1v[:, 8, n * 512:(n + 1) * 512], PS[b][:, :], ALU.add,
                          [("ps", b), ("x1", t, n)], [("x1", t, n)])
                    continue
                mm_group(PS[b][0:r, :], [(mixT[:, kc, c0:c0 + r], W[s][:, kc, :]) for kc in range(16)],
                         WK(s) + [("mix",)], [("ps", b)])
                tt_op(x1v[0:r, t, n * 512:(n + 1) * 512], x1v[0:r, t, n * 512:(n + 1) * 512], PS[b][0:r, :], ALU.add,
                      [("ps", b), ("x1", t, n)], [("x1", t, n)])
        smp_reduce(True)
        if stop == 4:
            S.fence()
            S.emit(nc, es)
            return nc

        xs4 = [ar.alloc([128, D], BF16), ar.alloc([128, D], BF16)]
        junk4 = ar.alloc([128, D], BF16)

        def dst4(i, hh):
            r = 128 if i < 8 else 16
            return (h2T[:, hh * 8:(hh + 1) * 8, i * 128:i * 128 + r],
                    [("h2", i, hh), ("hx", (8 + i) if i < 8 else 16, hh)])

        tiles4 = [(x1v[0:(128 if t < 8 else 16), t, :], 128 if t < 8 else 16, [("x1", t, n) for n in range(4)], t)
                  for t in range(9)]
        norm_transpose(tiles4, gf, "gf", xs4, junk4, dst4, "s4")
        if stop == 5:
            S.fence()
            S.emit(nc, es)
            return nc

        tr_ = [ar.alloc([128, 512], F32) for _ in range(3)]
        h2_keys = lambda t0, t1_: [("h2", i, hh) for i in range(t0, t1_) for hh in range(2)]
        it = 0
        for g in range(4):
            for q4 in range(4):
                s = next_slot()
                for fc in range(4):
                    fidx = q4 * 4 + fc
                    for (c0, ncol, hk) in ((0, 352, h2_keys(0, 3)), (352, 352, h2_keys(2, 6)), (704, 336, h2_keys(5, 9))):
                        b = bank()
                        mm_group(PS[b][:, 0:ncol],
                                 [(W[s][:, kc, fc * 128:(fc + 1) * 128], h2T[:, kc, c0:c0 + ncol]) for kc in range(16)],
                                 WK(s) + hk, [("ps", b)])
                        p_ = it % 3
                        it += 1
                        kt = ("t", "relu", p_)
                        act(tr_[p_][:, 0:ncol], PS[b][:, 0:ncol], AF.Relu, [("ps", b)], [kt])
                        tt_op(actT[:, fidx, c0:c0 + ncol], tr_[p_][:, 0:ncol], tr_[p_][:, 0:ncol], ALU.mult,
                              [kt], [("actT",)] + ([("mix",)] if g == 0 else []))
            for n in range(4):
                s = next_slot()
                for t, (src, r) in enumerate(xsrc):
                    c0 = t * 128
                    b = bank()
                    if t == 8:
                        smp_group(b, s, [("actT",)])
                        tt_op(x1v[:, 8, n * 512:(n + 1) * 512], x1v[:, 8, n * 512:(n + 1) * 512], PS[b][:, :],
                              ALU.add, [("ps", b), ("x1", t, n)], [("x1", t, n)])
                        continue
                    mm_group(PS[b][0:r, :], [(actT[:, fc, c0:c0 + r], W[s][:, fc, :]) for fc in range(16)],
                             WK(s) + [("actT",)], [("ps", b)])
                    tt_op(x1v[0:r, t, n * 512:(n + 1) * 512], x1v[0:r, t, n * 512:(n + 1) * 512], PS[b][0:r, :],
                          ALU.add, [("ps", b), ("x1", t, n)], [("x1", t, n)])

        smp_reduce(False)
        h2_all = [("h2", i, hh) for i in range(9) for hh in range(2)]
        yt = [view(R1, 0, [128, D], F32), view(R1, D * 4, [128, D], F32)]
        junk6 = view(R1, D * 8, [128, D], BF16)
        gfin = view(R1, D * 8 + D * 2, [128, D], F32)
        dma("sp", gfin, nfin_d, "misc", writes=[("c", "gfin")] + h2_all)
        for t, (src, r) in enumerate(xsrc):
            sl = t % 2
            ssv = ssb[:, sl * 2:sl * 2 + 2]
            kss = ("ss", sl)
            xk = [("x1", t, n) for n in range(4)]
            S.op("pool", lambda e, ssv=ssv: e.memset(ssv, 0.0), writes=[kss])
            act(junk6[0:r, :], x1v[0:r, t, :], AF.Square, xk + [kss, ("c", "gfin")], [("t", "junk6"), kss],
                accum_out=ssv[0:r, 0:1])
            act(ssv[0:r, 1:2], ssv[0:r, 0:1], AF.Ln, [kss], [kss], scale=1.0 / D, bias=EPS)
            act(ssv[0:r, 1:2], ssv[0:r, 1:2], AF.Exp, [kss], [kss], scale=-0.5)
            ky = ("t", "yt", sl)
            stt_op(yt[sl][0:r, :], x1v[0:r, t, :], ssv[0:r, 1:2], gfin[0:r, :], ALU.mult, ALU.mult,
                   xk + [kss, ("c", "gfin")], [ky])
            dst = y_main[t * 128:(t + 1) * 128, :] if t < 8 else y_smp
            dma("sp", dst, yt[sl][0:r, :], "yout%d" % sl, reads=[ky])

        if os.environ.get('DBG_MEM'):
            print('SBUF remaining', nc.sbuf_bytes_remaining)
        S.emit(nc, es)
    return nc


_CACHE = {}


def _consts():
    ident = np.eye(128, dtype=np.float32)
    s_idx = np.arange(128)[:, None]
    l_idx = np.arange(128)[None, :]
    mask2 = ((s_idx // 64 == l_idx // 64) & (l_idx >= s_idx)).astype(np.float32)
    rmask = np.ones((128, 512), np.float32)
    rmask[:, ::64] = 0.0
    sel = np.zeros((128, 16), np.float32)
    for p in range(128):
        if p % 32 < 16:
            sel[p, p % 32] = 1.0
    return ident, mask2, rmask, sel


def kernel(x_prompt, x_sample, state_conv, state_hgrn, norm_mix, w_in, conv_w, lb_logits, onorm_g,
           w_branch_a, w_branch_b, w_out, norm_ffn, w_up, w_down, norm_final):
    f32 = lambda a: np.ascontiguousarray(np.asarray(a, dtype=np.float32))
    x_prompt, x_sample, state_conv, state_hgrn = f32(x_prompt), f32(x_sample), f32(state_conv), f32(state_hgrn)
    if "nc" not in _CACHE:
        _CACHE["nc"] = build_program()
    nc = _CACHE["nc"]
    ident, mask2, rmask, sel = _consts()
    shared = {
        "gm": f32(np.asarray(norm_mix)[0].reshape(16, 128).T),
        "gf": f32(np.asarray(norm_ffn)[0].reshape(16, 128).T),
        "cw": f32(np.asarray(conv_w)[0].reshape(3, 8, 128).transpose(2, 1, 0).reshape(128, 24)),
        "lbl": f32(np.asarray(lb_logits).reshape(2, 8, 128).transpose(2, 0, 1).reshape(128, 16)),
        "ogg": f32(np.asarray(onorm_g)[0].reshape(128, 1)),
        "nfin": f32(np.broadcast_to(np.asarray(norm_final).reshape(1, D), (128, D))),
        "ident": ident, "mask2": mask2, "rmask": rmask, "sel": sel,
        "w_in": f32(np.asarray(w_in)[0]), "w_branch_a": f32(np.asarray(w_branch_a)[0]),
        "w_branch_b": f32(np.asarray(w_branch_b)[0]), "w_out": f32(np.asarray(w_out)[0]),
        "w_up": f32(np.asarray(w_up)[0]), "w_down": f32(np.asarray(w_down)[0]),
    }
    in_maps = []
    for c in range(N_CORES):
        sq, hf = c // 2, c % 2
        m = dict(shared)
        m["x_main"] = f32(x_prompt[sq, hf * 1024:(hf + 1) * 1024])
        m["x_pre"] = f32(x_prompt[sq, 0:1024]) if hf == 1 else np.zeros((1024, D), np.float32)
        m["x_smp"] = f32(x_sample[c * 16:(c + 1) * 16, 0])
        m["sconv"] = f32(state_conv[0, c * 16:(c + 1) * 16].reshape(16, 2048))
        m["shgrn"] = f32(state_hgrn[0, c * 16:(c + 1) * 16])
        in_maps.append(m)
    if _CACHE.get('debug_return_maps'):
        return in_maps
    res = run_bass_kernel_spmd(nc, in_maps, core_ids=list(range(N_CORES)))
    R = res.results
    yp = np.zeros((4, 2048, D), np.float32)
    ys = np.zeros((128, 1, D), np.float32)
    ncp = np.zeros((1, 4, 2, 1024), np.float32)
    nhp = np.zeros((1, 4, 8, 128, 128), np.float32)
    ncs = np.zeros((1, 128, 2, 1024), np.float32)
    nhs = np.zeros((1, 128, 8, 128, 128), np.float32)
    for c in range(N_CORES):
        sq, hf = c // 2, c % 2
        r = R[c]
        yp[sq, hf * 1024:(hf + 1) * 1024] = np.asarray(r["y_main"])
        ys[c * 16:(c + 1) * 16, 0] = np.asarray(r["y_smp"])
        ncs[0, c * 16:(c + 1) * 16] = np.asarray(r["o_cs"])
        nhs[0, c * 16:(c + 1) * 16] = np.asarray(r["o_hs"])
        if hf == 1:
            ncp[0, sq] = np.asarray(r["o_cp"])
            nhp[0, sq] = np.asarray(r["o_hp"])
    return yp, ys, ncp, nhp, ncs, nhs
```

```python
import bisect
import os
from contextlib import ExitStack
import numpy as np
import concourse.bass as bass
import concourse.mybir as mybir
from concourse.bass_utils import run_bass_kernel_spmd

F32 = mybir.dt.float32
BF16 = mybir.dt.bfloat16
ALU = mybir.AluOpType
AF = mybir.ActivationFunctionType

D = 2048
NK = 16
TM = 1024
TSM = 16
TT = TM + TSM
EPS = 1e-6
SAME_SYNC = True
SCHEDULE = True
LOOKAHEAD = 12
SCHED_SEGS = (1, 2)
TBL_COST = 1.3
PRIO = 1
SLACK = 0.5
N_CORES = 8
ALLOC_LOG = []


class Sched:
    ENGS = ("pe", "act", "dve", "pool", "sp")

    def __init__(self):
        self.ops = []
        self.keys = {}
        self.bounds = []
        self.fences = set()
        self.region_sched = {}
        self.scratch = None
        self.dma_sem_ops = {}

    def op(self, eng, fn, reads=(), writes=(), dma=None, cost=0.3, lat=0.0, tbl=None):
        gi = len(self.ops)
        deps = set()
        ps_r = [k for k in reads if k and k[0] == "ps"]
        if ps_r:
            reads = [k for k in reads if not (k and k[0] == "ps")]
            writes = list(writes) + ps_r
        for k in reads:
            st = self.keys.get(k)
            if st is not None and st[0] is not None:
                deps.add(st[0])
        for k in writes:
            st = self.keys.get(k)
            if st is not None:
                if st[0] is not None:
                    deps.add(st[0])
                deps.update(st[1])
        for k in reads:
            self.keys.setdefault(k, [None, []])[1].append(gi)
        for k in writes:
            self.keys[k] = [gi, []]
        deps.discard(gi)
        self.ops.append(dict(eng=eng, fn=fn, deps=deps, dma=dma, gi=gi, cost=cost, lat=lat, tbl=tbl))
        return gi

    def fence(self, sched=False):
        self.bounds.append(len(self.ops))
        self.fences.add(len(self.ops))
        self.region_sched[len(self.ops)] = sched

    def mark(self, sched=False):
        self.bounds.append(len(self.ops))
        self.region_sched[len(self.ops)] = sched

    def barrier_op(self, lo, hi, key):
        gi = self.op("dve", lambda e: e.memset(self.scratch, 0.0), writes=[key], cost=0.1)
        self.ops[gi]["deps"] |= set(range(lo, hi))
        return gi

    def finalize(self):
        ops = self.ops
        bounds = [0] + [b for b in self.bounds if 0 < b < len(ops)] + [len(ops)]
        bounds = sorted(set(bounds))
        order = []
        t_eng = {e: 0.0 for e in self.ENGS}
        finish = {}
        for si in range(len(bounds) - 1):
            lo, hi = bounds[si], bounds[si + 1]
            seg = range(lo, hi)
            t0 = max(t_eng.values())
            for e in self.ENGS:
                t_eng[e] = t0
            if (not SCHEDULE) or (not self.region_sched.get(lo, False)):
                order.extend(seg)
                continue
            ndep = {}
            users = {}
            for i in seg:
                c = 0
                for d in ops[i]["deps"]:
                    if d >= lo:
                        c += 1
                        users.setdefault(d, []).append(i)
                ndep[i] = c
            blevel = {}
            for i in reversed(seg):
                m = 0.0
                for u in users.get(i, ()):
                    if blevel[u] > m:
                        m = blevel[u]
                blevel[i] = ops[i]["cost"] + ops[i]["lat"] + m
            ready = {e: [] for e in self.ENGS}
            import heapq
            for i in seg:
                if ndep[i] == 0:
                    heapq.heappush(ready[ops[i]["eng"]], i)
            nleft = hi - lo
            cur_tbl = [None]
            while nleft:
                best = None
                for e in self.ENGS:
                    cand = heapq.nsmallest(LOOKAHEAD, ready[e])
                    for i in cand:
                        o = ops[i]
                        est = t_eng[e]
                        for d in o["deps"]:
                            if d >= lo:
                                fd = finish[d] + (0.15 if ops[d]["eng"] != e else 0.1)
                                if fd > est:
                                    est = fd
                        pen = TBL_COST if (o["tbl"] is not None and o["tbl"] != cur_tbl[0]) else 0.0
                        if PRIO == 0:
                            key = (est + pen, i)
                        elif PRIO == 1:
                            key = (est + pen, -blevel[i], i)
                        else:
                            key = (round((est + pen) / SLACK), -blevel[i], i)
                        if best is None or key < best[0]:
                            best = (key, i, e, est, pen)
                _, i, e, est, pen = best
                ready[e].remove(i)
                heapq.heapify(ready[e])
                o = ops[i]
                if o["tbl"] is not None:
                    cur_tbl[0] = o["tbl"]
                t_eng[e] = est + pen + o["cost"]
                finish[i] = est + pen + o["cost"] + o["lat"]
                order.append(i)
                nleft -= 1
                for u in users.get(i, ()):
                    ndep[u] -= 1
                    if ndep[u] == 0:
                        heapq.heappush(ready[ops[u]["eng"]], u)
        self.est_total = max(t_eng.values())
        if os.environ.get("DBG_SCHED"):
            te = {e: 0.0 for e in self.ENGS}
            fin = {}
            seg_of = lambda i: bisect.bisect_right(bounds, i) - 1
            cur = 0
            seg_start = 0.0
            busy = {e: 0.0 for e in self.ENGS}
            for i in order + [None]:
                sg = seg_of(i) if i is not None else -1
                if sg != cur:
                    t0 = max(te.values())
                    print("SCHED seg %d: %.1f us  busy %s" % (cur, t0 - seg_start,
                          " ".join("%s=%.0f" % (e, busy[e]) for e in self.ENGS)))
                    seg_start = t0
                    busy = {e: 0.0 for e in self.ENGS}
                    for e in self.ENGS:
                        te[e] = t0
                    cur = sg
                if i is None:
                    break
                o = ops[i]
                e = o["eng"]
                est = te[e]
                for d in o["deps"]:
                    if d in fin:
                        fd = fin[d] + (0.15 if ops[d]["eng"] != e else 0.1)
                        est = max(est, fd)
                pen = 0.0
                if o["tbl"] is not None:
                    if o["tbl"] != getattr(self, "_dbg_tbl", None):
                        pen = TBL_COST
                    self._dbg_tbl = o["tbl"]
                te[e] = est + pen + o["cost"]
                busy[e] += o["cost"] + pen
                fin[i] = est + pen + o["cost"] + o["lat"]
            print("SCHED model total %.1f us, nops %d" % (max(te.values()), len(ops)))
        fence_list = sorted(self.fences)
        newpos = {old: new for new, old in enumerate(order)}
        new_ops = []
        for new, old in enumerate(order):
            o = ops[old]
            o["deps"] = set(newpos[d] for d in o["deps"])
            o["gi"] = new
            o["seg"] = bisect.bisect_right(fence_list, old)
            new_ops.append(o)
        self.ops = ops = new_ops
        last_on_eng = {}
        last_dma = {}
        pending = {}
        cur_seg = 0
        seg_last_eng, seg_last_dma = {}, {}
        for o in ops:
            if o["seg"] != cur_seg:
                deps = set(last_on_eng.values()) | set(last_dma.values())
                for e in self.ENGS:
                    pending[e] = pending.get(e, set()) | deps
                cur_seg = o["seg"]
            fd = pending.pop(o["eng"], None)
            if fd:
                o["deps"] |= fd
                o["deps"].discard(o["gi"])
            last_on_eng[o["eng"]] = o["gi"]
            if o["dma"] is not None:
                last_dma[o["dma"]] = o["gi"]
        self.dma_sem_ops = {}
        for o in ops:
            if o["dma"] is not None:
                self.dma_sem_ops.setdefault(o["dma"], []).append(o["gi"])

    def emit(self, nc, es):
        self.finalize()
        ops = self.ops
        compute_engs = ("pe", "act", "dve", "pool")
        needed = set()
        for o in ops:
            for d in o["deps"]:
                do = ops[d]
                if do["dma"] is not None:
                    continue
                if do["eng"] == o["eng"] and (o["eng"] == "pe" or not SAME_SYNC):
                    continue
                needed.add(d)
        ordinal = {}
        cnt = {e: 0 for e in compute_engs + ("sp",)}
        for o in ops:
            if o["dma"] is None and o["gi"] in needed:
                cnt[o["eng"]] += 1
                ordinal[o["gi"]] = cnt[o["eng"]]
        eng_sem = {e: es.enter_context(nc.semaphore("sem_" + e)) for e in compute_engs + ("sp",)}
        dma_sem = {k: es.enter_context(nc.semaphore("dsem_" + k)) for k in self.dma_sem_ops}
        per_eng = {e: [] for e in compute_engs + ("sp",)}
        for o in ops:
            per_eng[o["eng"]].append(o)

        def replay(engname, e):
            waited = {}
            for o in per_eng[engname]:
                wl = {}
                for d in o["deps"]:
                    do = ops[d]
                    if do["dma"] is not None:
                        lst = self.dma_sem_ops[do["dma"]]
                        n = bisect.bisect_left(lst, o["gi"])
                        key = ("d", do["dma"])
                        val = 16 * n
                    else:
                        if do["eng"] == engname and (engname == "pe" or not SAME_SYNC):
                            continue
                        key = ("e", do["eng"])
                        val = ordinal[d]
                    if val > wl.get(key, 0):
                        wl[key] = val
                for key, val in wl.items():
                    if waited.get(key, 0) >= val:
                        continue
                    waited[key] = val
                    sem = dma_sem[key[1]] if key[0] == "d" else eng_sem[key[1]]
                    e.wait_ge(sem, val)
                ins = o["fn"](e)
                if o["dma"] is not None:
                    ins.then_inc(dma_sem[o["dma"]], 16)
                elif o["gi"] in needed:
                    ins.then_inc(eng_sem[engname], 1)
            if engname == "sp":
                for k, lst in self.dma_sem_ops.items():
                    e.wait_ge(dma_sem[k], 16 * len(lst))

        block = es.enter_context(nc.Block())

        @block.tensor
        def _(e):
            replay("pe", e)

        @block.scalar
        def _(e):
            replay("act", e)

        @block.vector
        def _(e):
            replay("dve", e)

        @block.gpsimd
        def _(e):
            replay("pool", e)

        @block.sync
        def _(e):
            replay("sp", e)


def build_program(stop=99):
    nc = bass.Bass("TRN2", target_bir_lowering=False)

    def din(name, shape):
        return nc.dram_tensor(name, list(shape), F32, kind="ExternalInput").ap()

    def dout(name, shape):
        return nc.dram_tensor(name, list(shape), F32, kind="ExternalOutput").ap()

    x_pre = din("x_pre", [1024, D])
    x_main = din("x_main", [1024, D])
    x_smp = din("x_smp", [16, D])
    sconv = din("sconv", [16, 2048])
    shgrn = din("shgrn", [16, 8, 128, 128])
    gm_d = din("gm", [128, 16])
    gf_d = din("gf", [128, 16])
    cw_d = din("cw", [128, 24])
    lbl_d = din("lbl", [128, 16])
    og_d = din("ogg", [128, 1])
    nfin_d = din("nfin", [128, D])
    ident_d = din("ident", [128, 128])
    sel_d = din("sel", [128, 16])
    mask2_d = din("mask2", [128, 128])
    rmask_d = din("rmask", [128, 512])
    w_in = din("w_in", [D, 11264])
    w_a = din("w_branch_a", [1024, D])
    w_b = din("w_branch_b", [1024, D])
    w_out = din("w_out", [D, D])
    w_up = din("w_up", [D, 8192])
    w_down = din("w_down", [8192, D])

    y_main = dout("y_main", [1024, D])
    y_smp = dout("y_smp", [16, D])
    o_cp = dout("o_cp", [2, 1024])
    o_hp = dout("o_hp", [8, 128, 128])
    o_cs = dout("o_cs", [16, 2, 1024])
    o_hs = dout("o_hs", [16, 8, 128, 128])

    S = Sched()
    with ExitStack() as es:
        def sb(name, shape, dtype):
            return es.enter_context(nc.sbuf_tensor("s_" + name, list(shape), dtype))

        R1 = sb("R1", [128, 16 * TT], BF16)
        R2 = sb("R2", [128, 18 * D], BF16)
        R3 = sb("R3", [128, 16 * TT + 32], BF16)
        W = [sb("W0", [128, 16, 512], BF16), sb("W1", [128, 16, 512], BF16)]
        E_BYTES = 30 * 1024
        EA = sb("EA", [128, E_BYTES // 2], BF16)
        idb = sb("idb", [128, 128], BF16)
        idf = sb("idf", [128, 128], F32)
        selT = sb("selT", [128, 16], F32)
        ones_b = sb("ones_b", [128, 128], BF16)
        mask2 = sb("mask2", [128, 128], F32)
        rmask = sb("rmask", [128, 512], F32)
        gm = sb("gm", [128, 16], F32)
        gf = sb("gf", [128, 16], F32)
        cw = sb("cw", [128, 24], F32)
        lbl = sb("lbl", [128, 16], F32)
        lbm = sb("lbm", [128, 8], F32)
        oml = sb("oml", [128, 8], F32)
        ogg = sb("ogg", [128, 1], F32)
        ssb = sb("ssb", [128, 8], F32)
        scT = sb("scT", [128, 256], F32)
        uo = sb("uo", [128, 8 * 18], F32)
        dec = sb("dec", [128, 32], F32)
        hxPt = sb("hxPt", [128, 16, 16], BF16)
        S.scratch = ssb[:, 6:7]
        PS = [es.enter_context(nc.psum_tensor("ps%d" % i, [128, 512], F32)) for i in range(8)]

        bank_ctr = [0]
        bank_set = [list(range(8))]

        def bank():
            bs_ = bank_set[0]
            b = bs_[bank_ctr[0] % len(bs_)]
            bank_ctr[0] += 1
            return b

        def view(raw, off, shape, dtype):
            n = 1
            for s_ in shape[1:]:
                n *= s_
            esz = 2 if dtype == BF16 else 4
            a = off // 2
            ln = n * esz // 2
            assert off % 4 == 0 and a + ln <= raw.shape[1], (off, shape, raw.shape)
            ap = raw[:, a:a + ln]
            if dtype != BF16:
                ap = ap.bitcast(dtype)
            if len(shape) == 3:
                ap = ap.rearrange("p (a b) -> p a b", a=shape[1])
            return ap

        class Arena:
            def __init__(self, raws):
                self.raws = [(r, 0, n) for (r, n) in raws]
                self.nbase = len(self.raws)
                self.reset()

            def reset(self):
                self.raws = self.raws[:self.nbase]
                self.cur = [0 for _ in self.raws]

            def add_region(self, raw, base, nbytes):
                self.raws.append((raw, base, nbytes))
                self.cur.append(0)

            def alloc(self, shape, dtype):
                n = 1
                for s_ in shape[1:]:
                    n *= s_
                nb = n * (2 if dtype == BF16 else 4)
                nb = (nb + 31) // 32 * 32
                for i, (raw, base, tot) in enumerate(self.raws):
                    if self.cur[i] + nb <= tot:
                        v = view(raw, base + self.cur[i], shape, dtype)
                        ALLOC_LOG.append((tuple(shape), str(dtype), i, self.cur[i]))
                        self.cur[i] += nb
                        return v
                raise RuntimeError("arena overflow %s" % (shape,))

        ar = Arena([(R3, 16 * TT * 2), (EA, E_BYTES)])

        hxM = R1[:, :].rearrange("p (k t) -> p k t", k=16)
        R2b = R2[:, :]
        aT = R2b[:, 0:8 * TT].rearrange("p (c t) -> p c t", c=8)
        oT = R2b[:, 8 * TT:16 * TT].rearrange("p (c t) -> p c t", c=8)
        hxP = R2b[:, 16 * TT:16 * TT + 16 * 1024].rearrange("p (k t) -> p k t", k=16)
        x1v = R2[:, :].bitcast(F32).rearrange("p (t c) -> p t c", t=9)
        mixT = R3[:, 0:16 * TT].rearrange("p (k t) -> p k t", k=16)
        actT = mixT
        h2T = hxM

        def psb(b):
            return PS[b][:, :].bitcast(BF16)

        XR = []

        def ncols_of(ap):
            n = 1
            for d_ in tuple(ap.shape)[1:]:
                n *= int(d_)
            return n

        def dma(eng, out, in_, sem, reads=(), writes=()):
            nbytes = ncols_of(out) * int(tuple(out.shape)[0]) * 4
            S.op(eng, lambda e: e.dma_start(out=out, in_=in_), reads=list(reads) + XR, writes=writes, dma=sem,
                 cost=(1.3 if eng == "pool" else 0.3), lat=2.0 + nbytes / 300e3)

        def mm_group(out_ap, pairs, reads, writes):
            def fn(e):
                n = len(pairs)
                ins = None
                for i, (l, r) in enumerate(pairs):
                    ins = e.matmul(out_ap, lhsT=l, rhs=r, start=(i == 0), stop=(i == n - 1))
                return ins
            cost = sum(max(ncols_of(r), 64) / 2400.0 + 0.01 for (_, r) in pairs)
            S.op("pe", fn, reads=list(reads) + XR, writes=writes, cost=cost)

        def act(out, in_, func, reads, writes, **kw):
            tbl = "sig" if func == AF.Sigmoid else ("lnexp" if func in (AF.Ln, AF.Exp) else None)
            S.op("act", lambda e: e.activation(out=out, in_=in_, func=func, **kw), reads=list(reads) + XR, writes=writes,
                 cost=0.22 + ncols_of(out) / 1200.0, tbl=tbl)

        def vcost(eng, out):
            return (0.25 + ncols_of(out) / 480.0) if eng == "pool" else (0.12 + ncols_of(out) / 960.0)

        def tt_op(out, in0, in1, op, reads, writes, eng="dve"):
            S.op(eng, lambda e: e.tensor_tensor(out=out, in0=in0, in1=in1, op=op), reads=list(reads) + XR, writes=writes,
                 cost=vcost(eng, out))

        def cp_op(out, in_, reads, writes, eng="dve"):
            S.op(eng, lambda e: e.tensor_copy(out=out, in_=in_), reads=list(reads) + XR, writes=writes, cost=vcost(eng, out))

        def ts_op(out, in0, s1, s2, op0, op1, reads, writes, eng="dve"):
            if op1 is None:
                S.op(eng, lambda e: e.tensor_scalar(out=out, in0=in0, scalar1=s1, scalar2=None, op0=op0),
                     reads=list(reads) + XR, writes=writes, cost=vcost(eng, out))
            else:
                S.op(eng, lambda e: e.tensor_scalar(out=out, in0=in0, scalar1=s1, scalar2=s2, op0=op0, op1=op1),
                     reads=list(reads) + XR, writes=writes, cost=vcost(eng, out))

        def stt_op(out, in0, scalar, in1, op0, op1, reads, writes, eng="dve"):
            S.op(eng, lambda e: e.scalar_tensor_tensor(out=out, in0=in0, scalar=scalar, in1=in1, op0=op0, op1=op1),
                 reads=list(reads) + XR, writes=writes, cost=vcost(eng, out))

        WK = lambda s: [("w", s, q) for q in range(4)]

        def wcols(src, c0, n, nkc=16):
            return src[:, c0:c0 + n].rearrange("(kc p) c -> p kc c", p=128)

        dma("pool", idb[:, :], ident_d, "constp", writes=[("c", "idb")])
        S.op("dve", lambda e: e.memset(R3[:, 16 * TT:16 * TT + 32], 0.0), writes=[("c", "r3pad")], cost=0.1)
        for t_, d_, nm in ((selT, sel_d, "sel"), (idf, ident_d, "idf"), (mask2, mask2_d, "mask2"), (rmask, rmask_d, "rmask"),
                           (gm, gm_d, "gm"), (gf, gf_d, "gf"), (cw, cw_d, "cw"), (lbl, lbl_d, "lbl"),
                           (ogg, og_d, "ogg")):
            dma("sp", t_[:, :], d_, "const", writes=[("c", nm)])
        S.op("dve", lambda e: e.memset(ones_b[:, :], 1.0), writes=[("c", "ones")])
        tt_op(lbm[:, :], lbl[:, 0:8], lbl[:, 8:16], ALU.subtract, [("c", "lbl")], [("c", "lbm")])
        act(lbm[:, :], lbm[:, :], AF.Sigmoid, [("c", "lbm")], [("c", "lbm")])
        ts_op(oml[:, :], lbm[:, :], -1.0, 1.0, ALU.mult, ALU.add, [("c", "lbm")], [("c", "oml")])

        sct = ar.alloc([128, 2048], F32)
        dma("sp", sct[0:16, :], sconv, "misc", writes=[("t", "sct")])
        dma("sp", o_cs[:, 0, :], sct[0:16, 1024:2048], "out", reads=[("t", "sct")])
        b0 = bank()
        def fn_sct(e):
            ins = None
            for j in range(16):
                ins = e.matmul(PS[b0][:, j * 16:(j + 1) * 16], lhsT=sct[0:16, j * 128:(j + 1) * 128],
                               rhs=idf[0:16, 0:16], start=True, stop=True)
            return ins
        S.op("pe", fn_sct, reads=[("t", "sct"), ("c", "idf")], writes=[("ps", b0)], cost=0.6)
        S.op("act", lambda e: e.copy(out=scT[:, :], in_=PS[b0][:, 0:256]), reads=[("ps", b0)], writes=[("c", "scT")])
        scTv = scT[:, :].rearrange("p (r c s) -> p r c s", r=2, c=8)

        fills = []

        def add_fill(fn):
            fills.append(fn)

        def fill_C(c):
            def f(s):
                for blk, base in enumerate((0, 1024, 2048)):
                    dma("pool", W[s][:, :, blk * 128:(blk + 1) * 128], wcols(w_in, base + c * 128, 128),
                        "w%db%d" % (s, blk), writes=[("w", s, blk)])
            return f

        def fill_H(h):
            def f(s):
                for blk, base in enumerate((3072, 4096, 5120, 6144)):
                    dma("pool", W[s][:, :, blk * 128:(blk + 1) * 128], wcols(w_in, base + h * 128, 128),
                        "w%db%d" % (s, blk), writes=[("w", s, blk)])
            return f

        def fill_J(j):
            def f(s):
                dma("pool", W[s][:, :, 0:128], wcols(w_in, 7168 + j * 128, 128), "w%db0" % s, writes=[("w", s, 0)])
                dma("pool", W[s][:, :, 128:256], wcols(w_in, 9216 + j * 128, 128), "w%db1" % s, writes=[("w", s, 1)])
                dma("pool", W[s][:, 0:8, 256:384], wcols(w_a, j * 128, 128), "w%db2" % s, writes=[("w", s, 2)])
                dma("pool", W[s][:, 0:8, 384:512], wcols(w_b, j * 128, 128), "w%db3" % s, writes=[("w", s, 3)])
            return f

        def fill_full(src, r0, c0):
            def f(s):
                dma("pool", W[s][:, :, :], src[r0:r0 + 2048, c0:c0 + 512].rearrange("(kc p) c -> p kc c", p=128),
                    "w%df" % s, writes=WK(s))
            return f

        def fill_P(h):
            def f(s):
                dma("pool", W[s][:, :, 128:256], wcols(w_in, 4096 + h * 128, 128), "w%db1" % s, writes=[("w", s, 1)])
                dma("pool", W[s][:, :, 256:384], wcols(w_in, 5120 + h * 128, 128), "w%db2" % s, writes=[("w", s, 2)])
            return f

        for h in range(8):
            add_fill(fill_P(h))
        for c in range(8):
            add_fill(fill_C(c))
            add_fill(fill_H(c))
        for j in range(16):
            add_fill(fill_J(j))
        for n in range(4):
            add_fill(fill_full(w_out, 0, n * 512))
        for g in range(4):
            for q4 in range(4):
                add_fill(fill_full(w_up, 0, g * 2048 + q4 * 512))
            for n in range(4):
                add_fill(fill_full(w_down, g * 2048, n * 512))
        fill_i = [0]

        fills[0](0)
        fills[1](1)

        def next_slot():
            k = fill_i[0]
            if k >= 1 and k + 1 < len(fills):
                fills[k + 1]((k + 1) % 2)
            fill_i[0] += 1
            return k % 2

        def norm_transpose(tiles, gvec, gname, xs_bufs, junk, dst_fn, tag):
            for (src, r, rkeys, i) in tiles:
                sl = i % 2
                ssv = ssb[:, sl * 2:sl * 2 + 2]
                kss = ("ss", sl)
                S.op("pool", lambda e, ssv=ssv: e.memset(ssv, 0.0), writes=[kss])
                act(junk[0:r, :], src, AF.Square, rkeys + [kss], [("t", "junk"), kss], accum_out=ssv[0:r, 0:1])
                act(ssv[0:r, 1:2], ssv[0:r, 0:1], AF.Ln, [kss], [kss], scale=1.0 / D, bias=EPS)
                act(ssv[0:r, 1:2], ssv[0:r, 1:2], AF.Exp, [kss], [kss], scale=-0.5)
                xs = xs_bufs[sl]
                kxs0 = ("t", tag + "xs", sl, 0)
                kxs1 = ("t", tag + "xs", sl, 1)
                act(xs[0:r, 0:1024], src[:, 0:1024], AF.Copy, rkeys + [kss], [kxs0], scale=ssv[0:r, 1:2])
                ts_op(xs[0:r, 1024:2048], src[:, 1024:2048], ssv[0:r, 1:2], None, ALU.mult, None, rkeys + [kss], [kxs1])
                for hh in range(2):
                    kxs = kxs0 if hh == 0 else kxs1
                    b = bank()
                    pb = psb(b)

                    def fn(e, pb=pb, xs=xs, r=r, hh=hh):
                        ins = None
                        for k in range(8):
                            kc = hh * 8 + k
                            ins = e.transpose(out=pb[:, k * 128:k * 128 + r], in_=xs[0:r, kc * 128:(kc + 1) * 128],
                                              identity=idb[0:r, 0:r])
                        return ins
                    S.op("pe", fn, reads=[kxs, ("c", "idb")], writes=[("ps", b)], cost=0.65)
                    pv = pb[:, 0:1024].rearrange("p (k t) -> p k t", k=8)[:, :, 0:r]
                    gb = gvec[:, hh * 8:(hh + 1) * 8].unsqueeze(2).to_broadcast([128, 8, r])
                    dst, dkeys = dst_fn(i, hh)
                    tt_op(dst, pv, gb, ALU.mult, [("ps", b), ("c", gname)], dkeys)

        def hx_keys(i0, i1):
            return [("hx", i, hh) for i in range(i0, i1) for hh in range(2)]
        hx_keys_early = hx_keys

        xt = [ar.alloc([128, D], F32) for _ in range(3)]
        xs0 = [ar.alloc([128, D], BF16), ar.alloc([128, D], BF16)]
        junk0 = ar.alloc([128, D], BF16)
        tiles0 = []
        for i in range(17):
            if i < 8:
                src, r = x_pre[i * 128:(i + 1) * 128, :], 128
            elif i < 16:
                src, r = x_main[(i - 8) * 128:(i - 7) * 128, :], 128
            else:
                src, r = x_smp, 16
            tiles0.append((i, src, r))

        def dst0(i, hh):
            if i < 8:
                return hxP[:, hh * 8:(hh + 1) * 8, i * 128:(i + 1) * 128], [("hx", i, hh)]
            if i < 16:
                return hxM[:, hh * 8:(hh + 1) * 8, (i - 8) * 128:(i - 7) * 128], [("hx", i, hh)]
            return hxM[:, hh * 8:(hh + 1) * 8, 1024:1040], [("hx", i, hh)]

        tl = []
        for (i, src, r) in tiles0:
            sl = i % 3
            dma("sp", xt[sl][0:r, :], src, "xt%d" % sl, writes=[("t", "xt", sl)])
            tl = [(xt[sl][0:r, :], r, [("t", "xt", sl)], i)]
            norm_transpose(tl, gm, "gm", xs0, junk0, dst0, "s0")
        cp_op(hxPt[:, :, :], hxP[:, :, 1008:1024], hx_keys_early(7, 8), [("hxPt",)])
        S.fence(sched=True)
        P_LO = len(S.ops)
        if stop == 0:
            S.emit(nc, es)
            return nc
        ar.reset()


        R2SP = (16 * TT + 16 * 1024) * 2
        S_init = view(R2, R2SP, [128, 8, 128], F32)
        ar.add_region(R2, R2SP + 4096, (18 * D * 2 - R2SP) - 4096)
        t_f = [ar.alloc([128, 512], F32) for _ in range(2)]
        t_q = [ar.alloc([128, 512], F32) for _ in range(2)]
        t_v = [ar.alloc([128, 512], BF16) for _ in range(2)]
        t_ke = [ar.alloc([128, 512], BF16) for _ in range(2)]
        t_lf = ar.alloc([128, 512], F32)
        t_b = ar.alloc([128, 512], F32)
        t_eb = ar.alloc([128, 512], F32)
        t_enb = ar.alloc([128, 512], F32)
        k_inT = ar.alloc([128, 1024], BF16)
        q_inT = ar.alloc([128, 1024], BF16)
        ke_tok = ar.alloc([128, 16, 128], BF16)
        v_tok = ar.alloc([128, 16, 128], BF16)
        sog = ar.alloc([128, TT], F32)
        S_pp = ar.alloc([128, 2, 128], F32)
        S_bf = ar.alloc([128, 16, 128], BF16)
        S0b = [ar.alloc([128, 8, 128], F32) for _ in range(2)]
        S_bfs = ar.alloc([128, 8, 128], BF16)
        vm = ar.alloc([128, 16, 128], BF16)
        fS = ar.alloc([128, 16], F32)
        fS2 = ar.alloc([128, 16], F32)
        sogS = [ar.alloc([128, 16], F32) for _ in range(2)]
        kS_b = ar.alloc([128, 16], BF16)
        vS_b = ar.alloc([128, 16], BF16)
        qS_b = ar.alloc([128, 16], BF16)
        ktok_s = ar.alloc([128, 128], BF16)
        vtok_s = ar.alloc([128, 128], BF16)
        scm = [ar.alloc([128, 128], BF16), ar.alloc([128, 128], BF16)]
        osq = ar.alloc([128, 512], BF16)
        rstd_t = ar.alloc([128, 512], F32)
        on_t = ar.alloc([128, 512], F32)
        K = lambda *n: ("t",) + n
        BOS, BO2 = 5, [6]
        nt_ctr = [0]

        NTS = [
            ("P", lambda kc: hxP[:, kc, 0:512], hx_keys(0, 4), 0),
            ("P", lambda kc: hxP[:, kc, 512:1024], hx_keys(4, 8), 4),
            ("M", lambda kc: hxM[:, kc, 0:512], hx_keys(8, 12), 8),
            ("M", lambda kc: hxM[:, kc, 512:1024], hx_keys(12, 16), 12),
        ]

        def head_ops(h, s):
            Wq = [W[s][:, kc, 0:128] for kc in range(16)]
            Wf = [W[s][:, kc, 128:256] for kc in range(16)]
            Wi = [W[s][:, kc, 256:384] for kc in range(16)]
            Wo = [W[s][:, kc, 384:512] for kc in range(16)]
            lb_h, oml_h = lbm[:, h:h + 1], oml[:, h:h + 1]
            st = {}

            HV = ((0, 256), (256, 512))

            def front(nt):
                kind, hsrc, hk, tb = NTS[nt]
                p = nt_ctr[0] % 2
                nt_ctr[0] += 1
                st[nt] = p
                mcol = (tb - 8) * 128
                tf, tq, tv = t_f[p], t_q[p], t_v[p]
                bf_ = bank()
                mm_group(PS[bf_][:, :], [(Wf[kc], hsrc(kc)) for kc in range(16)], [("w", s, 1)] + hk, [("ps", bf_)])
                bi = bank()
                mm_group(PS[bi][:, :], [(Wi[kc], hsrc(kc)) for kc in range(16)], [("w", s, 2)] + hk, [("ps", bi)])
                for hv, (a, b_) in enumerate(HV):
                    act(tf[:, a:b_], PS[bf_][:, a:b_], AF.Sigmoid, [("ps", bf_)], [K("f", p, hv)])
                act(tv[:, :], PS[bi][:, :], AF.Copy, [("ps", bi)], [K("v", p)])
                if kind == "M":
                    bq = bank()
                    mm_group(PS[bq][:, :], [(Wq[kc], hsrc(kc)) for kc in range(16)], [("w", s, 0)] + hk, [("ps", bq)])
                    bo = bank()
                    mm_group(PS[bo][:, :], [(Wo[kc], hsrc(kc)) for kc in range(16)], [("w", s, 3)] + hk, [("ps", bo)])
                    act(tq[:, :], PS[bq][:, :], AF.Sigmoid, [("ps", bq)], [K("q", p)])
                    act(sog[:, mcol:mcol + 512], PS[bo][:, :], AF.Sigmoid, [("ps", bo)], [K("sog", nt)])
                    tt_op(tq[:, :], tq[:, :], PS[bq][:, :], ALU.mult, [K("q", p), ("ps", bq)], [K("q", p)])
                    tt_op(sog[:, mcol:mcol + 512], sog[:, mcol:mcol + 512], PS[bo][:, :], ALU.mult,
                          [K("sog", nt), ("ps", bo)], [K("sog", nt)])

            def chain(nt):
                kind, hsrc, hk, tb = NTS[nt]
                p = st[nt]
                mcol = (tb - 8) * 128
                tf, tq, tke = t_f[p], t_q[p], t_ke[p]
                kq = K("q", p)
                n0 = tb * 2
                c3 = lambda ap: ap.rearrange("p (c l) -> p c l", l=64)
                steps = []
                for hv, (a, b_) in enumerate(HV):
                    kf, klf, kb, keb, kenb = K("f", p, hv), K("lf", hv), K("b", hv), K("eb", hv), K("enb", hv)
                    kke = K("ke", p) if False else K("ke", p, hv)
                    sl = slice(a, b_)
                    ebv = c3(t_eb[:, sl])
                    ops = []
                    ops.append(lambda kf=kf, sl=sl: ts_op(tf[:, sl], tf[:, sl], oml_h, lb_h, ALU.mult, ALU.add,
                                                          [kf, ("c", "oml"), ("c", "lbm")], [kf]))
                    ops.append(lambda kf=kf, klf=klf, sl=sl: act(t_lf[:, sl], tf[:, sl], AF.Ln, [kf], [klf]))
                    ops.append(lambda klf=klf, kb=kb, sl=sl: S.op(
                        "dve", lambda e: e.tensor_tensor_scan(out=t_b[:, sl], data0=rmask[:, sl], data1=t_lf[:, sl],
                                                              initial=0.0, op0=ALU.mult, op1=ALU.add),
                        reads=[klf, ("c", "rmask")], writes=[kb], cost=0.7))
                    ops.append(lambda kb=kb, keb=keb, sl=sl: act(t_eb[:, sl], t_b[:, sl], AF.Exp, [kb], [keb]))
                    ops.append(lambda kb=kb, kenb=kenb, sl=sl: act(t_enb[:, sl], t_b[:, sl], AF.Exp, [kb], [kenb], scale=-1.0))
                    ops.append(lambda kf=kf, sl=sl: ts_op(tf[:, sl], tf[:, sl], -1.0, 1.0, ALU.mult, ALU.add, [kf], [kf],
                                                          eng="pool"))
                    ops.append(lambda keb=keb, ebv=ebv, hv=hv: act(dec[:, n0 + 4 * hv:n0 + 4 * hv + 4], ebv[:, :, 63], AF.Copy,
                                                                   [keb], [K("dec", nt)]))
                    if kind == "M":
                        ops.append(lambda kf=kf, kenb=kenb, sl=sl, a=a, b_=b_: tt_op(
                            k_inT[:, mcol + a:mcol + b_], tf[:, sl], t_enb[:, sl], ALU.mult, [kf, kenb], [K("kin", nt)]))
                        ops.append(lambda keb=keb, sl=sl, a=a, b_=b_: tt_op(
                            q_inT[:, mcol + a:mcol + b_], tq[:, sl], t_eb[:, sl], ALU.mult, [kq, keb], [K("qin", nt)]))
                    ops.append(lambda kenb=kenb, keb=keb, klf=klf, sl=sl, ebv=ebv: tt_op(
                        c3(t_lf[:, sl]), c3(t_enb[:, sl]), ebv[:, :, 63:64].to_broadcast([128, 4, 64]), ALU.mult,
                        [kenb, keb, klf], [klf]))
                    ops.append(lambda klf=klf, kf=kf, sl=sl: tt_op(tke[:, sl], t_lf[:, sl], tf[:, sl], ALU.mult,
                                                                   [klf, kf], [K("ke", p)]))
                    steps.append(ops)
                for i in range(len(steps[0])):
                    steps[0][i]()
                    steps[1][i]()

            def trans(nt):
                kind, hsrc, hk, tb = NTS[nt]
                p = st[nt]
                for (src, ksrc, dst, kdst, use_act) in ((t_v[p], K("v", p), v_tok, K("vtok", nt), False),
                                                        (t_ke[p], K("ke", p), ke_tok, K("ketok", nt), True)):
                    bt = bank()
                    pbt = psb(bt)

                    def fn_t(e, pbt=pbt, src=src):
                        ins = None
                        for k in range(4):
                            ins = e.transpose(out=pbt[:, k * 128:(k + 1) * 128], in_=src[:, k * 128:(k + 1) * 128],
                                              identity=idb[:, :])
                        return ins
                    S.op("pe", fn_t, reads=[ksrc, ("c", "idb")], writes=[("ps", bt)], cost=0.35)
                    pv = pbt[:, 0:512].rearrange("p (k t) -> p k t", k=4)
                    if use_act:
                        act(dst[:, tb:tb + 4, :], pv, AF.Copy, [("ps", bt)], [kdst])
                    else:
                        cp_op(dst[:, tb:tb + 4, :], pv, [("ps", bt)], [kdst])

            def scan(part):
                if part == 0:
                    S.op("dve", lambda e: e.memset(S_pp[:, 0, :], 0.0), writes=[K("Spp", 0)])
                else:
                    act(S_pp[:, 0, :], S_init[:, h, :], AF.Copy, [("Sinit", h)], [K("Spp", 0)])
                for grp in (part * 2, part * 2 + 1):
                    bdp = [bank(), bank()]

                    def fn_d(e, grp=grp, bdp=bdp):
                        ins = None
                        for j in range(4):
                            for a_ in range(2):
                                tt_ = grp * 4 + j
                                r0 = a_ * 64
                                ins = e.matmul(PS[bdp[a_]][:, j * 128:(j + 1) * 128], lhsT=ke_tok[r0:r0 + 64, tt_, :],
                                               rhs=v_tok[r0:r0 + 64, tt_, :], start=True, stop=True)
                        return ins
                    S.op("pe", fn_d, reads=[K("ketok", grp), K("vtok", grp)], writes=[("ps", bdp[0]), ("ps", bdp[1])], cost=0.6)
                    for j in range(4):
                        for a_ in range(2):
                            n = grp * 8 + j * 2 + a_
                            cur, nxt = n % 2, (n + 1) % 2
                            if n >= 16:
                                act(S_bf[:, n - 16, :], S_pp[:, cur, :], AF.Copy, [K("Spp", cur)], [K("Sbf", (n - 16) // 2)])
                            stt_op(S_pp[:, nxt, :], S_pp[:, cur, :], dec[:, n:n + 1],
                                   PS[bdp[a_]][:, j * 128:(j + 1) * 128], ALU.mult, ALU.add,
                                   [K("Spp", cur), K("dec", grp), ("ps", bdp[a_])], [K("Spp", nxt)])
                if part == 1:
                    dma("sp", o_hp[h, :, :], S_pp[:, 0, :], "ohp", reads=[K("Spp", 0)])

            def o_main(half):
                pend = None
                for tt_ in range(half * 4, half * 4 + 4):
                    nt = 2 + tt_ // 4
                    bsc = bank()
                    mm_group(PS[bsc][:, 0:128],
                             [(k_inT[:, tt_ * 128:(tt_ + 1) * 128], q_inT[:, tt_ * 128:(tt_ + 1) * 128])],
                             [K("kin", nt), K("qin", nt)], [("ps", bsc)])
                    sc_ = scm[tt_ % 2]
                    ksc = K("scm", tt_ % 2)
                    tt_op(sc_[:, :], PS[bsc][:, 0:128], mask2[:, :], ALU.mult, [("ps", bsc), ("c", "mask2")], [ksc])
                    pob = PS[BO2[0]]
                    c0 = (tt_ % 4) * 128

                    def fn_o(e, tt_=tt_, pob=pob, c0=c0, sc_=sc_):
                        ins = None
                        for a_ in range(2):
                            oc = c0 + 64 * a_
                            e.matmul(pob[:, oc:oc + 64], lhsT=S_bf[:, 2 * tt_ + a_, :],
                                     rhs=q_inT[:, tt_ * 128 + 64 * a_:tt_ * 128 + 64 * a_ + 64], start=True, stop=False)
                            ins = e.matmul(pob[:, oc:oc + 64], lhsT=v_tok[:, 8 + tt_, :], rhs=sc_[:, 64 * a_:64 * a_ + 64],
                                           start=False, stop=True)
                        return ins

                    def emit_o(fn_o=fn_o, tt_=tt_, nt=nt, ksc=ksc):
                        S.op("pe", fn_o, reads=[K("vtok", nt), ksc, K("Sbf", tt_), K("qin", nt)]
                             + ([("ps", BO2[0])] if tt_ % 4 else []), writes=[("ps", BO2[0])])
                    if pend is not None:
                        pend()
                    pend = emit_o
                pend()

            def norm(piece):
                for (pb_, ncol, dcol, ksog) in (((BO2[0], 512, 0, K("sog", 2)), (BO2[0], 512, 512, K("sog", 3)),
                                                 (BOS, 16, 1024, ksgS))[piece],):
                    if pb_ == BOS:
                        po_ap = PS[pb_][:, 0:272].rearrange("p (a b) -> p a b", b=17)[:, :, 0]
                    else:
                        po_ap = PS[pb_][:, 0:ncol]
                    act(osq[:, 0:ncol], po_ap, AF.Square, [("ps", pb_)], [K("osq")])
                    bss = bank()
                    mm_group(PS[bss][:, 0:ncol], [(ones_b[:, :], osq[:, 0:ncol])], [K("osq"), ("c", "ones")], [("ps", bss)])
                    act(rstd_t[:, 0:ncol], PS[bss][:, 0:ncol], AF.Ln, [("ps", bss)], [K("rstd")], scale=1.0 / 128, bias=EPS)
                    act(rstd_t[:, 0:ncol], rstd_t[:, 0:ncol], AF.Exp, [K("rstd")], [K("rstd")], scale=-0.5)
                    tt_op(on_t[:, 0:ncol], po_ap, rstd_t[:, 0:ncol], ALU.mult, [("ps", pb_), K("rstd")], [K("on")])
                    sg_ap = sgS[:, :] if pb_ == BOS else sog[:, dcol:dcol + ncol]
                    stt_op(oT[:, h, dcol:dcol + ncol], on_t[:, 0:ncol], ogg[:, 0:1], sg_ap,
                           ALU.mult, ALU.mult, [K("on"), ksog, ("c", "ogg")], [("oT",)])

            sgS = sogS[h % 2]
            ksgS = K("sogS", h % 2)
            sv = {}

            def s1():
                bs = bank()

                def fn_s(e, bs=bs):
                    ins = None
                    for blk, Wx in enumerate((Wq, Wf, Wi, Wo)):
                        for kc in range(16):
                            ins = e.matmul(PS[bs][:, blk * 16:(blk + 1) * 16], lhsT=Wx[kc], rhs=hxM[:, kc, 1024:1040],
                                           start=(kc == 0), stop=(kc == 15))
                    return ins
                S.op("pe", fn_s, reads=WK(s) + hx_keys(16, 17), writes=[("ps", bs)], cost=2.0)
                act(fS[:, :], PS[bs][:, 16:32], AF.Sigmoid, [("ps", bs)], [K("fS")])
                ts_op(fS[:, :], fS[:, :], oml_h, lb_h, ALU.mult, ALU.add, [K("fS"), ("c", "oml"), ("c", "lbm")], [K("fS")])
                ts_op(kS_b[:, :], fS[:, :], -1.0, 1.0, ALU.mult, ALU.add, [K("fS")], [K("kS")])
                act(vS_b[:, :], PS[bs][:, 32:48], AF.Copy, [("ps", bs)], [K("vS")])
                act(fS2[:, :], PS[bs][:, 0:16], AF.Sigmoid, [("ps", bs)], [K("fS2")])
                tt_op(qS_b[:, :], fS2[:, :], PS[bs][:, 0:16], ALU.mult, [K("fS2"), ("ps", bs)], [K("qS")])
                act(sgS[:, :], PS[bs][:, 48:64], AF.Sigmoid, [("ps", bs)], [ksgS])
                tt_op(sgS[:, :], sgS[:, :], PS[bs][:, 48:64], ALU.mult, [ksgS, ("ps", bs)], [ksgS])

            def s2():
                bts = bank()
                pbts = psb(bts)

                def fn_ts(e, pbts=pbts):
                    e.transpose(out=pbts[0:16, 0:128], in_=kS_b[:, :], identity=idb[:, :])
                    return e.transpose(out=pbts[0:16, 128:256], in_=vS_b[:, :], identity=idb[:, :])
                S.op("pe", fn_ts, reads=[K("kS"), K("vS"), ("c", "idb")], writes=[("ps", bts)])
                act(ktok_s[0:16, :], pbts[0:16, 0:128], AF.Copy, [("ps", bts)], [K("ktoks")])
                act(vtok_s[0:16, :], pbts[0:16, 128:256], AF.Copy, [("ps", bts)], [K("vtoks")])
                tt_op(vm[0:16, :, :], vtok_s[0:16, :].unsqueeze(1).to_broadcast([16, 16, 128]),
                      idb[0:16, 0:16].unsqueeze(2).to_broadcast([16, 16, 128]), ALU.mult,
                      [K("vtoks"), ("c", "idb")], [K("vm")])

            def s3():
                for hf in range(2):
                    S0 = S0b[hf]
                    kS0 = K("S0", hf)
                    bd = [bank(), bank()]
                    for q_ in range(2):
                        mm_group(PS[bd[q_]][:, :],
                                 [(ktok_s[0:16, :], vm[0:16, hf * 8 + q_ * 4:hf * 8 + q_ * 4 + 4, :])],
                                 [K("ktoks"), K("vm")], [("ps", bd[q_])])
                    tt_op(S0[:, :, :], S0[:, :, :], fS[:, hf * 8:(hf + 1) * 8].unsqueeze(2).to_broadcast([128, 8, 128]),
                          ALU.mult, [kS0, K("fS")], [kS0])
                    for q_ in range(2):
                        tt_op(S0[:, q_ * 4:(q_ + 1) * 4, :], S0[:, q_ * 4:(q_ + 1) * 4, :],
                              PS[bd[q_]][:, :].rearrange("p (b e) -> p b e", b=4), ALU.add,
                              [kS0, ("ps", bd[q_])], [kS0])
                    dma("sp", o_hs[hf * 8:(hf + 1) * 8, h, :, :].rearrange("b d e -> d b e"), S0[:, :, :],
                        "s0out%d" % hf, reads=[kS0])

            def s4():
                for hf in range(2):
                    S0 = S0b[hf]
                    kS0 = K("S0", hf)
                    act(S_bfs[:, :, :], S0[:, :, :], AF.Copy, [kS0], [K("Sbfs")])

                    def fn_os(e, hf=hf):
                        ins = None
                        for b_ in range(8):
                            col = hf * 8 + b_
                            ins = e.matmul(PS[BOS][:, col * 16:(col + 1) * 16], lhsT=S_bfs[:, b_, :], rhs=qS_b[:, 0:16],
                                           start=True, stop=True)
                        return ins
                    S.op("pe", fn_os, reads=[K("Sbfs"), K("qS")] + ([("ps", BOS)] if hf else []), writes=[("ps", BOS)])

            def prefetch():
                for hf in range(2):
                    dma("sp", S0b[hf][:, :, :], shgrn[hf * 8:(hf + 1) * 8, h, :, :].rearrange("b d e -> d b e"),
                        "s0ld%d" % hf, writes=[K("S0", hf)])

            def save_init():
                act(S_init[:, h, :], S_pp[:, 0, :], AF.Copy, [K("Spp", 0)], [("Sinit", h)])

            return dict(front=front, chain=chain, trans=trans, scan=scan, o_main=o_main, norm=norm, save_init=save_init,
                        s1=s1, s2=s2, s3=s3, s4=s4, prefetch=prefetch)


        bank_set[0] = list(range(8))
        prevP = None
        for h in range(8):
            s = next_slot()
            Hp = head_ops(h, s)
            Hp["front"](0)
            Hp["front"](1)
            if prevP is not None:
                prevP["scan"](0)
                prevP["save_init"]()
            Hp["chain"](0)
            Hp["trans"](0)
            Hp["chain"](1)
            Hp["trans"](1)
            prevP = Hp
        prevP["scan"](0)
        prevP["save_init"]()
        S.barrier_op(P_LO, len(S.ops), ("bar", "P"))
        bank_set[0] = [0, 1, 2, 3, 4, 7]
        arC = Arena([])
        arC.add_region(R2, 16 * TT * 2, 16 * 1024 * 2)
        cbuf = [[arC.alloc([128, 1042], F32) for _ in range(3)] for _ in range(2)]
        accs_b = [arC.alloc([128, 16], F32) for _ in range(2)]
        uov = uo[:, :].rearrange("p (c t) -> p c t", c=8)
        cwv = cw[:, :].rearrange("p (c k) -> p c k", c=8)

        def C_blocks(c, s):
            par = c % 2
            hcs, ubuf, accb = cbuf[par]
            kh, ku, ka = ("t", "hcs", par), ("t", "ubuf", par), ("t", "acc", par)
            accs = accs_b[par]
            kas = ("t", "accs", par)

            def cgroups(blk):
                Wl = [W[s][:, kc, blk * 128:(blk + 1) * 128] for kc in range(16)]
                g0, g1, g2 = bank(), bank(), bank()
                mm_group(PS[g0][:, 0:352], [(Wl[kc], hxM[:, kc, 0:352]) for kc in range(16)],
                         [("w", s, blk)] + hx_keys(8, 11), [("ps", g0)])
                mm_group(PS[g1][:, 0:352], [(Wl[kc], hxM[:, kc, 352:704]) for kc in range(16)],
                         [("w", s, blk)] + hx_keys(10, 14), [("ps", g1)])

                def fn2(e, Wl=Wl, g2=g2, blk=blk):
                    ins = None
                    for kc in range(16):
                        ins = e.matmul(PS[g2][:, 0:336], lhsT=Wl[kc], rhs=hxM[:, kc, 704:1040],
                                       start=(kc == 0), stop=(kc == 15))
                    if blk == 1:
                        return ins
                    for kc in range(16):
                        ins = e.matmul(PS[g2][:, 336:352], lhsT=Wl[kc], rhs=hxPt[:, kc, :],
                                       start=(kc == 0), stop=(kc == 15))
                    return ins
                S.op("pe", fn2, reads=[("w", s, blk), ("hxPt",)] + hx_keys(13, 17) + XR, writes=[("ps", g2)],
                     cost=16 * (336 / 2400.0 + 0.01) + (0 if blk == 1 else 16 * 0.037))
                return g0, g1, g2

            def pieces(g0, g1, g2):
                return ((PS[g0][:, 0:352], slice(18, 370), g0), (PS[g1][:, 0:352], slice(370, 722), g1),
                        (PS[g2][:, 0:320], slice(722, 1042), g2), (PS[g2][:, 320:336], slice(0, 16), g2),
                        (PS[g2][:, 350:352], slice(16, 18), g2))

            def b0():
                for (src, dsl, g) in pieces(*cgroups(0)):
                    act(hcs[:, dsl], src, AF.Copy, [("ps", g)], [kh])

            def b1():
                for (src, dsl, g) in pieces(*cgroups(2)):
                    tt_op(ubuf[:, dsl], src, hcs[:, dsl], ALU.mult, [("ps", g), kh], [ku])
                ts_op(accb[:, 0:1024], ubuf[:, 16:1040], cwv[:, c, 0:1], None, ALU.mult, None, [ku, ("c", "cw")], [ka])
                stt_op(accb[:, 0:1024], ubuf[:, 17:1041], cwv[:, c, 1:2], accb[:, 0:1024], ALU.mult, ALU.add,
                       [ku, ka, ("c", "cw")], [ka])
                stt_op(accb[:, 0:1024], ubuf[:, 18:1042], cwv[:, c, 2:3], accb[:, 0:1024], ALU.mult, ALU.add,
                       [ku, ka, ("c", "cw")], [ka])
                ts_op(accs[:, :], scTv[:, 0, c, :], cwv[:, c, 0:1], None, ALU.mult, None,
                      [("c", "scT"), ("c", "cw")], [kas])
                stt_op(accs[:, :], scTv[:, 1, c, :], cwv[:, c, 1:2], accs[:, :], ALU.mult, ALU.add,
                       [("c", "scT"), kas, ("c", "cw")], [kas])
                stt_op(accs[:, :], ubuf[:, 0:16], cwv[:, c, 2:3], accs[:, :], ALU.mult, ALU.add,
                       [ku, kas, ("c", "cw")], [kas])
                act(uov[:, c, 0:2], ubuf[:, 1040:1042], AF.Copy, [ku], [("c", "uo")])
                act(uov[:, c, 2:18], ubuf[:, 0:16], AF.Copy, [ku], [("c", "uo")])

            def b2():
                g0, g1, g2 = cgroups(1)
                tt_op(aT[:, c, 0:352], PS[g0][:, 0:352], accb[:, 0:352], ALU.mult, [("ps", g0), ka], [("aT",)])
                tt_op(aT[:, c, 352:704], PS[g1][:, 0:352], accb[:, 352:704], ALU.mult, [("ps", g1), ka], [("aT",)])
                tt_op(aT[:, c, 704:1024], PS[g2][:, 0:320], accb[:, 704:1024], ALU.mult, [("ps", g2), ka], [("aT",)])
                tt_op(aT[:, c, 1024:1040], PS[g2][:, 320:336], accs[:, :], ALU.mult, [("ps", g2), kas], [("aT",)])

            return b0, b1, b2

        assert fill_i[0] == 8
        fills[9](1)
        prev = None
        for h in range(8):
            cb0, cb1, cb2 = C_blocks(h, 0)
            s = 1
            H_ = head_ops(h, s)
            H_["prefetch"]()
            if prev is not None:
                prev["scan"](1)
                prev["o_main"](0)
                prev["norm"](0)
                prev["o_main"](1)
                prev["norm"](1)
                prev["norm"](2)
            H_["front"](2)
            H_["s1"]()
            XR.append(("bar", "P")); cb0(); XR.pop()
            H_["chain"](2)
            H_["front"](3)
            if h < 7:
                fills[9 + 2 * (h + 1)](1)
            H_["trans"](2)
            H_["s2"]()
            XR.append(("bar", "P")); cb1(); XR.pop()
            H_["chain"](3)
            H_["s3"]()
            H_["trans"](3)
            XR.append(("bar", "P")); cb2(); XR.pop()
            fills[8 + 2 * (h + 1)](0)
            H_["s4"]()
            prev = H_
        fill_i[0] = 24
        prev["scan"](1)
        prev["o_main"](0)
        prev["norm"](0)
        prev["o_main"](1)
        prev["norm"](1)
        prev["norm"](2)
        b1, b2 = bank(), bank()
        for half, bb in ((0, b1), (1, b2)):
            def fn_u(e, half=half, bb=bb):
                ins = None
                for k in range(4):
                    c = half * 4 + k
                    ins = e.matmul(PS[bb][0:18, k * 128:(k + 1) * 128], lhsT=uov[:, c, :], rhs=idf[:, :],
                                   start=True, stop=True)
                return ins
            S.op("pe", fn_u, reads=[("c", "uo"), ("c", "idf")], writes=[("ps", bb)])
        S.mark(sched=True)
        S.barrier_op(P_LO, len(S.ops), ("bar", "H"))
        XR.append(("bar", "H"))
        uot = view(EA, 26624, [128, 1024], F32)
        act(uot[0:18, 0:512], PS[b1][0:18, :], AF.Copy, [("ps", b1)], [("t", "uot")])
        act(uot[0:18, 512:1024], PS[b2][0:18, :], AF.Copy, [("ps", b2)], [("t", "uot")])
        dma("sp", o_cp, uot[0:2, :], "out", reads=[("t", "uot")])
        dma("sp", o_cs[:, 1, :], uot[2:18, :], "out", reads=[("t", "uot")])
        bank_set[0] = list(range(8))
        ar.reset()

        t1 = [ar.alloc([128, 512], F32) for _ in range(2)]
        t2 = [ar.alloc([128, 512], F32) for _ in range(2)]
        ar.cur[0] = 16 * TT * 2
        ar.cur[1] = 0
        t1 = [ar.alloc([128, 512], F32) for _ in range(2)]
        t2 = [ar.alloc([128, 512], F32) for _ in range(2)]
        it = 0
        for j in range(16):
            s = next_slot()
            for (c0, ncol, hk) in ((0, 352, hx_keys(8, 11)), (352, 352, hx_keys(10, 14)), (704, 336, hx_keys(13, 17))):
                p_ = it % 2
                it += 1
                bga, bgb, bA, bB = bank(), bank(), bank(), bank()
                mm_group(PS[bga][:, 0:ncol], [(W[s][:, kc, 0:128], hxM[:, kc, c0:c0 + ncol]) for kc in range(16)],
                         [("w", s, 0)] + hk, [("ps", bga)])
                mm_group(PS[bgb][:, 0:ncol], [(W[s][:, kc, 128:256], hxM[:, kc, c0:c0 + ncol]) for kc in range(16)],
                         [("w", s, 1)] + hk, [("ps", bgb)])
                mm_group(PS[bA][:, 0:ncol], [(W[s][:, kc, 256:384], aT[:, kc, c0:c0 + ncol]) for kc in range(8)],
                         [("w", s, 2), ("aT",)], [("ps", bA)])
                mm_group(PS[bB][:, 0:ncol], [(W[s][:, kc, 384:512], oT[:, kc, c0:c0 + ncol]) for kc in range(8)],
                         [("w", s, 3), ("oT",)], [("ps", bB)])
                k1, k2 = ("t", "t1", p_), ("t", "t2", p_)
                act(t1[p_][:, 0:ncol], PS[bga][:, 0:ncol], AF.Sigmoid, [("ps", bga)], [k1])
                act(t2[p_][:, 0:ncol], PS[bgb][:, 0:ncol], AF.Sigmoid, [("ps", bgb)], [k2])
                tt_op(t1[p_][:, 0:ncol], t1[p_][:, 0:ncol], PS[bA][:, 0:ncol], ALU.mult, [k1, ("ps", bA)], [k1])
                tt_op(t2[p_][:, 0:ncol], t2[p_][:, 0:ncol], PS[bB][:, 0:ncol], ALU.mult, [k2, ("ps", bB)], [k2])
                tt_op(mixT[:, j, c0:c0 + ncol], t1[p_][:, 0:ncol], t2[p_][:, 0:ncol], ALU.add, [k1, k2], [("mix",)])
        if stop == 3:
            S.fence()
            S.emit(nc, es)
            return nc


        def smp_group(b, s, src_keys):
            def fn(e):
                ins = None
                for r_ in range(4):
                    for j in range(4):
                        kc = r_ * 4 + j
                        ins = e.matmul(PS[b][32 * j:32 * j + 32, :], lhsT=R3[:, kc * TT + 1024:kc * TT + 1056],
                                       rhs=W[s][:, kc, :], start=(r_ == 0), stop=(r_ == 3),
                                       tile_position=(0, 32 * j))
                return ins
            S.op("pe", fn, reads=WK(s) + list(src_keys) + [("c", "r3pad")] + XR, writes=[("ps", b)], cost=1.1)

        def smp_reduce(zero_rest):
            for n in range(4):
                b = bank()
                cols = slice(n * 512, (n + 1) * 512)
                S.op("pe", lambda e, b=b, cols=cols: e.matmul(PS[b][0:16, :], lhsT=selT[:, :], rhs=x1v[:, 8, cols],
                                                              start=True, stop=True),
                     reads=[("x1", 8, n), ("c", "sel")] + XR, writes=[("ps", b)], cost=0.9)
                cp_op(x1v[0:16, 8, cols], PS[b][0:16, :], [("ps", b)], [("x1", 8, n)])
            if zero_rest:
                S.op("dve", lambda e: e.memset(x1v[32:64, 8, :], 0.0), reads=list(XR),
                     writes=[("x1", 8, n) for n in range(4)], cost=2.2)
                S.op("dve", lambda e: e.memset(x1v[64:128, 8, :], 0.0), reads=list(XR),
                     writes=[("x1", 8, n) for n in range(4)], cost=2.2)

        xsrc = [(x_main[t * 128:(t + 1) * 128, :], 128) for t in range(8)] + [(x_smp, 16)]
        S.op("dve", lambda e: e.memset(x1v[:, 8, :], 0.0), reads=list(XR),
             writes=[("x1", 8, n) for n in range(4)] + [("aT",), ("oT",)], cost=2.2)
        for t, (src, r) in enumerate(xsrc):
            dma("sp", x1v[0:r, t, :], src, "x1ld", writes=[("x1", t, n) for n in range(4)] + [("aT",), ("oT",)])
        for n in range(4):
            s = next_slot()
            for t, (src, r) in enumerate(xsrc):
                c0 = t * 128
                b = bank()
                if t == 8:
                    smp_group(b, s, [("mix",)])
                    tt_op(x1v[:, 8, n * 512:(n + 1) * 512], x1v[:, 8, n * 512:(n + 1) * 512], PS[b][:, :], ALU.add,
                          [("ps", b), ("x1", t, n)], [("x1", t, n)])
                    continue
                mm_group(PS[b][0:r, :], [(mixT[:, kc, c0:c0 + r], W[s][:, kc, :]) for kc in range(16)],
                         WK(s) + [("mix",)], [("ps", b)])
                tt_op(x1v[0:r, t, n * 512:(n + 1) * 512], x1v[0:r, t, n * 512:(n + 1) * 512], PS[b][0:r, :], ALU.add,
                      [("ps", b), ("x1", t, n)], [("x1", t, n)])
        smp_reduce(True)
        if stop == 4:
            S.fence()
            S.emit(nc, es)
            return nc

        xs4 = [ar.alloc([128, D], BF16), ar.alloc([128, D], BF16)]
        junk4 = ar.alloc([128, D], BF16)

        def dst4(i, hh):
            r = 128 if i < 8 else 16
            return (h2T[:, hh * 8:(hh + 1) * 8, i * 128:i * 128 + r],
                    [("h2", i, hh), ("hx", (8 + i) if i < 8 else 16, hh)])

        tiles4 = [(x1v[0:(128 if t < 8 else 16), t, :], 128 if t < 8 else 16, [("x1", t, n) for n in range(4)], t)
                  for t in range(9)]
        norm_transpose(tiles4, gf, "gf", xs4, junk4, dst4, "s4")
        if stop == 5:
            S.fence()
            S.emit(nc, es)
            return nc

        tr_ = [ar.alloc([128, 512], F32) for _ in range(3)]
        h2_keys = lambda t0, t1_: [("h2", i, hh) for i in range(t0, t1_) for hh in range(2)]
        it = 0
        for g in range(4):
            for q4 in range(4):
                s = next_slot()
                for fc in range(4):
                    fidx = q4 * 4 + fc
                    for (c0, ncol, hk) in ((0, 352, h2_keys(0, 3)), (352, 352, h2_keys(2, 6)), (704, 336, h2_keys(5, 9))):
                        b = bank()
                        mm_group(PS[b][:, 0:ncol],
                                 [(W[s][:, kc, fc * 128:(fc + 1) * 128], h2T[:, kc, c0:c0 + ncol]) for kc in range(16)],
                                 WK(s) + hk, [("ps", b)])
                        p_ = it % 3
                        it += 1
                        kt = ("t", "relu", p_)
                        act(tr_[p_][:, 0:ncol], PS[b][:, 0:ncol], AF.Relu, [("ps", b)], [kt])
                        tt_op(actT[:, fidx, c0:c0 + ncol], tr_[p_][:, 0:ncol], tr_[p_][:, 0:ncol], ALU.mult,
                              [kt], [("actT",)] + ([("mix",)] if g == 0 else []))
            for n in range(4):
                s = next_slot()
                for t, (src, r) in enumerate(xsrc):
                    c0 = t * 128
                    b = bank()
                    if t == 8:
                        smp_group(b, s, [("actT",)])
                        tt_op(x1v[:, 8, n * 512:(n + 1) * 512], x1v[:, 8, n * 512:(n + 1) * 512], PS[b][:, :],
                              ALU.add, [("ps", b), ("x1", t, n)], [("x1", t, n)])
                        continue
                    mm_group(PS[b][0:r, :], [(actT[:, fc, c0:c0 + r], W[s][:, fc, :]) for fc in range(16)],
                             WK(s) + [("actT",)], [("ps", b)])
                    tt_op(x1v[0:r, t, n * 512:(n + 1) * 512], x1v[0:r, t, n * 512:(n + 1) * 512], PS[b][0:r, :],
                          ALU.add, [("ps", b), ("x1", t, n)], [("x1", t, n)])

        smp_reduce(False)
        h2_all = [("h2", i, hh) for i in range(9) for hh in range(2)]
        yt = [view(R1, 0, [128, D], F32), view(R1, D * 4, [128, D], F32)]
        junk6 = view(R1, D * 8, [128, D], BF16)
        gfin = view(R1, D * 8 + D * 2, [128, D], F32)
        dma("sp", gfin, nfin_d, "misc", writes=[("c", "gfin")] + h2_all)
        for t, (src, r) in enumerate(xsrc):
            sl = t % 2
            ssv = ssb[:, sl * 2:sl * 2 + 2]
            kss = ("ss", sl)
            xk = [("x1", t, n) for n in range(4)]
            S.op("pool", lambda e, ssv=ssv: e.memset(ssv, 0.0), writes=[kss])
            act(junk6[0:r, :], x1v[0:r, t, :], AF.Square, xk + [kss, ("c", "gfin")], [("t", "junk6"), kss],
                accum_out=ssv[0:r, 0:1])
            act(ssv[0:r, 1:2], ssv[0:r, 0:1], AF.Ln, [kss], [kss], scale=1.0 / D, bias=EPS)
            act(ssv[0:r, 1:2], ssv[0:r, 1:2], AF.Exp, [kss], [kss], scale=-0.5)
            ky = ("t", "yt", sl)
            stt_op(yt[sl][0:r, :], x1v[0:r, t, :], ssv[0:r, 1:2], gfin[0:r, :], ALU.mult, ALU.mult,
                   xk + [kss, ("c", "gfin")], [ky])
            dst = y_main[t * 128:(t + 1) * 128, :] if t < 8 else y_smp
            dma("sp", dst, yt[sl][0:r, :], "yout%d" % sl, reads=[ky])

        if os.environ.get('DBG_MEM'):
            print('SBUF remaining', nc.sbuf_bytes_remaining)
        S.emit(nc, es)
    return nc


_CACHE = {}


def _consts():
    ident = np.eye(128, dtype=np.float32)
    s_idx = np.arange(128)[:, None]
    l_idx = np.arange(128)[None, :]
    mask2 = ((s_idx // 64 == l_idx // 64) & (l_idx >= s_idx)).astype(np.float32)
    rmask = np.ones((128, 512), np.float32)
    rmask[:, ::64] = 0.0
    sel = np.zeros((128, 16), np.float32)
    for p in range(128):
        if p % 32 < 16:
            sel[p, p % 32] = 1.0
    return ident, mask2, rmask, sel


def kernel(x_prompt, x_sample, state_conv, state_hgrn, norm_mix, w_in, conv_w, lb_logits, onorm_g,
           w_branch_a, w_branch_b, w_out, norm_ffn, w_up, w_down, norm_final):
    f32 = lambda a: np.ascontiguousarray(np.asarray(a, dtype=np.float32))
    x_prompt, x_sample, state_conv, state_hgrn = f32(x_prompt), f32(x_sample), f32(state_conv), f32(state_hgrn)
    if "nc" not in _CACHE:
        _CACHE["nc"] = build_program()
    nc = _CACHE["nc"]
    ident, mask2, rmask, sel = _consts()
    shared = {
        "gm": f32(np.asarray(norm_mix)[0].reshape(16, 128).T),
        "gf": f32(np.asarray(norm_ffn)[0].reshape(16, 128).T),
        "cw": f32(np.asarray(conv_w)[0].reshape(3, 8, 128).transpose(2, 1, 0).reshape(128, 24)),
        "lbl": f32(np.asarray(lb_logits).reshape(2, 8, 128).transpose(2, 0, 1).reshape(128, 16)),
        "ogg": f32(np.asarray(onorm_g)[0].reshape(128, 1)),
        "nfin": f32(np.broadcast_to(np.asarray(norm_final).reshape(1, D), (128, D))),
        "ident": ident, "mask2": mask2, "rmask": rmask, "sel": sel,
        "w_in": f32(np.asarray(w_in)[0]), "w_branch_a": f32(np.asarray(w_branch_a)[0]),
        "w_branch_b": f32(np.asarray(w_branch_b)[0]), "w_out": f32(np.asarray(w_out)[0]),
        "w_up": f32(np.asarray(w_up)[0]), "w_down": f32(np.asarray(w_down)[0]),
    }
    in_maps = []
    for c in range(N_CORES):
        sq, hf = c // 2, c % 2
        m = dict(shared)
        m["x_main"] = f32(x_prompt[sq, hf * 1024:(hf + 1) * 1024])
        m["x_pre"] = f32(x_prompt[sq, 0:1024]) if hf == 1 else np.zeros((1024, D), np.float32)
        m["x_smp"] = f32(x_sample[c * 16:(c + 1) * 16, 0])
        m["sconv"] = f32(state_conv[0, c * 16:(c + 1) * 16].reshape(16, 2048))
        m["shgrn"] = f32(state_hgrn[0, c * 16:(c + 1) * 16])
        in_maps.append(m)
    if _CACHE.get('debug_return_maps'):
        return in_maps
    res = run_bass_kernel_spmd(nc, in_maps, core_ids=list(range(N_CORES)))
    R = res.results
    yp = np.zeros((4, 2048, D), np.float32)
    ys = np.zeros((128, 1, D), np.float32)
    ncp = np.zeros((1, 4, 2, 1024), np.float32)
    nhp = np.zeros((1, 4, 8, 128, 128), np.float32)
    ncs = np.zeros((1, 128, 2, 1024), np.float32)
    nhs = np.zeros((1, 128, 8, 128, 128), np.float32)
    for c in range(N_CORES):
        sq, hf = c // 2, c % 2
        r = R[c]
        yp[sq, hf * 1024:(hf + 1) * 1024] = np.asarray(r["y_main"])
        ys[c * 16:(c + 1) * 16, 0] = np.asarray(r["y_smp"])
        ncs[0, c * 16:(c + 1) * 16] = np.asarray(r["o_cs"])
        nhs[0, c * 16:(c + 1) * 16] = np.asarray(r["o_hs"])
        if hf == 1:
            ncp[0, sq] = np.asarray(r["o_cp"])
            nhp[0, sq] = np.asarray(r["o_hp"])
    return yp, ys, ncp, nhp, ncs, nhs
```
